# Optimizing a Trainium2 kernel written in Bass

```python
import jax, jax.numpy as jnp
from jax import lax
import numpy as np

D_MODEL = 1024
BATCH = 8
SEQ = 4096
DEPTH = 1

PLE_DIM = 256
EPS = 1e-6
NEG = -1e30

GDN_HEADS = 4
GDN_DK = 128
GDN_DV = 128
GDN_CONV = 4
GDN_CHUNK = 64

NSA_HEADS = 8
NSA_GROUPS = 2
NSA_HPG = NSA_HEADS // NSA_GROUPS
NSA_DK = 64
NSA_DV = 64
CMP_LEN = 32
CMP_STRIDE = 16
CMP_HIDDEN = 256
SEL_BLOCK = 64
SEL_TOPN = 16
SEL_QCHUNK = 64
WINDOW = 512
WIN_QBLOCK = 128
FORCED_SCORE = 1e9

D_FF = 2816
FFN_CONV = 3

IN_SPLITS = (
    GDN_HEADS * GDN_DK,
    GDN_HEADS * GDN_DK,
    GDN_HEADS * GDN_DV,
    GDN_HEADS * GDN_DV,
    GDN_HEADS,
    GDN_HEADS,
    NSA_HEADS * NSA_DK,
    NSA_GROUPS * NSA_DK,
    NSA_GROUPS * NSA_DV,
    NSA_GROUPS * NSA_DK,
    NSA_GROUPS * NSA_DV,
    NSA_GROUPS * NSA_DK,
    NSA_GROUPS * NSA_DV,
    NSA_HEADS * 3,
    2 * D_MODEL,
)
D_IN = sum(IN_SPLITS)

kernel_name = 'hybrid_gdn_nsa_convffn_layer'


def rmsnorm(x, w):
    xf = x.astype(jnp.float32)
    y = xf * lax.rsqrt(jnp.mean(xf * xf, axis=-1, keepdims=True) + EPS)
    return (y * w.astype(jnp.float32)).astype(x.dtype)


def l2norm(x):
    return x * lax.rsqrt(jnp.sum(x * x, axis=-1, keepdims=True) + EPS)


def causal_dwconv(x, w):
    width, c = w.shape
    return lax.conv_general_dilated(
        x, w[:, None, :].astype(x.dtype), window_strides=(1,), padding=((width - 1, 0),),
        dimension_numbers=('NWC', 'WIO', 'NWC'), feature_group_count=c)


def alibi_slopes(n):
    return 2.0 ** (-8.0 * jnp.arange(1, n + 1, dtype=jnp.float32) / n)


def masked_softmax(s, valid):
    p = jax.nn.softmax(jnp.where(valid, s, NEG), axis=-1)
    return jnp.where(valid, p, 0.0)


def chunk_gated_delta_rule(q, k, v, beta, g):
    B, S, H, dk = q.shape
    dv = v.shape[-1]
    C = GDN_CHUNK
    n = S // C

    def chunks(t):
        t = t.reshape((B, n, C, H) + t.shape[3:])
        return jnp.moveaxis(t, (1, 3), (0, 2))

    qc, kc, vc, bc = chunks(q), chunks(k), chunks(v), chunks(beta)
    gcum = jnp.cumsum(chunks(g), axis=-1)
    idx = jnp.arange(C)
    strict = idx[:, None] > idx[None, :]
    incl = idx[:, None] >= idx[None, :]
    gdiff = gcum[..., :, None] - gcum[..., None, :]
    dec_strict = jnp.where(strict, jnp.exp(jnp.where(strict, gdiff, 0.0)), 0.0)
    dec_incl = jnp.where(incl, jnp.exp(jnp.where(incl, gdiff, 0.0)), 0.0)
    lower = bc[..., None] * jnp.einsum('nbhid,nbhjd->nbhij', kc, kc) * dec_strict
    gam = jnp.exp(gcum)
    rhs = jnp.concatenate([(bc * gam)[..., None] * kc, bc[..., None] * vc], axis=-1)
    sol = lax.linalg.triangular_solve(lower + jnp.eye(C, dtype=q.dtype), rhs,
                                      left_side=True, lower=True, unit_diagonal=True)
    w_c, u0_c = sol[..., :dk], sol[..., dk:]
    qk = jnp.einsum('nbhid,nbhjd->nbhij', qc, kc) * dec_incl
    q_dec = qc * gam[..., None]
    k_tail = kc * jnp.exp(gcum[..., -1:] - gcum)[..., None]
    g_last = gam[..., -1]

    def step(state, inp):
        w_, u0_, qk_, qd_, kt_, gl_ = inp
        u = u0_ - jnp.einsum('bhcd,bhde->bhce', w_, state)
        o = jnp.einsum('bhcd,bhde->bhce', qd_, state) + jnp.einsum('bhij,bhje->bhie', qk_, u)
        new_state = gl_[..., None, None] * state + jnp.einsum('bhcd,bhce->bhde', kt_, u)
        return new_state, o

    s0 = jnp.zeros((B, H, dk, dv), q.dtype)
    _, o = lax.scan(step, s0, (w_c, u0_c, qk, q_dec, k_tail, g_last))
    return jnp.moveaxis(o, (0, 2), (1, 3)).reshape(B, S, H, dv)


def gdn_mixer(q, k, v, z, b, a, conv_w, a_log, dt_bias, norm_w):
    B, S, _ = q.shape
    H, dk, dv = GDN_HEADS, GDN_DK, GDN_DV
    f32 = jnp.float32
    qkv = jax.nn.silu(causal_dwconv(jnp.concatenate([q, k, v], axis=-1), conv_w))
    q, k, v = jnp.split(qkv, [H * dk, 2 * H * dk], axis=-1)
    q = l2norm(q.reshape(B, S, H, dk).astype(f32)) * (dk ** -0.5)
    k = l2norm(k.reshape(B, S, H, dk).astype(f32))
    v = v.reshape(B, S, H, dv).astype(f32)
    beta = jax.nn.sigmoid(b.astype(f32))
    g = -jnp.exp(a_log.astype(f32)) * jax.nn.softplus(a.astype(f32) + dt_bias.astype(f32))
    o = chunk_gated_delta_rule(q, k, v, beta, g).astype(z.dtype)
    o = rmsnorm(o, norm_w) * jax.nn.silu(z.reshape(B, S, H, dv))
    return o.reshape(B, S, H * dv)


def compress_blocks(x, pos, w1, w2):
    B, S, G, d = x.shape
    r = CMP_LEN // CMP_STRIDE
    xs = x.reshape(B, S // CMP_STRIDE, CMP_STRIDE, G, d)
    nc = S // CMP_STRIDE - r + 1
    blocks = jnp.concatenate([xs[:, i:i + nc] for i in range(r)], axis=2)
    blocks = blocks + pos[:, None, :]
    flat = jnp.moveaxis(blocks, 3, 2).reshape(B, nc, G, CMP_LEN * d)
    return jax.nn.gelu(flat @ w1) @ w2


def nsa_mixer(q, k_cmp, v_cmp, k_slc, v_slc, k_win, v_win, gate,
              cmp_pos_k, cmp_w1_k, cmp_w2_k, cmp_pos_v, cmp_w1_v, cmp_w2_v):
    B, S, _ = q.shape
    H, G, R, dk, dv = NSA_HEADS, NSA_GROUPS, NSA_HPG, NSA_DK, NSA_DV
    f32 = jnp.float32
    slopes = alibi_slopes(H).reshape(G, R)
    t = jnp.arange(S)
    qh = q.reshape(B, S, G, R, dk) * (dk ** -0.5)
    grp = lambda y, d: y.reshape(B, S, G, d)

    kc = compress_blocks(grp(k_cmp, dk), cmp_pos_k, cmp_w1_k, cmp_w2_k)
    vc = compress_blocks(grp(v_cmp, dv), cmp_pos_v, cmp_w1_v, cmp_w2_v)
    nc = kc.shape[1]
    blk_start = jnp.arange(nc) * CMP_STRIDE
    dist_c = t[:, None] - (blk_start + CMP_LEN - 1)[None, :]
    s_c = jnp.einsum('bsgrd,bngd->bgrsn', qh, kc).astype(f32) \
        - slopes[None, :, :, None, None] * dist_c.astype(f32)
    p_c = masked_softmax(s_c, dist_c >= 0)
    o_cmp = jnp.einsum('bgrsn,bngd->bsgrd', p_c.astype(vc.dtype), vc)

    ns = S // SEL_BLOCK
    sel_start = jnp.arange(ns) * SEL_BLOCK
    overlap = jnp.clip(jnp.minimum(blk_start[:, None] + CMP_LEN, sel_start[None, :] + SEL_BLOCK)
                       - jnp.maximum(blk_start[:, None], sel_start[None, :]), 0, None).astype(f32) / CMP_LEN
    imp = jnp.einsum('bgsn,nj->bgsj', p_c.sum(axis=2), overlap)
    cur = t // SEL_BLOCK
    j = jnp.arange(ns)
    forced = (j[None, :] == 0) | (j[None, :] == cur[:, None]) | (j[None, :] == cur[:, None] - 1)
    sel_valid = j[None, :] <= cur[:, None]
    imp = jnp.where(sel_valid, jnp.where(forced, FORCED_SCORE, imp), NEG)
    n_sel = min(SEL_TOPN, ns)
    _, sel_idx = lax.top_k(imp, n_sel)

    ks = k_slc.reshape(B, ns, SEL_BLOCK, G, dk).transpose(0, 3, 1, 2, 4)
    vs = v_slc.reshape(B, ns, SEL_BLOCK, G, dv).transpose(0, 3, 1, 2, 4)
    QC = SEL_QCHUNK
    nq = S // QC
    q_ch = qh.reshape(B, nq, QC, G, R, dk).transpose(1, 0, 3, 4, 2, 5)
    idx_ch = sel_idx.reshape(B, G, nq, QC, n_sel).transpose(2, 0, 1, 3, 4)
    t_ch = t.reshape(nq, QC)
    nk = n_sel * SEL_BLOCK

    def sel_chunk(args):
        qb, ib, tb = args
        flat = ib.reshape(B, G, QC * n_sel)[:, :, :, None, None]
        kg = jnp.take_along_axis(ks, flat, axis=2).reshape(B, G, QC, nk, dk)
        vg = jnp.take_along_axis(vs, flat, axis=2).reshape(B, G, QC, nk, dv)
        pos = (ib[..., None] * SEL_BLOCK + jnp.arange(SEL_BLOCK)).reshape(B, G, QC, nk)
        dist = tb[None, None, :, None] - pos
        s = jnp.einsum('bgrqd,bgqkd->bgrqk', qb, kg).astype(f32) \
            - slopes[None, :, :, None, None] * dist[:, :, None].astype(f32)
        pr = masked_softmax(s, (dist >= 0)[:, :, None])
        return jnp.einsum('bgrqk,bgqkd->bgrqd', pr.astype(vg.dtype), vg)

    o_slc = lax.map(sel_chunk, (q_ch, idx_ch, t_ch))
    o_slc = o_slc.transpose(1, 0, 4, 2, 3, 5).reshape(B, S, G, R, dv)

    QB = WIN_QBLOCK
    nb = S // QB
    nwb = WINDOW // QB

    def band(y, d):
        yb = y.reshape(B, nb, QB, G, d).transpose(1, 0, 3, 2, 4)
        yp = jnp.pad(yb, ((nwb, 0), (0, 0), (0, 0), (0, 0), (0, 0)))
        return jnp.concatenate([yp[i:i + nb] for i in range(nwb + 1)], axis=3)

    kband, vband = band(k_win, dk), band(v_win, dv)
    q_wb = qh.reshape(B, nb, QB, G, R, dk).transpose(1, 0, 3, 4, 2, 5)
    qpos = t.reshape(nb, QB)
    kpos = (jnp.arange(nb)[:, None] - nwb) * QB + jnp.arange((nwb + 1) * QB)[None, :]

    def win_block(args):
        qb, kb, vb, tq, tk = args
        dist = tq[:, None] - tk[None, :]
        valid = (dist >= 0) & (dist < WINDOW) & (tk[None, :] >= 0)
        s = jnp.einsum('bgrqd,bgkd->bgrqk', qb, kb).astype(f32) \
            - slopes[None, :, :, None, None] * dist.astype(f32)
        pr = masked_softmax(s, valid)
        return jnp.einsum('bgrqk,bgkd->bgrqd', pr.astype(vb.dtype), vb)

    o_win = lax.map(win_block, (q_wb, kband, vband, qpos, kpos))
    o_win = o_win.transpose(1, 0, 4, 2, 3, 5).reshape(B, S, G, R, dv)

    gates = jax.nn.sigmoid(gate).reshape(B, S, G, R, 3)
    o = gates[..., 0:1] * o_cmp + gates[..., 1:2] * o_slc + gates[..., 2:3] * o_win
    return o.reshape(B, S, H * dv)


def setup_inputs(seed: int = 0) -> dict:
    key = jax.random.key(seed)
    ks = jax.random.split(key, 32)
    L, D = DEPTH, D_MODEL
    f32 = jnp.float32

    def nrm(k, shape, fan_in):
        return jax.random.normal(k, shape, f32) * (fan_in ** -0.5)

    def gain(k, n):
        return 1.0 + 0.05 * jax.random.normal(k, (L, n), f32)

    dt = jnp.exp(jax.random.uniform(ks[6], (L, GDN_HEADS), f32, minval=np.log(1e-3), maxval=np.log(1e-1)))
    return {
        'x': jax.random.normal(ks[0], (BATCH, SEQ, D), f32),
        'p': jax.random.normal(ks[1], (DEPTH, BATCH, SEQ, PLE_DIM), f32),
        'norm_mix_pre': gain(ks[2], D),
        'w_in': nrm(ks[3], (L, D, D_IN), D),
        'conv_qkv': nrm(ks[4], (L, GDN_CONV, 2 * GDN_HEADS * GDN_DK + GDN_HEADS * GDN_DV), GDN_CONV),
        'a_log': jnp.log(jax.random.uniform(ks[5], (L, GDN_HEADS), f32, minval=1.0, maxval=16.0)),
        'dt_bias': dt + jnp.log(-jnp.expm1(-dt)),
        'gdn_norm': gain(ks[7], GDN_DV),
        'cmp_pos_k': 0.1 * jax.random.normal(ks[8], (L, CMP_LEN, NSA_DK), f32),
        'cmp_w1_k': nrm(ks[9], (L, CMP_LEN * NSA_DK, CMP_HIDDEN), CMP_LEN * NSA_DK),
        'cmp_w2_k': nrm(ks[10], (L, CMP_HIDDEN, NSA_DK), CMP_HIDDEN),
        'cmp_pos_v': 0.1 * jax.random.normal(ks[11], (L, CMP_LEN, NSA_DV), f32),
        'cmp_w1_v': nrm(ks[12], (L, CMP_LEN * NSA_DV, CMP_HIDDEN), CMP_LEN * NSA_DV),
        'cmp_w2_v': nrm(ks[13], (L, CMP_HIDDEN, NSA_DV), CMP_HIDDEN),
        'w_a2d': nrm(ks[14], (L, GDN_HEADS * GDN_DV, D), GDN_HEADS * GDN_DV),
        'w_b2d': nrm(ks[15], (L, NSA_HEADS * NSA_DV, D), NSA_HEADS * NSA_DV),
        'w_o': nrm(ks[16], (L, D, D), D),
        'norm_mix_post': gain(ks[17], D),
        'norm_ffn_pre': gain(ks[18], D),
        'w_up': nrm(ks[19], (L, D, 2 * D_FF), D),
        'conv_ffn': nrm(ks[20], (L, FFN_CONV, 2 * D_FF), FFN_CONV),
        'conv_ffn_b': 0.02 * jax.random.normal(ks[21], (L, 2 * D_FF), f32),
        'w_down': nrm(ks[22], (L, D_FF, D), D_FF),
        'norm_ffn_post': gain(ks[23], D),
        'w_ple': nrm(ks[24], (L, PLE_DIM, D), PLE_DIM),
        'w_ple_gate': nrm(ks[25], (L, D, D), D),
        'norm_ple_post': gain(ks[26], D),
    }


def reference(x, p, norm_mix_pre, w_in, conv_qkv, a_log, dt_bias, gdn_norm,
              cmp_pos_k, cmp_w1_k, cmp_w2_k, cmp_pos_v, cmp_w1_v, cmp_w2_v,
              w_a2d, w_b2d, w_o, norm_mix_post, norm_ffn_pre, w_up, conv_ffn, conv_ffn_b,
              w_down, norm_ffn_post, w_ple, w_ple_gate, norm_ple_post):
    offs = np.cumsum(IN_SPLITS)[:-1].tolist()
    h = x
    for i in range(DEPTH):
        u = rmsnorm(h, norm_mix_pre[i])
        (qa, ka, va, za, ba, aa, qb, kcm, vcm, ksl, vsl, kwi, vwi, gnsa, gmix) = \
            jnp.split(u @ w_in[i], offs, axis=-1)
        ya = gdn_mixer(qa, ka, va, za, ba, aa, conv_qkv[i], a_log[i], dt_bias[i], gdn_norm[i])
        yb = nsa_mixer(qb, kcm, vcm, ksl, vsl, kwi, vwi, gnsa,
                       cmp_pos_k[i], cmp_w1_k[i], cmp_w2_k[i], cmp_pos_v[i], cmp_w1_v[i], cmp_w2_v[i])
        g_a, g_b = jnp.split(jax.nn.sigmoid(gmix), 2, axis=-1)
        mixed = (g_a * (ya @ w_a2d[i]) + g_b * (yb @ w_b2d[i])) @ w_o[i]
        h = h + rmsnorm(mixed, norm_mix_post[i])
        f = rmsnorm(h, norm_ffn_pre[i]) @ w_up[i]
        f = causal_dwconv(f, conv_ffn[i]) + conv_ffn_b[i]
        f_gate, f_val = jnp.split(f, 2, axis=-1)
        f = (jax.nn.gelu(f_gate) * f_val) @ w_down[i]
        h = h + rmsnorm(f, norm_ffn_post[i])
        e = (p[i] @ w_ple[i]) * jax.nn.sigmoid(h @ w_ple_gate[i])
        h = h + rmsnorm(e, norm_ple_post[i])
    return h
```

```python
import contextlib
import numpy as np
import ml_dtypes
import concourse.bass as bass
import concourse.mybir as mybir
from concourse.bass_utils import run_bass_kernel_spmd

F32 = mybir.dt.float32
BF16 = mybir.dt.bfloat16
AF = mybir.ActivationFunctionType
ALU = mybir.AluOpType
AX = mybir.AxisListType

D = 1024
DIN = 5408
DFF = 2816
EPS = 1e-6
NEGM = -30000.0
NPOOL = 20
import os
SAME_INORDER_ENGS = tuple(x for x in os.environ.get('SAME_INORDER', 'act').split(',') if x)


class Sched:
    def __init__(self, nc):
        self.nc = nc
        self.E = {}
        for name, eng in (("pe", nc.tensor), ("dve", nc.vector), ("act", nc.scalar),
                          ("pool", nc.gpsimd), ("sp", nc.sync)):
            self.E[name] = dict(eng=eng, sem=nc.alloc_semaphore(name="sem_" + name), cnt=0, waited={})
        self.dq = {}
        for q in ("sp", "pool"):
            self.dq[q] = dict(sems=[nc.alloc_semaphore(name=f"dq_{q}_{i}") for i in range(NPOOL)],
                              vals=[0] * NPOOL, nxt=0)
        self.res = {}
        self.ninst = 0

    def semof(self, owner):
        if isinstance(owner, tuple):
            return self.dq[owner[1]]["sems"][owner[2]]
        return self.E[owner]["sem"]

    def _wait(self, en, tok):
        owner, val = tok
        if val <= 0:
            return
        e = self.E[en]
        if e["waited"].get(owner, 0) >= val:
            return
        e["eng"].wait_ge(self.semof(owner), val)
        e["waited"][owner] = val
        self.ninst += 1

    def deps(self, en, rd, wr):
        same_ok = en in SAME_INORDER_ENGS
        for k in rd:
            r = self.res.get(k)
            if r and r["w"]:
                t = r["w"]
                if not (t[0] == en and (en == "pe" or same_ok)):
                    self._wait(en, t)
        for k in wr:
            r = self.res.get(k)
            if r:
                if r["w"]:
                    t = r["w"]
                    if not (t[0] == en and (en == "pe" or same_ok)):
                        self._wait(en, t)
                for o, v in r["r"].items():
                    if o != en:
                        self._wait(en, (o, v))

    def commit(self, tok, rd, wr):
        for k in rd:
            r = self.res.setdefault(k, {"w": None, "r": {}})
            if r["r"].get(tok[0], 0) < tok[1]:
                r["r"][tok[0]] = tok[1]
        for k in wr:
            self.res[k] = {"w": tok, "r": {}}

    def op(self, en, meth, *a, rd=(), wr=(), **kw):
        self.deps(en, rd, wr)
        e = self.E[en]
        inst = getattr(e["eng"], meth)(*a, **kw)
        e["cnt"] += 1
        inst.then_inc(e["sem"], 1)
        self.commit((en, e["cnt"]), rd, wr)
        self.ninst += 1

    def dma(self, q, out, in_, rd=(), wr=(), **kw):
        self.deps(q, rd, wr)
        pool = self.dq[q]
        i = pool["nxt"]
        pool["nxt"] = (i + 1) % NPOOL
        owner = ("d", q, i)
        self._wait(q, (owner, pool["vals"][i]))
        inst = self.E[q]["eng"].dma_start(out=out, in_=in_, **kw)
        pool["vals"][i] += 16
        inst.then_inc(pool["sems"][i], 16)
        self.commit((owner, pool["vals"][i]), rd, wr)
        self.ninst += 1

    def barrier(self):
        sp = self.E["sp"]
        for en in ("pe", "dve", "act", "pool"):
            self._wait("sp", (en, self.E[en]["cnt"]))
        for q, pool in self.dq.items():
            for i, v in enumerate(pool["vals"]):
                self._wait("sp", (("d", q, i), v))
        inst = sp["eng"].nop()
        sp["cnt"] += 1
        inst.then_inc(sp["sem"], 1)
        for en in ("pe", "dve", "act", "pool"):
            self._wait(en, ("sp", sp["cnt"]))
        self.res = {}
        for en, e in self.E.items():
            for o in self.E:
                e["waited"][o] = self.E[o]["cnt"]
            for q, pool in self.dq.items():
                for i, v in enumerate(pool["vals"]):
                    e["waited"][("d", q, i)] = v


class Ctx:
    def __init__(self, nc, T):
        self.nc = nc
        self.T = T
        self.S = Sched(nc)
        self.top = contextlib.ExitStack()
        self.ps = [self.top.enter_context(nc.psum_tensor(f"psb{i}", [128, 512], F32)) for i in range(8)]
        self.psi = 0
        self.ps_lim = 8
        self.es = None
        self.uid = 0

    def psum(self):
        i = self.psi % self.ps_lim
        self.psi = (i + 1) % self.ps_lim
        return self.ps[i], ("ps", i)

    def psum_fixed(self, i):
        return self.ps[i], ("ps", i)

    def begin(self):
        self.es = contextlib.ExitStack()

    def end(self):
        self.S.barrier()
        self.es.close()
        self.es = None

    def sb(self, name, shape, dt):
        self.uid += 1
        return self.es.enter_context(self.nc.sbuf_tensor(f"{name}_{self.uid}", shape, dt))

    def sbtop(self, name, shape, dt):
        return self.top.enter_context(self.nc.sbuf_tensor(name, shape, dt))

    def mm(self, out, lhsT, rhs, start, stop, rd, wr):
        self.S.op("pe", "matmul", out, lhsT=lhsT, rhs=rhs, start=start, stop=stop, rd=rd, wr=wr)

    def tr(self, out, in_, ident, rd, wr):
        self.S.op("pe", "transpose", out, in_, ident, rd=rd, wr=wr)

    def act(self, out, in_, func, rd, wr, **kw):
        self.S.op("act", "activation", out=out, in_=in_, func=func, rd=rd, wr=wr, **kw)

    def v(self, en, meth, rd, wr, *a, **kw):
        self.S.op(en, meth, *a, rd=rd, wr=wr, **kw)

    def dma(self, q, out, in_, rd, wr, **kw):
        self.S.dma(q, out, in_, rd=rd, wr=wr, **kw)


class Rot:
    def __init__(self, cx, name, shape, dt, n):
        self.bufs = [cx.sb(f"{name}{i}", shape, dt) for i in range(n)]
        self.keys = [(name, cx.uid, i) for i in range(n)]
        self.i = 0

    def next(self):
        i = self.i
        self.i = (i + 1) % len(self.bufs)
        return self.bufs[i], self.keys[i]


def phase_A(cx, io):
    nc, T = cx.nc, cx.T
    cx.begin()
    NS = T // 512
    w_sb = cx.sb("w_in", [128, 8, DIN], BF16)
    wsm = cx.sb("wsm", [128, 8, 288], BF16)
    normw = cx.sb("normw", [128, D], F32)
    cw = cx.sb("cw", [128, 4, 12], F32)
    xc = [cx.sb(f"xc{i}", [128, 515], F32) for i in range(12)]
    xtR = Rot(cx, "xt", [128, 4, D], F32, 1)
    rots = make_norm_rots2(cx, 4, 4)
    yR = Rot(cx, "y", [128, 512], F32, 4)
    ysR = Rot(cx, "ys", [128, 512], F32, 5)
    lnR = Rot(cx, "ln", [128, 512], F32, 4)
    sqR = Rot(cx, "sq", [128, 512], BF16, 3)
    oR = Rot(cx, "o", [128, 512], BF16, 4)
    smt = Rot(cx, "smt", [128, 288], F32, 2)
    vpR = Rot(cx, "vpad", [128, 260], BF16, 2)
    for b_ in vpR.bufs:
        cx.v("pool", "memset", [], ["vpinit"], b_[:], 1.0)
    C = dict(cx.consts)
    C["ones_bf"] = cx.sb("ones_bf", [128, 128], BF16)
    cx.v("pool", "memset", [], ["consts"], C["ones_bf"][:], 1.0)

    WB = [0, 512, 2048, 3584, DIN]

    def wg(c0, width=128):
        return sorted({max(i for i in range(len(WB) - 1) if WB[i] <= c) for c in (c0, c0 + width - 1)})

    for gi in range(len(WB) - 1):
        c0_, c1_ = WB[gi], WB[gi + 1]
        for kc in range(8):
            cx.dma("pool", w_sb[:, kc, c0_:c1_], io["w_in"][kc * 128:(kc + 1) * 128, c0_:c1_], rd=[], wr=[("w", kc, gi)])
    for kc in range(8):
        rows = slice(kc * 128, (kc + 1) * 128)
        for (d0, s0_, n) in ((0, 2048, 8), (8, 2952, 128), (136, 3208, 128), (264, 3336, 24)):
            cx.dma("pool", wsm[:, kc, d0:d0 + n], io["w_in"][rows, s0_:s0_ + n], rd=[], wr=[("wsm", kc)])
    cx.dma("sp", normw[:], io["norm_mix_pre"].partition_broadcast(128), rd=[], wr=["normw"])
    cwr = cx.sb("cwr", [12, 4, 128], F32)
    idf = cx.sb("idf", [12, 12], F32)
    cx.dma("sp", idf[:], io["c_ident_f"][0:12, 0:12], rd=[], wr=["idf"])
    for k in range(4):
        cx.dma("sp", cwr[:, k, :], io["conv_qkv"][k, :].rearrange("(c p) -> c p", p=128), rd=[], wr=[("cwr", k)])
    for k in range(4):
        ps_, pk_ = cx.psum()
        cx.tr(ps_[:, 0:12], cwr[:, k, :], idf[:, :], rd=[("cwr", k), "idf"], wr=[pk_])
        cx.act(cw[:, k, :], ps_[:, 0:12], AF.Copy, rd=[pk_], wr=["cw"])
    for i in range(12):
        cx.v("pool", "memset", [], [("xc", i)], xc[i][:, 0:3], 0.0)

    fch = []
    for c in range(12):
        fch.append((c * 128, "gdn", c))
    for c in range(4):
        fch.append((2056 + c * 128, "qb", c))
    for i, c0 in enumerate((2568, 2696, 2824, 3080)):
        fch.append((c0, "feat", i))
    for c in range(16):
        fch.append((3360 + c * 128, "gmix", c))
    ST = [dict() for _ in range(NS)]

    def xload_item(s):
        st = ST[s]
        tok = slice(s * 512, (s + 1) * 512)
        st["xb"], st["xk"] = xtR.next()
        cx.dma("sp", st["xb"][:], io["x"][tok, :].rearrange("(a p) d -> p a d", p=128), rd=[], wr=[st["xk"]])
        yield

    def n_item(s):
        st = ST[s]
        yield from norm_T_gen(cx, st["xb"], st["xk"], 4, normw, "normw", rots, st)

    def f_item(s, c0, kind, idx):
        st = ST[s]
        tok = slice(s * 512, (s + 1) * 512)
        yield
        yield
        uTb, uTall = st["uT"], st["uTall"]
        ps, pk = cx.psum()
        for kc in range(8):
            cx.mm(ps[:, :], w_sb[:, kc, c0:c0 + 128], uTb[:, kc, :], kc == 0, kc == 7,
                  rd=[("w", kc, g_) for g_ in wg(c0)] + uTall, wr=[pk])
        if kind == "gdn":
            c = idx
            h = c % 4
            xk_ = ("xc", c)
            cx.act(xc[c][:, 3:515], ps[:, :], AF.Copy, rd=[pk], wr=[xk_])
            yield
            y, yk = yR.next()
            cx.v("dve", "tensor_scalar", [xk_, "cw"], [yk], out=y[:], in0=xc[c][:, 0:512],
                 scalar1=cw[:, 0, c:c + 1], scalar2=None, op0=ALU.mult)
            yield
            cx.v("dve", "scalar_tensor_tensor", [xk_, "cw", yk], [yk], out=y[:], in0=xc[c][:, 1:513],
                 scalar=cw[:, 1, c:c + 1], in1=y[:], op0=ALU.mult, op1=ALU.add)
            yield
            cx.v("dve", "scalar_tensor_tensor", [xk_, "cw", yk], [yk], out=y[:], in0=xc[c][:, 2:514],
                 scalar=cw[:, 2, c:c + 1], in1=y[:], op0=ALU.mult, op1=ALU.add)
            yield
            cx.v("dve", "scalar_tensor_tensor", [xk_, "cw", yk], [yk], out=y[:], in0=xc[c][:, 3:515],
                 scalar=cw[:, 3, c:c + 1], in1=y[:], op0=ALU.mult, op1=ALU.add)
            cx.v("pool", "tensor_copy", [xk_], [xk_], out=xc[c][:, 0:3], in_=xc[c][:, 512:515])
            ys, ysk = ysR.next()
            cx.act(ys[:], y[:], AF.Silu, rd=[yk], wr=[ysk])
            yield
            if c < 8:
                sq, sqk = sqR.next()
                cx.v("pool", "tensor_tensor", [ysk], [sqk], out=sq[:], in0=ys[:], in1=ys[:], op=ALU.mult)
                yield
                ps2, pk2 = cx.psum()
                cx.mm(ps2[:, :], C["ones_bf"][:, :], sq[:], True, True, rd=[sqk, "consts"], wr=[pk2])
                ln, lnk = lnR.next()
                cx.act(ln[:], ps2[:, :], AF.Ln, rd=[pk2, "consts"], wr=[lnk], bias=C["eps_col"][:, 0:1])
                bcol = C["lnq_col"] if c < 4 else C["zero_col"]
                cx.act(ln[:], ln[:], AF.Exp, rd=[lnk, "consts"], wr=[lnk], scale=-0.5, bias=bcol[:, 0:1])
                yield
                o, ok = oR.next()
                cx.v("dve", "tensor_tensor", [ysk, lnk], [ok], out=o[:], in0=ys[:], in1=ln[:], op=ALU.mult)
                dst = io["gq"] if c < 4 else io["gk"]
                cx.dma("sp", dst[h, :, tok], o[:], rd=[ok], wr=[])
            else:
                yv, yvk = sqR.next()
                cx.v("pool", "tensor_copy", [ysk], [yvk], out=yv[:], in_=ys[:])
                yield
                ps2, pk2 = cx.psum()
                ps2b = ps2[:].bitcast(BF16)
                for a in range(4):
                    cx.tr(ps2b[:, a * 128:(a + 1) * 128], yv[:, a * 128:(a + 1) * 128], C["ident_bf"][:],
                          rd=[yvk, "consts"], wr=[pk2])
                vt, vtk = oR.next()
                cx.act(vt[:], ps2b[:, 0:512], AF.Copy, rd=[pk2], wr=[vtk])
                cx.dma("sp", io["gv"][tok, h * 128:(h + 1) * 128].rearrange("(a p) e -> p a e", p=128),
                       vt[:].rearrange("p (a e) -> p a e", a=4), rd=[vtk], wr=[])
        elif kind == "qb":
            o, ok = oR.next()
            cx.act(o[:], ps[:, :], AF.Copy, rd=[pk], wr=[ok], scale=0.125)
            cx.dma("sp", io["QN"][2 * idx, 0:64, tok], o[0:64, :], rd=[ok], wr=[])
            cx.dma("sp", io["QN"][2 * idx + 1, 0:64, tok], o[64:128, :], rd=[ok], wr=[])
        elif kind == "feat":
            o, ok = oR.next()
            cx.v("dve", "tensor_copy", [pk], [ok], out=o[:], in_=ps[:, :])
            cx.dma("sp", io["featT"][idx, :, tok], o[:], rd=[ok], wr=[])
        else:
            o, ok = oR.next()
            cx.act(o[:], ps[:, :], AF.Sigmoid, rd=[pk], wr=[ok])
            cx.dma("sp", io["gmixT"][idx * 128:(idx + 1) * 128, tok], o[:], rd=[ok], wr=[])

    def t_item(s, a):
        st = ST[s]
        yield
        yield
        uTb, uTall = st["uT"], st["uTall"]
        tk = slice(s * 512 + a * 128, s * 512 + (a + 1) * 128)
        ps, pk = cx.psum()
        for kc in range(8):
            cx.mm(ps[:, :], uTb[:, kc, a * 128:(a + 1) * 128], w_sb[:, kc, 1536:2048], kc == 0, kc == 7,
                  rd=[("w", kc, g_) for g_ in wg(1536, 512)] + uTall, wr=[pk])
        o, ok = oR.next()
        cx.act(o[:], ps[:, :], AF.Copy, rd=[pk], wr=[ok])
        cx.dma("sp", io["gz"][tk, :], o[:], rd=[ok], wr=[])
        ps, pk = cx.psum()
        for kc in range(8):
            cx.mm(ps[:, 0:288], uTb[:, kc, a * 128:(a + 1) * 128], wsm[:, kc, :], kc == 0, kc == 7,
                  rd=[("wsm", kc)] + uTall, wr=[pk])
        o2, ok2 = smt.next()
        cx.v("dve", "tensor_copy", [pk], [ok2], out=o2[:], in_=ps[:, 0:288])
        cx.dma("sp", io["gsm"][tk, :], o2[:], rd=[ok2], wr=[])
        o3, ok3 = vpR.next()
        cx.v("pool", "tensor_copy", [ok2, "vpinit"], [ok3], out=o3[:].rearrange("p (b e) -> p b e", b=4)[:, :, 0:64],
             in_=o2[:, 8:264].rearrange("p (b e) -> p b e", b=4))
        cx.dma("sp", io["gvs"][tk, :], o3[:], rd=[ok3], wr=[])

    def items():
        yield xload_item(0)
        yield n_item(0)
        for s in range(NS):
            for j, (c0, kind, idx) in enumerate(fch):
                yield f_item(s, c0, kind, idx)
                if j == 4 and s + 1 < NS:
                    yield xload_item(s + 1)
                if j == 22 and s + 1 < NS:
                    yield n_item(s + 1)
            for a in range(4):
                yield t_item(s, a)

    pipeline(items())
    cx.end()


def norm_T(cx, xb, xk, nsub, normw, nk, rots, do_norm=True):
    C = cx.consts
    ssr, msr, rsr, ubr, uTr, junk = rots
    uTb, uTk = uTr.next()
    if do_norm:
        ssb, ssk = ssr.next()
        msb, msk = msr.next()
        rsb, rsk = rsr.next()
        for a in range(nsub):
            cx.act(junk[:], xb[:, a, :], AF.Square, rd=[xk], wr=["junk", (ssk, a)], accum_out=ssb[:, a:a + 1])
        cx.v("dve", "tensor_scalar", [(ssk, a) for a in range(nsub)], [msk], out=msb[:, 0:nsub], in0=ssb[:, 0:nsub],
             scalar1=1.0 / D, scalar2=EPS, op0=ALU.mult, op1=ALU.add)
        cx.act(msb[:, 0:nsub], msb[:, 0:nsub], AF.Sqrt, rd=[msk], wr=[msk])
        cx.v("dve", "reciprocal", [msk], [rsk], out=rsb[:, 0:nsub], in_=msb[:, 0:nsub])
    for a in range(nsub):
        u, uk = ubr.next()
        if do_norm:
            cx.v("dve", "scalar_tensor_tensor", [xk, rsk, nk], [uk], out=u[:], in0=xb[:, a, :],
                 scalar=rsb[:, a:a + 1], in1=normw[:], op0=ALU.mult, op1=ALU.mult)
        else:
            cx.v("dve", "tensor_copy", [xk], [uk], out=u[:], in_=xb[:, a, :])
        ps, pk = cx.psum()
        psb = ps[:].bitcast(BF16)
        for kc in range(8):
            cx.tr(psb[:, kc * 128:(kc + 1) * 128], u[:, kc * 128:(kc + 1) * 128], C["ident_bf"][:],
                  rd=[uk, "consts"], wr=[pk])
        cx.act(uTb[:, :, a * 128:(a + 1) * 128], psb.rearrange("p (k t) -> p k t", k=8), AF.Copy,
               rd=[pk], wr=[(uTk, a)])
    return uTb, [(uTk, a) for a in range(nsub)]


def make_norm_rots(cx, nsub):
    return (Rot(cx, "ss", [128, 4], F32, 2), Rot(cx, "ms", [128, 4], F32, 2), Rot(cx, "rstd", [128, 4], F32, 2),
            Rot(cx, "ub", [128, D], BF16, 2), Rot(cx, "uT", [128, 8, nsub * 128], BF16, 2),
            cx.sb("junk", [128, D], BF16))


def epilogue(cx, m, mk, resid, rk, wB, wk, dst, er):
    ssr, junk = er
    ssb, ssk = ssr.next()
    cx.act(junk[:], m[:], AF.Square, rd=[mk], wr=["junk", ssk], accum_out=ssb[:, 0:1])
    cx.v("dve", "tensor_scalar", [ssk], [ssk], out=ssb[:, 1:2], in0=ssb[:, 0:1],
         scalar1=1.0 / D, scalar2=EPS, op0=ALU.mult, op1=ALU.add)
    cx.act(ssb[:, 1:2], ssb[:, 1:2], AF.Sqrt, rd=[ssk], wr=[ssk])
    cx.v("dve", "reciprocal", [ssk], [ssk], out=ssb[:, 2:3], in_=ssb[:, 1:2])
    cx.v("dve", "scalar_tensor_tensor", [mk, ssk, wk], [mk], out=m[:], in0=m[:],
         scalar=ssb[:, 2:3], in1=wB[:], op0=ALU.mult, op1=ALU.mult)
    cx.v("pool", "tensor_tensor", [mk, rk], [mk], out=m[:], in0=m[:], in1=resid, op=ALU.add)
    cx.dma("sp", dst, m[:], rd=[mk], wr=[])


def load_w_bf16(cx, name, dram, K, N, key, gsize=None, order=None):
    kc_n = K // 128
    t = cx.sb(name, [128, kc_n, N], BF16)
    gsize = gsize or N
    ng = (N + gsize - 1) // gsize
    for gi in (order or range(ng)):
        c0, c1 = gi * gsize, min(N, (gi + 1) * gsize)
        for kc in range(kc_n):
            cx.dma("pool", t[:, kc, c0:c1], dram[kc * 128:(kc + 1) * 128, c0:c1], rd=[], wr=[(key, kc, gi)])
    return t


def wkeys(key, kcs, c0, width, gsize):
    gs = sorted({c0 // gsize, (c0 + width - 1) // gsize})
    return [(key, kc, g) for kc in kcs for g in gs]


def phase_M(cx, io):
    nc, T = cx.nc, cx.T
    cx.begin()
    NS = T // 512
    wa = load_w_bf16(cx, "wa", io["w_a2d"], 512, D, "wa", 512)
    wb = load_w_bf16(cx, "wb", io["w_b2d"], 512, D, "wb", 512)
    wo = load_w_bf16(cx, "wo", io["w_o"], D, D, "wo", 512)
    normw = cx.sb("normw", [128, D], F32)
    cx.dma("sp", normw[:], io["norm_mix_post"].partition_broadcast(128), rd=[], wr=["normw"])
    yaR = Rot(cx, "ya", [128, 4, 512], BF16, 2)
    ybR = Rot(cx, "yb", [128, 4, 512], BF16, 2)
    gmR = Rot(cx, "gm", [128, 16, 512], BF16, 2)
    xR = Rot(cx, "xt", [128, 4, D], F32, 3)
    mixR = Rot(cx, "mix", [128, 8, 512], BF16, 2)
    t1R = Rot(cx, "t1", [128, 512], F32, 3)
    t2R = Rot(cx, "t2", [128, 512], F32, 3)
    mR = Rot(cx, "m", [128, D], F32, 6)
    er = (Rot(cx, "ess", [128, 4], F32, 6), cx.sb("junk", [128, D], BF16))
    ST = [dict() for _ in range(NS)]

    def load_item(s):
        st = ST[s]
        tok = slice(s * 512, (s + 1) * 512)
        st["ya"], st["yak"] = yaR.next()
        st["yb"], st["ybk"] = ybR.next()
        st["gm"], st["gmk"] = gmR.next()
        st["x"], st["xk"] = xR.next()
        st["mix"], st["mixk"] = mixR.next()
        cx.dma("sp", st["ya"][:], io["yaT"][:, tok].rearrange("(k p) t -> p k t", p=128), rd=[], wr=[st["yak"]])
        cx.dma("sp", st["yb"][:], io["ybT"][:, tok].rearrange("(k p) t -> p k t", p=128), rd=[], wr=[st["ybk"]])
        cx.dma("sp", st["gm"][:], io["gmixT"][:, tok].rearrange("(k p) t -> p k t", p=128), rd=[], wr=[st["gmk"]])
        cx.dma("sp", st["x"][:], io["x"][tok, :].rearrange("(a p) d -> p a d", p=128), rd=[], wr=[st["xk"]])
        yield

    def oc_item(s, oc):
        st = ST[s]
        yield
        ya, yak, yb, ybk, gm, gmk, mix, mixk = (st[k] for k in ("ya", "yak", "yb", "ybk", "gm", "gmk", "mix", "mixk"))
        psA, pkA = cx.psum()
        for kc in range(4):
            cx.mm(psA[:, :], wa[:, kc, oc * 128:(oc + 1) * 128], ya[:, kc, :], kc == 0, kc == 3, rd=[("wa", kc, oc // 4), yak], wr=[pkA])
        psB, pkB = cx.psum()
        for kc in range(4):
            cx.mm(psB[:, :], wb[:, kc, oc * 128:(oc + 1) * 128], yb[:, kc, :], kc == 0, kc == 3, rd=[("wb", kc, oc // 4), ybk], wr=[pkB])
        t1, t1k = t1R.next()
        t2, t2k = t2R.next()
        cx.v("dve", "tensor_tensor", [pkA, gmk], [t1k], out=t1[:], in0=psA[:, :], in1=gm[:, oc, :], op=ALU.mult)
        cx.v("dve", "tensor_tensor", [pkB, gmk], [t2k], out=t2[:], in0=psB[:, :], in1=gm[:, 8 + oc, :], op=ALU.mult)
        yield
        cx.v("pool", "tensor_tensor", [t1k, t2k], [(mixk, oc)], out=mix[:, oc, :], in0=t1[:], in1=t2[:], op=ALU.add)

    def a_item(s, a):
        st = ST[s]
        for _ in range(4):
            yield
        mix, mixk = st["mix"], st["mixk"]
        mixall = [(mixk, oc) for oc in range(8)]
        m, mk = mR.next()
        for half in range(2):
            ps, pk = cx.psum()
            for kc in range(8):
                cx.mm(ps[:, :], mix[:, kc, a * 128:(a + 1) * 128], wo[:, kc, half * 512:(half + 1) * 512],
                      kc == 0, kc == 7, rd=[("wo", kc, half)] + mixall, wr=[pk])
            cx.act(m[:, half * 512:(half + 1) * 512], ps[:, :], AF.Copy, rd=[pk], wr=[mk])
        tk = slice(s * 512 + a * 128, s * 512 + (a + 1) * 128)
        yield from epilogue_gen(cx, m, mk, st["x"][:, a, :], st["xk"], normw, "normw", io["out"][tk, :], er)

    def items():
        yield load_item(0)
        for s in range(NS):
            for oc in range(8):
                yield oc_item(s, oc)
                if oc == 2 and s + 1 < NS:
                    yield load_item(s + 1)
            for a in range(4):
                yield a_item(s, a)

    pipeline(items())
    cx.end()


def norm_T_gen(cx, xb, xk, nsub, normw, nk, rots, st, do_norm=True):
    C = cx.consts
    ssr, msr, rsr, ubr, uTr, junk = rots
    uTb, uTk = uTr.next()
    st["uT"], st["uTall"] = uTb, [(uTk, a) for a in range(nsub)]
    if do_norm:
        ssb, ssk = ssr.next()
        msb, msk = msr.next()
        rsb, rsk = rsr.next()
        for a in range(nsub):
            cx.act(junk[:], xb[:, a, :], AF.Square, rd=[xk], wr=["junk", (ssk, a)], accum_out=ssb[:, a:a + 1])
        cx.v("dve", "tensor_scalar", [(ssk, a) for a in range(nsub)], [msk], out=msb[:, 0:nsub], in0=ssb[:, 0:nsub],
             scalar1=1.0 / D, scalar2=EPS, op0=ALU.mult, op1=ALU.add)
        cx.v("pool", "tensor_tensor", [msk, "consts"], [rsk], out=rsb[:, 0:nsub], in0=msb[:, 0:nsub],
             in1=C["negh_col"][:, 0:1].to_broadcast([128, nsub]), op=ALU.pow)
        yield
    us = []
    for a in range(nsub):
        u, uk = ubr.next()
        us.append((u, uk))
        if do_norm:
            cx.v("dve", "scalar_tensor_tensor", [xk, rsk, nk], [uk], out=u[:], in0=xb[:, a, :],
                 scalar=rsb[:, a:a + 1], in1=normw[:], op0=ALU.mult, op1=ALU.mult)
        else:
            cx.v("pool", "tensor_copy", [xk], [uk], out=u[:], in_=xb[:, a, :])
    yield
    for a in range(nsub):
        u, uk = us[a]
        ps, pk = cx.psum()
        psb = ps[:].bitcast(BF16)
        for kc in range(8):
            cx.tr(psb[:, kc * 128:(kc + 1) * 128], u[:, kc * 128:(kc + 1) * 128], C["ident_bf"][:],
                  rd=[uk, "consts"], wr=[pk])
        cx.act(uTb[:, :, a * 128:(a + 1) * 128], psb.rearrange("p (k t) -> p k t", k=8), AF.Copy,
               rd=[pk], wr=[(uTk, a)])


def make_norm_rots2(cx, nsub, nu):
    return (Rot(cx, "ss", [128, 4], F32, 2), Rot(cx, "ms", [128, 4], F32, 2), Rot(cx, "rstd", [128, 4], F32, 2),
            Rot(cx, "ub", [128, D], BF16, nu), Rot(cx, "uT", [128, 8, nsub * 128], BF16, 2),
            cx.sb("junk", [128, D], BF16))


def epilogue_gen(cx, m, mk, resid, rk, wB, wk, dst, er):
    ssr, junk = er
    ssb, ssk = ssr.next()
    cx.act(junk[:], m[:], AF.Square, rd=[mk], wr=["junk", ssk], accum_out=ssb[:, 0:1])
    yield
    cx.v("dve", "tensor_scalar", [ssk], [ssk], out=ssb[:, 1:2], in0=ssb[:, 0:1],
         scalar1=1.0 / D, scalar2=EPS, op0=ALU.mult, op1=ALU.add)
    yield
    cx.v("pool", "tensor_tensor", [ssk, "consts"], [ssk], out=ssb[:, 2:3], in0=ssb[:, 1:2],
         in1=cx.consts["negh_col"][:, 0:1], op=ALU.pow)
    yield
    cx.v("dve", "scalar_tensor_tensor", [mk, ssk, wk], [mk], out=m[:], in0=m[:],
         scalar=ssb[:, 2:3], in1=wB[:], op0=ALU.mult, op1=ALU.mult)
    yield
    if resid is None:
        cx.dma("pool", dst, m[:], rd=[mk], wr=[], accum_op=ALU.add)
    else:
        cx.v("pool", "tensor_tensor", [mk, rk], [mk], out=m[:], in0=m[:], in1=resid, op=ALU.add)
        cx.dma("sp", dst, m[:], rd=[mk], wr=[])


def phase_F(cx, io):
    nc, T = cx.nc, cx.T
    cx.begin()
    W = 256
    NSUB = W // 128
    NS = T // W
    wu = load_w_bf16(cx, "wu", io["w_up"], D, 2 * DFF, "wu", 1408, [0, 2, 1, 3])
    wd = load_w_bf16(cx, "wd", io["w_down"], DFF, D, "wd", 512)
    npre = cx.sb("npre", [128, D], F32)
    npost = cx.sb("npost", [128, D], F32)
    cx.dma("sp", npre[:], io["norm_ffn_pre"].partition_broadcast(128), rd=[], wr=["npre"])
    cx.dma("sp", npost[:], io["norm_ffn_post"].partition_broadcast(128), rd=[], wr=["npost"])
    cf = cx.sb("cf", [128, 3, 44], F32)
    cb = cx.sb("cb", [128, 44], F32)
    carry = cx.sb("carry", [128, 44, 2], F32)
    cx.v("pool", "memset", [], ["carry"], carry[:], 0.0)
    hR = Rot(cx, "h", [128, NSUB, D], F32, 1)
    rots = make_norm_rots2(cx, NSUB, 2)
    actR = Rot(cx, "actT", [128, 22, W], BF16, 2)
    fbR = Rot(cx, "fb", [128, 2, W + 2], F32, 3)
    gR = Rot(cx, "g", [128, W], F32, 8)
    mR = Rot(cx, "m", [128, D], F32, 2)
    er = (Rot(cx, "ess", [128, 4], F32, 4), rots[5])
    ST = [dict() for _ in range(NS)]
    junkF = rots[5][:].bitcast(F32)
    cfr = junkF[0:44, :].rearrange("p (k e) -> p k e", k=4)
    idf = mR.bufs[0][0:44, 0:44]
    idk = mR.keys[0]
    cx.dma("sp", idf, io["c_ident_f"][0:44, 0:44], rd=[], wr=[idk])
    for k in range(3):
        cx.dma("sp", cfr[:, k, :], io["conv_ffn"][k, :].rearrange("(c p) -> c p", p=128), rd=[], wr=["junk"])
    cx.dma("sp", cfr[:, 3, :], io["conv_ffn_b"].rearrange("(c p) -> c p", p=128), rd=[], wr=["junk"])
    for k in range(4):
        ps_, pk_ = cx.psum()
        cx.tr(ps_[:, 0:44], cfr[:, k, :], idf, rd=["junk", idk], wr=[pk_])
        dst_ = cf[:, k, :] if k < 3 else cb[:]
        cx.act(dst_, ps_[:, 0:44], AF.Copy, rd=[pk_], wr=["cf"])

    def hload_item(s):
        st = ST[s]
        tok = slice(s * W, (s + 1) * W)
        st["h"], st["hk"] = hR.next()
        cx.dma("sp", st["h"][:], io["out"][tok, :].rearrange("(a p) d -> p a d", p=128), rd=[], wr=[st["hk"]])
        yield

    def n_item(s):
        st = ST[s]
        st["act"], st["actk"] = actR.next()
        yield from norm_T_gen(cx, st["h"], st["hk"], NSUB, npre, "npre", rots, st)

    def pair_item(s, i):
        st = ST[s]
        yield
        yield
        uTb, uTall = st["uT"], st["uTall"]
        actT, actk = st["act"], st["actk"]
        fb, fk = fbR.next()
        gs_ = []
        for j, c in enumerate((i, 22 + i)):
            ps, pk = cx.psum()
            for kc in range(8):
                cx.mm(ps[:, 0:W], wu[:, kc, c * 128:(c + 1) * 128], uTb[:, kc, :], kc == 0, kc == 7,
                      rd=[("wu", kc, c // 11)] + uTall, wr=[pk])
            g, gk = gR.next()
            cx.act(fb[:, j, 2:W + 2], ps[:, 0:W], AF.Copy, rd=[pk], wr=[(fk, j)])
            cx.act(g[:], ps[:, 0:W], AF.Identity, rd=[pk, "cf"], wr=[gk], scale=cf[:, 2, c:c + 1], bias=cb[:, c:c + 1])
            gs_.append((c, g, gk))
        cv = carry[:, :, :].rearrange("p (j c) k -> p j c k", j=2)[:, :, i, :]
        cx.v("pool", "tensor_copy", ["carry", ("carry", i)], [(fk, 2)], out=fb[:, :, 0:2], in_=cv)
        yield
        for k in (0, 1):
            for j, (c, g, gk) in enumerate(gs_):
                cx.v("dve", "scalar_tensor_tensor", [(fk, j), (fk, 2), "cf", gk], [gk], out=g[:], in0=fb[:, j, k:k + W],
                     scalar=cf[:, k, c:c + 1], in1=g[:], op0=ALU.mult, op1=ALU.add)
        cx.v("pool", "tensor_copy", [(fk, 0), (fk, 1)], [("carry", i)], out=cv, in_=fb[:, :, W:W + 2])
        yield
        (_, g, gk), (_, vv, vk) = gs_
        cx.act(g[:], g[:], AF.Gelu_apprx_tanh, rd=[gk], wr=[gk])
        yield
        cx.v("pool", "tensor_tensor", [gk, vk], [(actk, i)], out=actT[:, i, :], in0=g[:], in1=vv[:], op=ALU.mult)

    def down_item(s, a):
        st = ST[s]
        for _ in range(6):
            yield
        actT, actk = st["act"], st["actk"]
        actall = [(actk, i) for i in range(22)]
        m, mk = mR.next()
        for half in range(2):
            ps, pk = cx.psum()
            for i in range(22):
                cx.mm(ps[:, :], actT[:, i, a * 128:(a + 1) * 128], wd[:, i, half * 512:(half + 1) * 512],
                      i == 0, i == 21, rd=[("wd", i, half)] + actall, wr=[pk])
            cx.act(m[:, half * 512:(half + 1) * 512], ps[:, :], AF.Copy, rd=[pk], wr=[mk])
        tk = slice(s * W + a * 128, s * W + (a + 1) * 128)
        yield from epilogue_gen(cx, m, mk, None, None, npost, "npost", io["out"][tk, :], er)

    def items():
        yield hload_item(0)
        yield n_item(0)
        for s in range(NS):
            for i in range(22):
                yield pair_item(s, i)
                if i == 2 and s + 1 < NS:
                    yield hload_item(s + 1)
                if i == 12 and s + 1 < NS:
                    yield n_item(s + 1)
                if i == 5 and s > 0:
                    for a in range(NSUB):
                        yield down_item(s - 1, a)
        for _ in range(8):
            yield iter(())
        for a in range(NSUB):
            yield down_item(NS - 1, a)

    pipeline(items())
    cx.end()


def phase_P(cx, io):
    nc, T = cx.nc, cx.T
    cx.begin()
    C = cx.consts
    NS = T // 512
    wg = load_w_bf16(cx, "wg", io["w_ple_gate"], D, D, "wg", 512)
    wp = load_w_bf16(cx, "wp", io["w_ple"], 256, D, "wp", 512)
    npost = cx.sb("npost", [128, D], F32)
    cx.dma("sp", npost[:], io["norm_ple_post"].partition_broadcast(128), rd=[], wr=["npost"])
    hR = Rot(cx, "h", [128, 4, D], F32, 1)
    pR = Rot(cx, "p", [128, 4, 256], F32, 1)
    pbR = Rot(cx, "pb", [128, 256], BF16, 4)
    pTR = Rot(cx, "pT", [128, 2, 512], BF16, 2)
    rots = make_norm_rots2(cx, 4, 4)
    sgR = Rot(cx, "sg", [128, 512], F32, 3)
    mR = Rot(cx, "m", [128, D], F32, 6)
    er = (Rot(cx, "ess", [128, 4], F32, 6), rots[5])
    ST = [dict() for _ in range(NS)]

    def hload_item(s):
        st = ST[s]
        tok = slice(s * 512, (s + 1) * 512)
        st["hb"], st["hk"] = hR.next()
        st["pb"], st["pk_"] = pR.next()
        cx.dma("sp", st["hb"][:], io["out"][tok, :].rearrange("(a p) d -> p a d", p=128), rd=[], wr=[st["hk"]])
        cx.dma("sp", st["pb"][:], io["p"][tok, :].rearrange("(a p) d -> p a d", p=128), rd=[], wr=[st["pk_"]])
        yield

    def n_item(s):
        st = ST[s]
        hb, hk, pb, pk_ = st["hb"], st["hk"], st["pb"], st["pk_"]
        pT, pTk = pTR.next()
        st["pT"], st["pTall"] = pT, [(pTk, a) for a in range(4)]
        pbs = []
        for a in range(4):
            pbb, pbk = pbR.next()
            pbs.append((pbb, pbk))
            cx.v("pool", "tensor_copy", [pk_], [pbk], out=pbb[:], in_=pb[:, a, :])
        gen = norm_T_gen(cx, hb, hk, 4, None, None, rots, st, do_norm=False)
        next(gen)
        yield
        for a in range(4):
            pbb, pbk = pbs[a]
            ps, pk = cx.psum()
            psb = ps[:].bitcast(BF16)
            for kc in range(2):
                cx.tr(psb[:, kc * 128:(kc + 1) * 128], pbb[:, kc * 128:(kc + 1) * 128], C["ident_bf"][:],
                      rd=[pbk, "consts"], wr=[pk])
            cx.act(pT[:, :, a * 128:(a + 1) * 128], psb[:, 0:256].rearrange("p (k t) -> p k t", k=2), AF.Copy,
                   rd=[pk], wr=[(pTk, a)])
        for _ in gen:
            yield

    def a_item(s, a, half):
        st = ST[s]
        for _ in range(3):
            yield
        hT, hTall, pT, pTall = st["uT"], st["uTall"], st["pT"], st["pTall"]
        if half == 0:
            st[("m", a)] = mR.next()
        m, mk = st[("m", a)]
        hs = slice(half * 512, (half + 1) * 512)
        ps, pk = cx.psum()
        for kc in range(8):
            cx.mm(ps[:, :], hT[:, kc, a * 128:(a + 1) * 128], wg[:, kc, hs], kc == 0, kc == 7,
                  rd=[("wg", kc, half)] + hTall, wr=[pk])
        sg, sgk = sgR.next()
        cx.act(sg[:], ps[:, :], AF.Sigmoid, rd=[pk], wr=[sgk])
        ps2, pk2 = cx.psum()
        for kc in range(2):
            cx.mm(ps2[:, :], pT[:, kc, a * 128:(a + 1) * 128], wp[:, kc, hs], kc == 0, kc == 1,
                  rd=[("wp", kc, half)] + pTall, wr=[pk2])
        cx.v("dve", "tensor_tensor", [pk2, sgk], [mk], out=m[:, hs], in0=ps2[:, :], in1=sg[:], op=ALU.mult)
        if half == 1:
            tk = slice(s * 512 + a * 128, s * 512 + (a + 1) * 128)
            yield
            yield from epilogue_gen(cx, m, mk, None, None, npost, "npost", io["out"][tk, :], er)

    def items():
        yield hload_item(0)
        yield n_item(0)
        if NS > 1:
            yield hload_item(1)
        for s in range(NS):
            for a in range(4):
                for half in range(2):
                    yield a_item(s, a, half)
                if a == 0 and s + 1 < NS:
                    yield n_item(s + 1)
                if a == 1 and s + 2 < NS:
                    yield hload_item(s + 2)

    pipeline(items())
    cx.end()


def phase_G(cx, io):
    nc, T = cx.nc, cx.T
    cx.begin()
    C = cx.consts
    NB = 2
    NBLK = T // 512
    id64 = C["ident_bf"][0:64, 0:64]
    tri = cx.sb("tri", [64, 64], F32)
    stri = cx.sb("stri", [64, 64], F32)
    msk = cx.sb("msk", [64, 2, 64], F32)
    onesf = cx.sb("onesf", [64, 128], F32)
    identB = cx.sb("identB", [64, 64], F32)
    ab = cx.sb("ab", [64, 8], F32)
    nexpA = cx.sb("nexpA", [64, 4], F32)
    gnw = cx.sb("gnw", [64, 128], F32)
    one_col = cx.sb("one_col", [128, 1], F32)
    cx.dma("sp", tri[:], io["c_tri"][:, :], rd=[], wr=["gc"])
    cx.dma("sp", stri[:], io["c_stri"][:, :], rd=[], wr=["gc"])
    cx.dma("sp", msk[:], io["c_msk"][:, :, :], rd=[], wr=["gc"])
    cx.dma("sp", identB[:], io["c_ident_f"][0:64, 0:64], rd=[], wr=["gc"])
    cx.dma("sp", ab[:, 0:4], io["a_log"].partition_broadcast(64), rd=[], wr=["gc"])
    cx.dma("sp", ab[:, 4:8], io["dt_bias"].partition_broadcast(64), rd=[], wr=["gc"])
    cx.dma("sp", gnw[:], io["gdn_norm"].partition_broadcast(64), rd=[], wr=["gc"])
    cx.v("pool", "memset", [], ["gc"], onesf[:], 1.0)
    cx.v("pool", "memset", [], ["gc"], one_col[:], 1.0)
    cx.act(nexpA[:], ab[:, 0:4], AF.Exp, rd=["gc"], wr=["gc2"])
    cx.v("dve", "tensor_scalar", ["gc2"], ["gc2"], out=nexpA[:], in0=nexpA[:], scalar1=-1.0, scalar2=None, op0=ALU.mult)
    S32 = cx.sb("S32", [128, 4, 128], F32)
    Sb = cx.sb("Sb", [128, 4, 128], BF16)
    cx.v("pool", "memset", [], ["S32"], S32[:], 0.0)
    cx.v("pool", "memset", [], ["Sb"], Sb[:], 0.0)

    qR = Rot(cx, "gq", [128, 4, 512], BF16, 2)
    kR = Rot(cx, "gk", [128, 4, 512], BF16, 2)
    vR = Rot(cx, "gv", [64, 8, 512], BF16, 2)
    zR = Rot(cx, "gz", [64, 8, 512], BF16, 2)
    smR = Rot(cx, "gsm", [64, 8, 8], F32, 2)
    betaR = Rot(cx, "beta", [64, 8, 4], F32, 2)
    g8R = Rot(cx, "g8", [64, 8, 4], F32, 2)
    yaTR = Rot(cx, "yaT", [128, 4, 512], BF16, 2)
    def R2(name, shape, dt, n=2):
        return Rot(cx, name, shape, dt, n)
    ktokR = R2("ktok", [64, NB, 4, 128], BF16)
    ggR = R2("gg", [64, 16], F32)
    gamR = R2("gam", [64, NB, 4], F32, 3)
    tailR = R2("tail", [64, NB, 4], F32)
    glR = R2("gl", [128, NB, 4], F32, 3)
    bgR = R2("bg", [64, NB, 4], F32)
    gTriR = R2("gTri", [64, NB * 4, 64], F32)
    decR = R2("dec", [64, NB * 4, 64], F32)
    decmR = R2("decm", [64, 2, NB * 4, 64], F32)
    tmpR = R2("ltmp", [64, NB, 4, 64], F32, 6)
    pqR = [R2(f"pq{i}", [64, NB, 4, 64], BF16, 8) for i in range(2)]
    qkR = R2("qk", [64, NB, 4, 64], BF16)
    qkTR = R2("qkT", [64, NB, 4, 64], BF16, 3)
    T32R = R2("T32", [64, NB, 4, 64], F32)
    TbR = R2("Tb", [64, NB, 4, 64], BF16, 6)
    TfR = R2("Tf", [64, NB, 4, 64], BF16, 3)
    RkR = R2("Rk", [64, NB, 4, 128], BF16)
    RvR = R2("Rv", [64, NB, 4, 128], BF16, 3)
    ktR = R2("ktail", [64, NB, 4, 128], BF16, 3)
    wTR = R2("wT", [128, NB, 4, 64], BF16, 3)
    uR = R2("u", [64, 4, 128], BF16)
    o1R = R2("o1", [64, 4, 128], F32, 6)
    sqR = R2("osq", [64, 4, 128], F32, 4)
    ossR = R2("oss", [64, 8], F32)
    yaR = R2("ya", [64, 4, 128], BF16)

    def bc(ap, shape):
        return ap.to_broadcast(shape)

    BS = [dict() for _ in range(NBLK)]

    def load_block(blk):
        B = BS[blk]
        tok = slice(blk * 512, (blk + 1) * 512)
        q4, qk_ = qR.next()
        k4, kk_ = kR.next()
        v8, vk_ = vR.next()
        z8, zk_ = zR.next()
        sm, smk = smR.next()
        cx.dma("sp", q4[:], io["gq"][:, :, tok].rearrange("h d t -> d h t"), rd=[], wr=[qk_])
        cx.dma("sp", k4[:], io["gk"][:, :, tok].rearrange("h d t -> d h t"), rd=[], wr=[kk_])
        cx.dma("sp", v8[:], io["gv"][tok, :].rearrange("(c p) e -> p c e", p=64), rd=[], wr=[vk_])
        cx.dma("sp", z8[:], io["gz"][tok, :].rearrange("(c p) e -> p c e", p=64), rd=[], wr=[zk_])
        cx.dma("sp", sm[:], io["gsm"][tok, 0:8].rearrange("(c p) e -> p c e", p=64), rd=[], wr=[smk])
        cx.act(z8[:], z8[:], AF.Silu, rd=[zk_], wr=[zk_])
        beta8, bk = betaR.next()
        g8, g8k = g8R.next()
        cx.act(beta8[:], sm[:, :, 0:4], AF.Sigmoid, rd=[smk], wr=[bk])
        cx.v("dve", "tensor_tensor", [smk, "gc"], [g8k], out=g8[:], in0=sm[:, :, 4:8],
             in1=bc(ab[:, 4:8].unsqueeze(1), [64, 8, 4]), op=ALU.add)
        cx.act(g8[:], g8[:], AF.Exp, rd=[g8k], wr=[g8k])
        cx.act(g8[:], g8[:], AF.Ln, rd=[g8k, "gc"], wr=[g8k], bias=one_col[0:64, 0:1])
        cx.v("dve", "tensor_tensor", [g8k, "gc2"], [g8k], out=g8[:], in0=g8[:],
             in1=bc(nexpA[:].unsqueeze(1), [64, 8, 4]), op=ALU.mult)
        yaT, yaTk = yaTR.next()

        B.update(q4=q4, qk_=qk_, k4=k4, kk_=kk_, v8=v8, vk_=vk_, z8=z8, zk_=zk_, beta8=beta8, bk=bk, g8=g8, g8k=g8k, yaT=yaT, yaTk=yaTk)

    def prep(blk, pc, st):
        if pc == 0:
            load_block(blk)
        B = BS[blk]
        q4, qk_, k4, kk_, v8, vk_, z8, zk_, beta8, bk, g8, g8k = (B[k] for k in ('q4', 'qk_', 'k4', 'kk_', 'v8', 'vk_', 'z8', 'zk_', 'beta8', 'bk', 'g8', 'g8k'))
        yield
        cs = slice(pc * NB, (pc + 1) * NB)
        ps, pk = cx.psum()
        psb = ps[0:64, :].bitcast(BF16)
        for cb in range(NB):
            for h in range(4):
                cc = (pc * NB + cb) * 64
                i0 = (cb * 4 + h) * 128
                cx.tr(psb[:, i0:i0 + 128], k4[:, h, cc:cc + 64], C["ident_bf"][:], rd=[kk_, "consts"], wr=[pk])
        ktok, ktk = ktokR.next()
        cx.act(ktok[:].rearrange("p a h e -> p (a h e)"), psb[:, :], AF.Copy, rd=[pk], wr=[ktk])
        yield
        gsl = g8[:, cs, :].rearrange("p a h -> p (a h)")
        ps, pk = cx.psum()
        cx.mm(ps[0:64, 0:8], tri[:, :], gsl, True, True, rd=["gc", g8k], wr=[pk])
        cx.mm(ps[0:64, 8:16], onesf[:, 0:64], gsl, True, True, rd=["gc", g8k], wr=[pk])
        cx.mm(ps[:, 16:24], onesf[:, :], gsl, True, True, rd=["gc", g8k], wr=[pk])
        gg, ggk = ggR.next()
        gam, gamk = gamR.next()
        tail, tailk = tailR.next()
        gl, glk = glR.next()
        cx.act(gg[:], ps[0:64, 0:16], AF.Copy, rd=[pk], wr=[ggk])
        cx.act(gl[:].rearrange("p a h -> p (a h)"), ps[:, 16:24], AF.Exp, rd=[pk], wr=[glk])
        yield
        cx.act(gam[:].rearrange("p a h -> p (a h)"), gg[:, 0:8], AF.Exp, rd=[ggk], wr=[gamk])
        cx.v("dve", "tensor_tensor", [ggk], [tailk], out=tail[:].rearrange("p a h -> p (a h)"), in0=gg[:, 8:16],
             in1=gg[:, 0:8], op=ALU.subtract)
        cx.act(tail[:], tail[:], AF.Exp, rd=[tailk], wr=[tailk])
        bg, bgk = bgR.next()
        cx.v("dve", "tensor_tensor", [bk, gamk], [bgk], out=bg[:], in0=beta8[:, cs, :], in1=gam[:], op=ALU.mult)
        gTri, gTk = gTriR.next()
        cx.v("dve", "tensor_tensor", ["gc", g8k], [gTk], out=gTri[:], in0=bc(stri[:].unsqueeze(1), [64, NB * 4, 64]),
             in1=bc(gsl.unsqueeze(2), [64, NB * 4, 64]), op=ALU.mult)
        yield
        ps, pk = cx.psum()
        cx.mm(ps[0:64, 0:NB * 4 * 64], tri[:, :], gTri[:].rearrange("p a j -> p (a j)"), True, True, rd=[gTk, "gc"], wr=[pk])
        dec, deck = decR.next()
        cx.act(dec[:].rearrange("p a j -> p (a j)"), ps[0:64, :], AF.Exp, rd=[pk], wr=[deck])
        decm, dmk = decmR.next()
        for s_ in range(2):
            cx.v("pool", "tensor_tensor", [deck, "gc"], [(dmk, s_)], out=decm[:, s_], in0=dec[:],
                 in1=bc(msk[:, s_:s_ + 1, :], [64, NB * 4, 64]), op=ALU.mult)
        yield
        psk, pkk = cx.psum()
        psq, pkq = cx.psum()
        for cb in range(NB):
            for h in range(4):
                cc = (pc * NB + cb) * 64
                i0 = (cb * 4 + h) * 64
                cx.mm(psk[0:64, i0:i0 + 64], k4[:, h, cc:cc + 64], k4[:, h, cc:cc + 64], True, True, rd=[kk_], wr=[pkk])
                cx.mm(psq[0:64, i0:i0 + 64], q4[:, h, cc:cc + 64], k4[:, h, cc:cc + 64], True, True, rd=[kk_, qk_], wr=[pkq])
        ltmp, ltk = tmpR.next()
        cx.v("dve", "tensor_tensor", [pkk, (dmk, 0)], [ltk], out=ltmp[:].rearrange("p a h j -> p (a h j)"),
             in0=psk[0:64, :], in1=decm[:, 0].rearrange("p a j -> p (a j)"), op=ALU.mult)
        Q1, Q1k = pqR[1].next()
        cx.v("dve", "tensor_tensor", [ltk, bk], [Q1k], out=Q1[:], in0=ltmp[:],
             in1=bc(beta8[:, cs, :].unsqueeze(3), [64, NB, 4, 64]), op=ALU.mult)
        qkm, qkmk = qkR.next()
        ltmp2, ltk2 = tmpR.next()
        cx.v("dve", "tensor_tensor", [pkq, (dmk, 1)], [ltk2], out=ltmp2[:].rearrange("p a h j -> p (a h j)"),
             in0=psq[0:64, :], in1=decm[:, 1].rearrange("p a j -> p (a j)"), op=ALU.mult)
        cx.act(qkm[:], ltmp2[:], AF.Copy, rd=[ltk2], wr=[qkmk])
        yield
        ps, pk = cx.psum()
        psb = ps[0:64, :].bitcast(BF16)
        for cb in range(NB):
            for h in range(4):
                i0 = (cb * 4 + h) * 64
                cx.tr(psb[:, i0:i0 + 64], Q1[:, cb, h, :], id64, rd=[Q1k, "consts"], wr=[pk])
                cx.tr(psb[:, 512 + i0:512 + i0 + 64], qkm[:, cb, h, :], id64, rd=[qkmk, "consts"], wr=[pk])
        P1, P1k = pqR[0].next()
        qkT, qkTk = qkTR.next()
        cx.act(P1[:].rearrange("p a h j -> p (a h j)"), psb[:, 0:512], AF.Copy, rd=[pk], wr=[P1k])
        cx.act(qkT[:].rearrange("p a h j -> p (a h j)"), psb[:, 512:1024], AF.Copy, rd=[pk], wr=[qkTk])
        T32, T32k = T32R.next()
        Tb, Tbk = TbR.next()
        cx.v("dve", "tensor_tensor", ["gc", P1k], [T32k], out=T32[:].rearrange("p a h j -> p (a h) j"),
             in0=bc(identB[:].unsqueeze(1), [64, NB * 4, 64]), in1=P1[:].rearrange("p a h j -> p (a h) j"), op=ALU.subtract)
        cx.act(Tb[:], T32[:], AF.Copy, rd=[T32k], wr=[Tbk])
        Pm, Pmk, Qm, Qmk = P1, P1k, Q1, Q1k
        yield
        for step in range(5):
            last = step == 4
            psQ, pkQ = cx.psum()
            for cb in range(NB):
                for h in range(4):
                    i0 = (cb * 4 + h) * 64
                    cx.mm(psQ[0:64, i0:i0 + 64], Pm[:, cb, h, :], Qm[:, cb, h, :], True, True, rd=[Pmk, Qmk], wr=[pkQ])
            Qn, Qnk = pqR[1].next()
            cx.act(Qn[:].rearrange("p a h j -> p (a h j)"), psQ[0:64, :], AF.Copy, rd=[pkQ], wr=[Qnk])
            if not last:
                psP, pkP = cx.psum()
                for cb in range(NB):
                    for h in range(4):
                        i0 = (cb * 4 + h) * 64
                        cx.mm(psP[0:64, i0:i0 + 64], Qm[:, cb, h, :], Pm[:, cb, h, :], True, True, rd=[Pmk, Qmk], wr=[pkP])
                Pn, Pnk = pqR[0].next()
                cx.act(Pn[:].rearrange("p a h j -> p (a h j)"), psP[0:64, :], AF.Copy, rd=[pkP], wr=[Pnk])
            yield
            psT, pkT = cx.psum()
            for cb in range(NB):
                for h in range(4):
                    i0 = (cb * 4 + h) * 64
                    cx.mm(psT[0:64, i0:i0 + 64], Qn[:, cb, h, :], Tb[:, cb, h, :], True, True, rd=[Qnk, Tbk], wr=[pkT])
            cx.v("dve", "tensor_tensor", [T32k, pkT], [T32k], out=T32[:].rearrange("p a h j -> p (a h j)"),
                 in0=T32[:].rearrange("p a h j -> p (a h j)"), in1=psT[0:64, :], op=ALU.add)
            Tb, Tbk = (TfR if last else TbR).next()
            cx.act(Tb[:], T32[:], AF.Copy, rd=[T32k], wr=[Tbk])
            if not last:
                Pm, Pmk = Pn, Pnk
            Qm, Qmk = Qn, Qnk
            yield
        Rk, Rkk = RkR.next()
        Rv, Rvk = RvR.next()
        kt, ktlk = ktR.next()
        cx.v("dve", "tensor_tensor", [ktk, bgk], [Rkk], out=Rk[:], in0=ktok[:],
             in1=bc(bg[:].unsqueeze(3), [64, NB, 4, 128]), op=ALU.mult)
        cx.v("pool", "tensor_tensor", [vk_, bk], [Rvk], out=Rv[:],
             in0=v8[:, cs, :].rearrange("p a (h e) -> p a h e", h=4),
             in1=bc(beta8[:, cs, :].unsqueeze(3), [64, NB, 4, 128]), op=ALU.mult)
        cx.v("pool", "tensor_tensor", [ktk, tailk], [ktlk], out=kt[:], in0=ktok[:],
             in1=bc(tail[:].unsqueeze(3), [64, NB, 4, 128]), op=ALU.mult)
        psw, pkw = cx.psum()
        for cb in range(NB):
            for h in range(4):
                i0 = (cb * 4 + h) * 64
                cx.mm(psw[:, i0:i0 + 64], Rk[:, cb, h, :], Tb[:, cb, h, :], True, True, rd=[Rkk, Tbk], wr=[pkw])
        wT, wTk = wTR.next()
        cx.act(wT[:].rearrange("p a h j -> p (a h j)"), psw[:, :], AF.Copy, rd=[pkw], wr=[wTk], scale=-1.0)
        st.update(wT=wT, wTk=wTk, u0=(Tb, Rv), u0k=(Tbk, Rvk), kt=kt, ktlk=ktlk, qkT=qkT, qkTk=qkTk, gam=gam, gamk=gamk, gl=gl, glk=glk)

    outq = []

    def outgen(blk, pc, cb, o1, o1k):
        B = BS[blk]
        z8, zk_, yaT, yaTk = (B[k] for k in ('z8', 'zk_', 'yaT', 'yaTk'))
        cl = pc * NB + cb
        cc = cl * 64
        sq, sqk = sqR.next()
        oss, ossk = ossR.next()
        cx.v("pool", "tensor_tensor", [o1k], [sqk], out=sq[:], in0=o1[:], in1=o1[:], op=ALU.mult)
        cx.v("dve", "tensor_reduce", [sqk], [ossk], out=oss[:, 0:4], in_=sq[:], axis=AX.X, op=ALU.add)
        cx.v("dve", "tensor_scalar", [ossk], [ossk], out=oss[:, 4:8], in0=oss[:, 0:4], scalar1=1.0 / 128, scalar2=EPS,
             op0=ALU.mult, op1=ALU.add)
        cx.v("pool", "tensor_tensor", [ossk, "consts"], [ossk], out=oss[:, 0:4], in0=oss[:, 4:8],
             in1=C["negh_col"][0:64, 0:1].to_broadcast([64, 4]), op=ALU.pow)
        yield
        cx.v("dve", "tensor_tensor", [o1k, ossk], [o1k], out=o1[:], in0=o1[:],
             in1=bc(oss[:, 0:4].unsqueeze(2), [64, 4, 128]), op=ALU.mult)
        cx.v("pool", "tensor_tensor", [o1k, "gc"], [o1k], out=o1[:], in0=o1[:],
             in1=bc(gnw[:].unsqueeze(1), [64, 4, 128]), op=ALU.mult)
        ya, yak = yaR.next()
        cx.v("pool", "tensor_tensor", [o1k, zk_], [yak], out=ya[:], in0=o1[:], in1=z8[:, cl, :].rearrange("p (h e) -> p h e", h=4), op=ALU.mult)
        yield
        ps, pk = cx.psum()
        psb = ps[:].bitcast(BF16)
        for h in range(4):
            cx.tr(psb[:, h * 64:(h + 1) * 64], ya[:, h, :], id64, rd=[yak, "consts"], wr=[pk])
        cx.act(yaT[:, :, cc:cc + 64], psb[:, 0:256].rearrange("p (h t) -> p h t", h=4), AF.Copy,
               rd=[pk], wr=[(yaTk, cl)])
        if pc == 3 and cb == NB - 1:
            tok = slice(blk * 512, (blk + 1) * 512)
            cx.dma("sp", io["yaT"][:, tok].rearrange("(h e) t -> e h t", h=4), yaT[:], rd=[(yaTk, cl_) for cl_ in range(8)], wr=[])

    def scan(blk, pc, st):
        B = BS[blk]
        q4, qk_, z8, zk_, yaT, yaTk = (B[k] for k in ('q4', 'qk_', 'z8', 'zk_', 'yaT', 'yaTk'))
        wT, wTk, u0, u0k, kt, ktlk, qkT, qkTk, gam, gamk, gl, glk = (st[k] for k in ('wT', 'wTk', 'u0', 'u0k', 'kt', 'ktlk', 'qkT', 'qkTk', 'gam', 'gamk', 'gl', 'glk'))
        for cb in range(NB):
            cl = pc * NB + cb
            cc = cl * 64
            psw2, pkw2 = cx.psum()
            pso1, pko1 = cx.psum()
            (Tf, Rvv), (Tfk, Rvvk) = u0, u0k
            for h in range(4):
                cx.mm(psw2[0:64, h * 128:(h + 1) * 128], Tf[:, cb, h, :], Rvv[:, cb, h, :], True, False, rd=[Rvvk, Tfk], wr=[pkw2])
                cx.mm(psw2[0:64, h * 128:(h + 1) * 128], wT[:, cb, h, :], Sb[:, h, :], False, True, rd=[wTk, "Sb"], wr=[pkw2])
            for h in range(4):
                cx.mm(pso1[0:64, h * 128:(h + 1) * 128], q4[:, h, cc:cc + 64], Sb[:, h, :], True, True, rd=[qk_, "Sb"], wr=[pko1])
            u, uk = uR.next()
            cx.act(u[:].rearrange("p h e -> p (h e)"), psw2[0:64, :], AF.Copy, rd=[pkw2], wr=[uk])
            o1, o1k = o1R.next()
            cx.v("dve", "tensor_tensor", [pko1, gamk], [o1k], out=o1[:], in0=pso1[0:64, :].rearrange("p (h e) -> p h e", h=4),
                 in1=bc(gam[:, cb, :].unsqueeze(2), [64, 4, 128]), op=ALU.mult)
            yield
            psS, pkS = cx.psum()
            for h in range(4):
                cx.mm(psS[:, h * 128:(h + 1) * 128], kt[:, cb, h, :], u[:, h, :], True, True, rd=[ktlk, uk], wr=[pkS])
            pso2, pko2 = cx.psum()
            for h in range(4):
                cx.mm(pso2[0:64, h * 128:(h + 1) * 128], qkT[:, cb, h, :], u[:, h, :], True, True, rd=[qkTk, uk], wr=[pko2])
            cx.v("dve", "tensor_tensor", ["S32", glk], ["S32"], out=S32[:], in0=S32[:],
                 in1=bc(gl[:, cb, :].unsqueeze(2), [128, 4, 128]), op=ALU.mult)
            cx.v("dve", "tensor_tensor", ["S32", pkS], ["S32"], out=S32[:].rearrange("p h e -> p (h e)"),
                 in0=S32[:].rearrange("p h e -> p (h e)"), in1=psS[:, :], op=ALU.add)
            cx.act(Sb[:], S32[:], AF.Copy, rd=["S32"], wr=["Sb"])
            cx.v("dve", "tensor_tensor", [o1k, pko2], [o1k], out=o1[:].rearrange("p h e -> p (h e)"),
                 in0=o1[:].rearrange("p h e -> p (h e)"), in1=pso2[0:64, :], op=ALU.add)
            outq.append((blk * 4 + pc, outgen(blk, pc, cb, o1, o1k)))
            yield
        me = blk * 4 + pc
        while outq and outq[0][0] < me:
            _, og = outq.pop(0)
            for _ in og:
                yield

    pairs = [(blk, pc) for blk in range(NBLK) for pc in range(4)]
    sts = [dict() for _ in pairs]

    def drain(gen):
        for _ in gen:
            pass

    N = len(pairs)
    active = []
    done_prep = set()
    nxt = 0
    scan_i = 0
    scan_gen = None
    while scan_i < N:
        while len(active) < 2 and nxt < N and nxt <= scan_i + 2:
            active.append((nxt, prep(pairs[nxt][0], pairs[nxt][1], sts[nxt])))
            nxt += 1
        if scan_gen is None and scan_i in done_prep:
            scan_gen = scan(pairs[scan_i][0], pairs[scan_i][1], sts[scan_i])
        if scan_gen is not None:
            try:
                next(scan_gen)
            except StopIteration:
                scan_gen = None
                scan_i += 1
        still = []
        for (j, gen) in active:
            try:
                next(gen)
                still.append((j, gen))
            except StopIteration:
                done_prep.add(j)
        active = still
    for _, og in outq:
        for _ in og:
            pass
    cx.end()


def pipeline(gens):
    active = []

    def rnd():
        nonlocal active
        nxt = []
        for a in active:
            try:
                next(a)
                nxt.append(a)
            except StopIteration:
                pass
        active = nxt

    for g in gens:
        active.insert(0, g)
        rnd()
    while active:
        rnd()


def gelu_tanh(cx, x, xk, out, outk, shape, tR):
    t, tk = tR
    cx.v("pool", "tensor_tensor", [xk], [tk], out=t[:], in0=x, in1=x, op=ALU.mult)
    cx.v("pool", "tensor_scalar", [tk], [tk], out=t[:], in0=t[:], scalar1=0.044715, scalar2=1.0, op0=ALU.mult, op1=ALU.add)
    cx.v("pool", "tensor_tensor", [tk, xk], [tk], out=t[:], in0=t[:], in1=x, op=ALU.mult)
    cx.act(t[:], t[:], AF.Sigmoid, rd=[tk], wr=[tk], scale=1.5957691216057308)
    cx.v("dve", "tensor_tensor", [tk, xk], [outk], out=out, in0=t[:], in1=x, op=ALU.mult)


def phase_C(cx, io):
    nc, T = cx.nc, cx.T
    cx.begin()
    C = cx.consts
    NCH = T // 512
    NCP = T // 16
    NT = (NCP + 127) // 128
    NTW = min(NCP, 128)
    kcT = [cx.sb(f"kcT{g}", [64, NT * 128], BF16) for g in range(2)]
    VE = [cx.sb(f"VE{g}", [128, NT, 128], BF16) for g in range(2)]
    cmask = cx.sb("cmaskC", [128, NT, T], BF16)
    alc = cx.sb("alc", [1, 8, T], BF16)
    bcmp = cx.sb("bcmp", [128, NT, 8], F32)
    ones_row = cx.sb("ones_row", [1, 128], BF16)
    cx.dma("sp", cmask[:], io["c_cmaskC"][:, :, :], rd=[], wr=["cc"])
    cx.dma("sp", alc[:], io["c_alc"][:, :, :], rd=[], wr=["cc"])
    cx.dma("sp", bcmp[:], io["c_bcmp"][:, :, :], rd=[], wr=["cc"])
    cx.v("pool", "memset", [], ["cc"], ones_row[:], 1.0)
    for g in range(2):
        cx.v("pool", "memset", [], [("VE", g)], VE[g][:], 0.0)
        cx.v("pool", "memset", [], [("kcT", g)], kcT[g][:], 0.0)
        cx.dma("sp", VE[g][:, :, 0:64], io["c_ovE"][:, 0:NT, :], rd=[], wr=[("VE", g)])
    with contextlib.ExitStack() as es2:
        def sb2(name, shape, dt):
            cx.uid += 1
            return es2.enter_context(nc.sbuf_tensor(f"{name}_{cx.uid}", shape, dt))
        kblk = sb2("kblk", [64, 32, NCP], BF16)
        srcT = sb2("srcT", [64, T], BF16)
        posT = sb2("posT", [64, 32], F32)
        w1s = sb2("w1s", [64, 32, 256], BF16)
        w2s = sb2("w2s", [128, 2, 64], BF16)
        hT = sb2("hT", [128, 2, NCP], BF16)
        hx = sb2("hx", [128, NCP], F32)
        tR = (sb2("gt", [128, NCP], F32), "gt")
        for kv in range(2):
            sfx = "k" if kv == 0 else "v"
            cx.dma("pool", w1s[:], io["cmp_w1_" + sfx].rearrange("(l d) h -> d l h", d=64), rd=[], wr=["w1s"])
            cx.dma("pool", w2s[:], io["cmp_w2_" + sfx].rearrange("(c p) e -> p c e", p=128), rd=[], wr=["w2s"])
            cx.dma("sp", posT[:], io["cmp_pos_" + sfx].rearrange("l d -> d l"), rd=[], wr=["posT"], allow_slow_non_contiguous=True)
            for g in range(2):
                cx.dma("sp", srcT[:], io["featT"][kv, g * 64:(g + 1) * 64, :], rd=[], wr=["srcT"])
                cx.v("pool", "memset", [], ["kblk"], kblk[:, :, NCP - 1:NCP], 0.0)
                sv = srcT[:].rearrange("d (n s) -> d s n", s=16)
                for half in range(2):
                    cx.v("dve", "tensor_tensor", ["srcT", "posT", "kblk"], ["kblk"], out=kblk[:, half * 16:(half + 1) * 16, 0:NCP - 1],
                         in0=sv[:, :, half:half + NCP - 1],
                         in1=posT[:, half * 16:(half + 1) * 16].unsqueeze(2).to_broadcast([64, 16, NCP - 1]), op=ALU.add)
                for hc in range(2):
                    ps, pk = cx.psum()
                    for l in range(32):
                        cx.mm(ps[:, 0:NCP], w1s[:, l, hc * 128:(hc + 1) * 128], kblk[:, l, :], l == 0, l == 31,
                              rd=["w1s", "kblk"], wr=[pk])
                    cx.act(hT[:, hc, :], ps[:, 0:NCP], AF.Gelu_apprx_tanh, rd=[pk], wr=[("hT", hc)])
                if kv == 0:
                    ps, pk = cx.psum()
                    for hc in range(2):
                        cx.mm(ps[0:64, 0:NCP], w2s[:, hc, :], hT[:, hc, :], hc == 0, hc == 1,
                              rd=["w2s", ("hT", 0), ("hT", 1)], wr=[pk])
                    cx.act(kcT[g][:, 0:NCP], ps[0:64, 0:NCP], AF.Copy, rd=[pk], wr=[("kcT", g)])
                else:
                    for nt in range(NT):
                        ps, pk = cx.psum()
                        for hc in range(2):
                            cx.mm(ps[0:NTW, 0:64], hT[:, hc, nt * 128:nt * 128 + NTW], w2s[:, hc, :], hc == 0, hc == 1,
                                  rd=["w2s", ("hT", 0), ("hT", 1)], wr=[pk])
                        cx.act(VE[g][0:NTW, nt, 64:128], ps[0:NTW, 0:64], AF.Copy, rd=[pk], wr=[("VE", g)])
        cx.S.barrier()
    qR = Rot(cx, "qc", [64, 8, 512], BF16, 2)
    gsR = Rot(cx, "gs", [128, 4, 24], F32, 2)
    imR = Rot(cx, "imm", [128, 4, 64], F32, 2)
    iaR = Rot(cx, "ima", [128, 4, 64], F32, 2)
    alwR = Rot(cx, "alw", [64, 8, 512], BF16, 2)
    pR = Rot(cx, "pc", [128, 512], BF16, 4)
    ocR = Rot(cx, "oc", [128, 4, 512], F32, 2)
    impR = Rot(cx, "imp", [128, 4, 2, 64], F32, 2)
    itR = Rot(cx, "impt", [128, 4, 64], F32, 2)
    zR = Rot(cx, "zc", [128, 12], F32, 4)
    m8R = Rot(cx, "m8", [128, 16], F32, 4)
    wkR = Rot(cx, "wk", [128, 64], F32, 4)
    thR = Rot(cx, "thr", [128, 4, 2], F32, 2)
    selR = Rot(cx, "self", [128, 4, 2, 64], F32, 2)
    smR = Rot(cx, "selm", [128, 4, 2, 64], BF16, 2)
    stR = Rot(cx, "selT", [64, 2, 512], BF16, 2)
    mtR = Rot(cx, "MT", [64, 8, 512], BF16, 2)
    cx.ps_lim = 4

    def load_item(c, st):
        tok = slice(c * 512, (c + 1) * 512)
        st["qc"], st["qk"] = qR.next()
        st["gs"], st["gsk"] = gsR.next()
        st["imm"], st["immk"] = imR.next()
        st["ima"], st["imak"] = iaR.next()
        st["alw"], st["alwk"] = alwR.next()
        st["oc"], st["ock"] = ocR.next()
        st["imp"], st["impk"] = impR.next()
        cx.dma("sp", st["qc"][:], io["QN"][:, 0:64, tok].rearrange("h d t -> d h t"), rd=[], wr=[st["qk"]])
        cx.dma("sp", st["gs"][:], io["gsm"][tok, 264:288].rearrange("(a p) e -> p a e", p=128), rd=[], wr=[st["gsk"]])
        cx.dma("sp", st["imm"][:], io["c_impmul"][tok, :].rearrange("(a p) e -> p a e", p=128), rd=[], wr=[st["immk"]])
        cx.dma("sp", st["ima"][:], io["c_impadd"][tok, :].rearrange("(a p) e -> p a e", p=128), rd=[], wr=[st["imak"]])
        cx.dma("sp", st["alw"][:], io["c_ALW"][:, :, tok].rearrange("h j t -> j h t"), rd=[], wr=[st["alwk"]])
        cx.act(st["gs"][:], st["gs"][:], AF.Sigmoid, rd=[st["gsk"]], wr=[st["gsk"]])
        yield

    def job(c, st, h, nt, first, last):
        tok = slice(c * 512, (c + 1) * 512)
        g = h // 4
        pu, puk = cx.psum_fixed(4 + h % 4)
        if first:
            cx.v("dve", "memset", [], [puk], pu[:, :], 0.0)
        ps, pk = cx.psum()
        cx.mm(ps[:, :], kcT[g][:, nt * 128:(nt + 1) * 128], st["qc"][:, h, :], True, False, rd=[("kcT", g), st["qk"]], wr=[pk])
        cx.mm(ps[:, :], ones_row[:, :], alc[:, h, tok], False, False, rd=["cc"], wr=[pk])
        cx.mm(ps[:, :], C["ident_bf"][:], cmask[:, nt, tok], False, True, rd=["cc", "consts"], wr=[pk])
        pc, pck = pR.next()
        cx.act(pc[:], ps[:, :], AF.Exp, rd=[pk, "cc"], wr=[pck], bias=bcmp[:, nt, h:h + 1])
        yield
        yield
        for a in range(4):
            cx.mm(pu[:, a * 128:(a + 1) * 128], pc[:, a * 128:(a + 1) * 128], VE[g][:, nt, :], False, last,
                  rd=[pck, ("VE", g)], wr=[puk])
        if last:
            gs, gsk, oc, ock, imp, impk = st["gs"], st["gsk"], st["oc"], st["ock"], st["imp"], st["impk"]
            pu4 = pu[:, :].rearrange("p (a e) -> p a e", a=4)
            z, zk = zR.next()
            cx.v("dve", "tensor_reduce", [puk], [zk], out=z[:, 0:4], in_=pu4[:, :, 0:64], axis=AX.X, op=ALU.add)
            cx.v("dve", "tensor_scalar", [zk], [zk], out=z[:, 0:4], in0=z[:, 0:4], scalar1=1e-30, scalar2=None, op0=ALU.max)
            cx.v("dve", "reciprocal", [zk], [zk], out=z[:, 4:8], in_=z[:, 0:4])
            cx.v("dve", "tensor_tensor", [zk, gsk], [zk], out=z[:, 8:12], in0=z[:, 4:8], in1=gs[:, :, 3 * h], op=ALU.mult)
            cx.v("dve", "tensor_tensor", [puk, zk], [(ock, h)], out=oc[:, :, h * 64:(h + 1) * 64], in0=pu4[:, :, 64:128],
                 in1=z[:, 8:12].unsqueeze(2).to_broadcast([128, 4, 64]), op=ALU.mult)
            if h % 4 == 0:
                cx.v("dve", "tensor_tensor", [puk, zk], [(impk, g)], out=imp[:, :, g, :], in0=pu4[:, :, 0:64],
                     in1=z[:, 4:8].unsqueeze(2).to_broadcast([128, 4, 64]), op=ALU.mult)
            else:
                it, itk = itR.next()
                cx.v("dve", "tensor_tensor", [puk, zk], [itk], out=it[:], in0=pu4[:, :, 0:64],
                     in1=z[:, 4:8].unsqueeze(2).to_broadcast([128, 4, 64]), op=ALU.mult)
                cx.v("pool", "tensor_tensor", [itk, (impk, g)], [(impk, g)], out=imp[:, :, g, :], in0=imp[:, :, g, :], in1=it[:], op=ALU.add)

    def fin_item(c, st):
        tok = slice(c * 512, (c + 1) * 512)
        yield
        yield
        yield
        oc, ock, imp, impk = st["oc"], st["ock"], st["imp"], st["impk"]
        imm, immk, ima, imak, alw, alwk = st["imm"], st["immk"], st["ima"], st["imak"], st["alw"], st["alwk"]
        cx.dma("sp", io["ocmp"][tok, :].rearrange("(a p) e -> p a e", p=128), oc[:], rd=[(ock, h) for h in range(8)], wr=[])
        thr, thrk = thR.next()
        for g in range(2):
            ik = (impk, g)
            cx.v("dve", "tensor_tensor", [ik, immk], [ik], out=imp[:, :, g, :], in0=imp[:, :, g, :], in1=imm[:], op=ALU.mult)
            cx.v("dve", "tensor_tensor", [ik, imak], [ik], out=imp[:, :, g, :], in0=imp[:, :, g, :], in1=ima[:], op=ALU.add)
        yield
        m8s = []
        for g in range(2):
            for a in range(4):
                m8, m8k = m8R.next()
                wk, wkk = wkR.next()
                iv = imp[:, a, g, :]
                cx.v("dve", "max", [(impk, g)], [m8k], out=m8[:, 0:8], in_=iv)
                cx.v("dve", "match_replace", [(impk, g), m8k], [wkk], out=wk[:], in_to_replace=m8[:, 0:8], in_values=iv, imm_value=-3.0e38)
                cx.v("dve", "max", [wkk, m8k], [m8k], out=m8[:, 8:16], in_=wk[:])
                cx.v("pool", "tensor_copy", [m8k], [(thrk, a, g)], out=thr[:, a, g:g + 1], in_=m8[:, 15:16])
            yield
        selfl, selfk = selR.next()
        selm, selmk = smR.next()
        cx.v("dve", "tensor_tensor", [(impk, 0), (impk, 1)] + [(thrk, a, g) for a in range(4) for g in range(2)], [selfk],
             out=selfl[:], in0=imp[:], in1=thr[:].unsqueeze(3).to_broadcast([128, 4, 2, 64]), op=ALU.is_ge)
        yield
        cx.v("dve", "tensor_scalar", [selfk], [selmk], out=selm[:], in0=selfl[:], scalar1=-NEGM, scalar2=NEGM, op0=ALU.mult, op1=ALU.add)
        yield
        selT, selTk = stR.next()
        for g in range(2):
            ps, pk = cx.psum()
            psb = ps[0:64, :].bitcast(BF16)
            for a in range(4):
                cx.tr(psb[:, a * 128:(a + 1) * 128], selm[:, a, g, :], C["ident_bf"][:], rd=[selmk, "consts"], wr=[pk])
            cx.act(selT[:, g, :], psb[:, 0:512], AF.Copy, rd=[pk], wr=[(selTk, g)])
        yield
        MT, MTk = mtR.next()
        for h in range(8):
            cx.v("pool", "tensor_tensor", [alwk, (selTk, h // 4)], [(MTk, h)], out=MT[:, h, :], in0=alw[:, h, :], in1=selT[:, h // 4, :], op=ALU.add)
        cx.dma("sp", io["QN"][:, 64:128, tok].rearrange("h j t -> j h t"), MT[:], rd=[(MTk, h) for h in range(8)], wr=[])

    def items():
        for c in range(NCH):
            st = {}
            yield load_item(c, st)
            nts = [nt for nt in range(NT) if 16 * nt * 128 + 31 <= c * 512 + 511]
            for h in range(8):
                for nt in nts:
                    yield job(c, st, h, nt, nt == nts[0], nt == nts[-1])
            yield fin_item(c, st)

    pipeline(items())
    cx.ps_lim = 8
    cx.end()


def phase_S(cx, io):
    nc, T = cx.nc, cx.T
    cx.begin()
    C = cx.consts
    NCH = T // 512
    KT = T // 128
    KE = [cx.sb(f"KE{g}", [128, T], BF16) for g in range(2)]
    KW = [cx.sb(f"KW{g}", [128, T], BF16) for g in range(2)]
    Vall = cx.sb("Vall", [128, KT, 260], BF16)
    cmS = cx.sb("cmS", [128, 4, 512], BF16)
    wmS = cx.sb("wmS", [128, 8, 512], BF16)
    bsel = cx.sb("bsel", [128, 8], F32)
    cx.dma("sp", cmS[:], io["c_cmaskS"][:, :, :], rd=[], wr=["sc"])
    cx.dma("sp", wmS[:], io["c_wmask"][:, :, :], rd=[], wr=["sc"])
    cx.dma("sp", bsel[:], io["c_bsel"][:, :], rd=[], wr=["sc"])
    for g in range(2):
        cx.dma("sp", KE[g][0:64, :], io["featT"][2, g * 64:(g + 1) * 64, :], rd=[], wr=[("K", g)])
        cx.dma("sp", KE[g][64:128, :], io["c_E"][:, :], rd=[], wr=[("K", g)])
        cx.dma("sp", KW[g][0:64, :], io["featT"][3, g * 64:(g + 1) * 64, :], rd=[], wr=[("K", g)])
        cx.dma("sp", KW[g][64:128, :], io["c_E"][:, :], rd=[], wr=[("K", g)])
        pass
    for k0 in range(0, KT, 8):
        k1 = min(KT, k0 + 8)
        cx.dma("sp", Vall[:, k0:k1, :], io["gvs"][k0 * 128:k1 * 128, :].rearrange("(k p) e -> p k e", p=128), rd=[], wr=[("V", k0)])
    qnR = Rot(cx, "qn", [128, 8, 512], BF16, 2)
    wnR = Rot(cx, "wn", [128, 8, 512], BF16, 2)
    gsR = Rot(cx, "gs", [128, 4, 24], F32, 2)
    ocR = Rot(cx, "oc", [128, 4, 512], F32, 2)
    pR = Rot(cx, "pp", [128, 512], BF16, 4)
    zR = Rot(cx, "zs", [128, 4], F32, 4)
    ybR = Rot(cx, "ybb", [128, 512], BF16, 2)
    ybTR = Rot(cx, "ybT", [128, 4, 512], BF16, 2)
    cx.ps_lim = 4

    def load_item(c, st):
        tok = slice(c * 512, (c + 1) * 512)
        st["qn"], st["qnk"] = qnR.next()
        st["wn"], st["wnk"] = wnR.next()
        st["gs"], st["gsk"] = gsR.next()
        st["oc"], st["ock"] = ocR.next()
        qn, wn, gs, oc = st["qn"], st["wn"], st["gs"], st["oc"]
        cx.dma("sp", qn[:], io["QN"][:, :, tok].rearrange("h r t -> r h t"), rd=[], wr=[st["qnk"]])
        cx.dma("sp", wn[0:64], io["QN"][:, 0:64, tok].rearrange("h r t -> r h t"), rd=[], wr=[st["wnk"]])
        cx.dma("sp", wn[64:128], io["c_ALW"][:, :, tok].rearrange("h j t -> j h t"), rd=[], wr=[st["wnk"]])
        cx.dma("sp", gs[:], io["gsm"][tok, 264:288].rearrange("(a p) e -> p a e", p=128), rd=[], wr=[st["gsk"]])
        cx.dma("sp", oc[:], io["ocmp"][tok, :].rearrange("(a p) e -> p a e", p=128), rd=[], wr=[st["ock"]])
        cx.act(gs[:], gs[:], AF.Sigmoid, rd=[st["gsk"]], wr=[st["gsk"]])
        yield

    def job(c, st, h, br, kt, first_kt, last_kt):
        g = h // 4
        pso, pko = cx.psum_fixed(4 + 2 * (h % 2) + br)
        if kt == first_kt:
            cx.v("dve", "memset", [], [pko], pso[:, 0:260], 0.0)
        r = kt - 4 * c
        if br == 0:
            Kt, vo, rhs, rk = KE[g], g * 65, st["qn"], st["qnk"]
            a_lo, a_hi = max(0, r), 3
            mask = cmS[:, r, :] if r >= 0 else None
        else:
            Kt, vo, rhs, rk = KW[g], (2 + g) * 65, st["wn"], st["wnk"]
            a_lo, a_hi = max(0, r), min(3, r + 4)
            mask = wmS[:, r + 4, :]
        cs = slice(a_lo * 128, (a_hi + 1) * 128)
        ps, pk = cx.psum()
        cx.mm(ps[:, cs], Kt[:, kt * 128:(kt + 1) * 128], rhs[:, h, cs], True, mask is None, rd=[("K", g), rk], wr=[pk])
        if mask is not None:
            cx.mm(ps[:, cs], C["ident_bf"][:], mask[:, cs], False, True, rd=["sc", "consts"], wr=[pk])
        pp, ppk = pR.next()
        cx.act(pp[:, cs], ps[:, cs], AF.Exp, rd=[pk, "sc"], wr=[ppk], bias=bsel[:, h:h + 1])
        yield
        yield
        for a in range(a_lo, a_hi + 1):
            cx.mm(pso[:, a * 65:(a + 1) * 65], pp[:, a * 128:(a + 1) * 128], Vall[:, kt, vo:vo + 65], False, kt == 4 * c + a,
                  rd=[ppk, ("V", (kt // 8) * 8)], wr=[pko])
        if br == 1 and kt == last_kt:
            ps_s, pk_s = cx.psum_fixed(4 + 2 * (h % 2))
            ps_w, pk_w = pso, pko
            gs, gsk, oc, ock = st["gs"], st["gsk"], st["oc"], st["ock"]
            zs = [zR.next() for a in range(4)]
            for a in range(4):
                z, zk = zs[a]
                cx.v("dve", "tensor_scalar", [pk_s], [zk], out=z[:, 0:1], in0=ps_s[:, a * 65 + 64:a * 65 + 65], scalar1=1e-30, scalar2=None, op0=ALU.max)
            for a in range(4):
                z, zk = zs[a]
                cx.v("dve", "tensor_scalar", [pk_w], [zk], out=z[:, 1:2], in0=ps_w[:, a * 65 + 64:a * 65 + 65], scalar1=1e-30, scalar2=None, op0=ALU.max)
            for a in range(4):
                z, zk = zs[a]
                cx.v("dve", "reciprocal", [zk], [zk], out=z[:, 2:4], in_=z[:, 0:2])
            for a in range(4):
                z, zk = zs[a]
                cx.v("dve", "tensor_tensor", [zk, gsk], [zk], out=z[:, 0:2], in0=z[:, 2:4], in1=gs[:, a, 3 * h + 1:3 * h + 3], op=ALU.mult)
            for a in range(4):
                z, zk = zs[a]
                ov = oc[:, a, h * 64:(h + 1) * 64]
                cx.v("dve", "scalar_tensor_tensor", [pk_s, zk, ock, (ock, a, h)], [(ock, a, h)], out=ov, in0=ps_s[:, a * 65:a * 65 + 64], scalar=z[:, 0:1], in1=ov,
                     op0=ALU.mult, op1=ALU.add)
            for a in range(4):
                z, zk = zs[a]
                ov = oc[:, a, h * 64:(h + 1) * 64]
                cx.v("dve", "scalar_tensor_tensor", [pk_w, zk, ock, (ock, a, h)], [(ock, a, h)], out=ov, in0=ps_w[:, a * 65:a * 65 + 64], scalar=z[:, 1:2], in1=ov,
                     op0=ALU.mult, op1=ALU.add)

    def fin_item(c, st):
        tok = slice(c * 512, (c + 1) * 512)
        yield
        yield
        yield
        oc, ock = st["oc"], st["ock"]
        ybT, ybTk = ybTR.next()
        for a in range(4):
            yb, ybk = ybR.next()
            cx.v("pool", "tensor_copy", [ock] + [(ock, a, h) for h in range(8)], [ybk], out=yb[:], in_=oc[:, a, :])
            ps, pk = cx.psum()
            psb = ps[:].bitcast(BF16)
            for kc in range(4):
                cx.tr(psb[:, kc * 128:(kc + 1) * 128], yb[:, kc * 128:(kc + 1) * 128], C["ident_bf"][:], rd=[ybk, "consts"], wr=[pk])
            cx.act(ybT[:, :, a * 128:(a + 1) * 128], psb[:, 0:512].rearrange("p (k t) -> p k t", k=4), AF.Copy, rd=[pk], wr=[(ybTk, a)])
        cx.dma("sp", io["ybT"][:, tok].rearrange("(k p) t -> p k t", p=128), ybT[:], rd=[(ybTk, a) for a in range(4)], wr=[])

    def items():
        for c in range(NCH):
            st = {}
            yield load_item(c, st)
            for h in range(8):
                for br in range(2):
                    kts = list(range(0, 4 * c + 4)) if br == 0 else list(range(max(0, 4 * c - 4), 4 * c + 4))
                    for kt in kts:
                        yield job(c, st, h, br, kt, kts[0], kts[-1])
            yield fin_item(c, st)

    pipeline(items())
    cx.ps_lim = 8
    cx.end()


def host_consts(T):
    bf = ml_dtypes.bfloat16
    c = {}
    c["ident_bf"] = np.eye(128, dtype=np.float32).astype(bf)
    c["ident_f"] = np.eye(128, dtype=np.float32)
    m = np.arange(64)
    c["tri"] = (m[:, None] <= m[None, :]).astype(np.float32)
    c["stri"] = (m[:, None] > m[None, :]).astype(np.float32)
    c["msk"] = np.stack([(m[:, None] > m[None, :]), (m[:, None] >= m[None, :])], 1).astype(np.float32)
    slopes = 2.0 ** (-(np.arange(8) + 1.0))
    q = np.arange(T)
    NCP = T // 16
    NC = NCP - 1
    NT = (NCP + 127) // 128
    n = np.arange(NT * 128)
    valid = (n[:, None] < NC) & (q[None, :] >= 16 * n[:, None] + 31)
    c["cmaskC"] = np.where(valid, 0.0, NEGM).astype(np.float32).reshape(NT, 128, T).transpose(1, 0, 2).astype(bf)
    c["alc"] = (-slopes[:, None] * 16.0 * (q[None, :] // 16)).astype(np.float32)[None].astype(bf)
    c["bcmp"] = (slopes[None, None, :] * 16.0 * n.reshape(NT, 128).T[:, :, None]).astype(np.float32)
    NS = T // 64
    j = np.arange(NS)
    bs = n * 16
    ov = np.clip(np.minimum(bs[:, None] + 32, j[None, :] * 64 + 64) - np.maximum(bs[:, None], j[None, :] * 64), 0, None) / 32.0
    ov = np.where(n[:, None] < NC, ov, 0.0)
    ovp = np.zeros((NT * 128, 64), np.float32)
    ovp[:, :NS] = ov
    c["ovE"] = ovp.reshape(NT, 128, 64).transpose(1, 0, 2).astype(bf)
    cur = q // 64
    j64 = np.arange(64)
    forced = (j64[None, :] == 0) | (j64[None, :] == cur[:, None]) | (j64[None, :] == cur[:, None] - 1)
    selv = (j64[None, :] <= cur[:, None]) & (j64[None, :] < NS)
    c["impmul"] = (selv & ~forced).astype(np.float32)
    c["impadd"] = np.where(selv, np.where(forced, 1e9, 0.0), -1e30).astype(np.float32)
    c["ALW"] = (-slopes[:, None, None] * 64.0 * (cur[None, None, :] - j64[None, :, None])).astype(np.float32).astype(bf)
    c["E"] = (q[None, :] // 64 == j64[:, None]).astype(np.float32).astype(bf)
    p = np.arange(128)
    qq = np.arange(512)
    c["cmaskS"] = np.stack([np.where((128 * r + p[:, None]) <= qq[None, :], 0.0, NEGM) for r in range(4)], 1).astype(np.float32).astype(bf)
    wm = []
    for r in range(-4, 4):
        dist = qq[None, :] - (128 * r + p[:, None])
        wm.append(np.where((dist >= 0) & (dist < 512), 0.0, NEGM))
    c["wmask"] = np.stack(wm, 1).astype(np.float32).astype(bf)
    c["bsel"] = (slopes[None, :] * (p[:, None] % 64)).astype(np.float32)
    return c


def build(T, phases, debug=False, dbg_in=()):
    nc = bass.Bass("TRN2", target_bir_lowering=False)
    io = {}

    def din(name, shape, dt=F32):
        io[name] = nc.dram_tensor(name, list(shape), dt, kind="ExternalInput").ap()

    def dscr(name, shape, dt):
        kind = "ExternalOutput" if debug else "Internal"
        if debug and name in dbg_in:
            kind = "ExternalInput"
        io[name] = nc.dram_tensor(name, list(shape), dt, kind=kind).ap()

    din("x", [T, D])
    din("p", [T, 256])
    for name, shape in WSHAPES.items():
        din(name, shape)
    hc = host_consts(T)
    for name, arr in hc.items():
        din("c_" + name, arr.shape, BF16 if arr.dtype == ml_dtypes.bfloat16 else F32)
    io["out"] = nc.dram_tensor("out", [T, D], F32, kind="ExternalOutput").ap()
    dscr("gq", [4, 128, T], BF16)
    dscr("gk", [4, 128, T], BF16)
    dscr("gv", [T, 512], BF16)
    dscr("gz", [T, 512], BF16)
    dscr("gsm", [T, 288], F32)
    dscr("gvs", [T, 260], BF16)
    dscr("QN", [8, 128, T], BF16)
    dscr("featT", [4, 128, T], BF16)
    dscr("gmixT", [2048, T], BF16)
    dscr("yaT", [512, T], BF16)
    dscr("ocmp", [T, 512], F32)
    dscr("ybT", [512, T], BF16)

    cx = Ctx(nc, T)
    C = {}
    C["ident_bf"] = cx.sbtop("ident_bf", [128, 128], BF16)
    C["eps_col"] = cx.sbtop("eps_col", [128, 1], F32)
    C["zero_col"] = cx.sbtop("zero_col", [128, 1], F32)
    C["lnq_col"] = cx.sbtop("lnq_col", [128, 1], F32)
    C["negh_col"] = cx.sbtop("negh_col", [128, 1], F32)
    cx.consts = C
    cx.dma("sp", C["ident_bf"][:], io["c_ident_bf"][:, :], rd=[], wr=["consts"])
    cx.v("pool", "memset", [], ["consts"], C["eps_col"][:], EPS)
    cx.v("pool", "memset", [], ["consts"], C["zero_col"][:], 0.0)
    cx.v("pool", "memset", [], ["consts"], C["lnq_col"][:], float(np.log(128.0 ** -0.5)))
    cx.v("pool", "memset", [], ["consts"], C["negh_col"][:], -0.5)
    cx.S.barrier()

    if "A" in phases:
        phase_A(cx, io)
    if "C" in phases:
        phase_C(cx, io)
    if "S" in phases:
        phase_S(cx, io)
    if "G" in phases:
        phase_G(cx, io)
    if "M" in phases:
        phase_M(cx, io)
    if "F" in phases:
        phase_F(cx, io)
    if "P" in phases:
        phase_P(cx, io)
    cx.S.barrier()
    cx.top.close()
    return nc, hc


WSHAPES = {
    "norm_mix_pre": [D], "w_in": [D, DIN], "conv_qkv": [4, 1536], "a_log": [4], "dt_bias": [4],
    "gdn_norm": [128], "cmp_pos_k": [32, 64], "cmp_w1_k": [2048, 256], "cmp_w2_k": [256, 64],
    "cmp_pos_v": [32, 64], "cmp_w1_v": [2048, 256], "cmp_w2_v": [256, 64],
    "w_a2d": [512, D], "w_b2d": [512, D], "w_o": [D, D], "norm_mix_post": [D], "norm_ffn_pre": [D],
    "w_up": [D, 2 * DFF], "conv_ffn": [3, 2 * DFF], "conv_ffn_b": [2 * DFF], "w_down": [DFF, D],
    "norm_ffn_post": [D], "w_ple": [256, D], "w_ple_gate": [D, D], "norm_ple_post": [D],
}


def run(inputs, T, ncores, phases, debug=False, dbg_in=None):
    dbg_in = dbg_in or {}
    nc, hc = build(T, phases, debug, tuple(dbg_in))
    in_maps = []
    for c in range(ncores):
        m = {"x": np.ascontiguousarray(inputs["x"][c]), "p": np.ascontiguousarray(inputs["p"][0, c])}
        for name in WSHAPES:
            m[name] = np.ascontiguousarray(inputs[name][0])
        for name, arr in hc.items():
            m["c_" + name] = arr
        for name, arr in dbg_in.items():
            m[name] = arr
        in_maps.append(m)
    res = run_bass_kernel_spmd(nc, in_maps, core_ids=list(range(ncores)))
    return res.results


def kernel(**inputs):
    inputs = {k: np.asarray(v) for k, v in inputs.items()}
    B, T, _ = inputs["x"].shape
    results = run(inputs, T, B, "ACSGMFP")
    return np.stack([r["out"] for r in results], axis=0).astype(np.float32)
```

```python
import contextlib
import numpy as np
import ml_dtypes
import concourse.bass as bass
import concourse.mybir as mybir
from concourse.bass_utils import run_bass_kernel_spmd

F32 = mybir.dt.float32
BF16 = mybir.dt.bfloat16
AF = mybir.ActivationFunctionType
ALU = mybir.AluOpType
AX = mybir.AxisListType

D = 1024
DIN = 5408
DFF = 2816
EPS = 1e-6
NEGM = -30000.0
NPOOL = 20
import os
SAME_INORDER_ENGS = tuple(x for x in os.environ.get('SAME_INORDER', 'act').split(',') if x)


class Sched:
    def __init__(self, nc):
        self.nc = nc
        self.E = {}
        for name, eng in (("pe", nc.tensor), ("dve", nc.vector), ("act", nc.scalar),
                          ("pool", nc.gpsimd), ("sp", nc.sync)):
            self.E[name] = dict(eng=eng, sem=nc.alloc_semaphore(name="sem_" + name), cnt=0, waited={})
        self.dq = {}
        for q, n_ in (("sp", NPOOL), ("pool", 64)):
            self.dq[q] = dict(sems=[nc.alloc_semaphore(name=f"dq_{q}_{i}") for i in range(n_)],
                              vals=[0] * n_, nxt=0)
        self.res = {}
        self.ninst = 0

    def semof(self, owner):
        if isinstance(owner, tuple):
            return self.dq[owner[1]]["sems"][owner[2]]
        return self.E[owner]["sem"]

    def _wait(self, en, tok):
        owner, val = tok
        if val <= 0:
            return
        e = self.E[en]
        if e["waited"].get(owner, 0) >= val:
            return
        e["eng"].wait_ge(self.semof(owner), val)
        e["waited"][owner] = val
        self.ninst += 1

    def deps(self, en, rd, wr):
        same_ok = en in SAME_INORDER_ENGS
        for k in rd:
            r = self.res.get(k)
            if r and r["w"]:
                t = r["w"]
                if not (t[0] == en and (en == "pe" or same_ok)):
                    self._wait(en, t)
        for k in wr:
            r = self.res.get(k)
            if r:
                if r["w"]:
                    t = r["w"]
                    if not (t[0] == en and (en == "pe" or same_ok)):
                        self._wait(en, t)
                for o, v in r["r"].items():
                    if o != en:
                        self._wait(en, (o, v))

    def commit(self, tok, rd, wr):
        for k in rd:
            r = self.res.setdefault(k, {"w": None, "r": {}})
            if r["r"].get(tok[0], 0) < tok[1]:
                r["r"][tok[0]] = tok[1]
        for k in wr:
            self.res[k] = {"w": tok, "r": {}}

    def op(self, en, meth, *a, rd=(), wr=(), **kw):
        self.deps(en, rd, wr)
        e = self.E[en]
        inst = getattr(e["eng"], meth)(*a, **kw)
        e["cnt"] += 1
        inst.then_inc(e["sem"], 1)
        self.commit((en, e["cnt"]), rd, wr)
        self.ninst += 1

    def dma(self, q, out, in_, rd=(), wr=(), **kw):
        self.deps(q, rd, wr)
        pool = self.dq[q]
        i = pool["nxt"]
        pool["nxt"] = (i + 1) % len(pool["sems"])
        owner = ("d", q, i)
        self._wait(q, (owner, pool["vals"][i]))
        inst = self.E[q]["eng"].dma_start(out=out, in_=in_, **kw)
        pool["vals"][i] += 16
        inst.then_inc(pool["sems"][i], 16)
        self.commit((owner, pool["vals"][i]), rd, wr)
        self.ninst += 1

    def barrier(self):
        sp = self.E["sp"]
        for en in ("pe", "dve", "act", "pool"):
            self._wait("sp", (en, self.E[en]["cnt"]))
        for q, pool in self.dq.items():
            for i, v in enumerate(pool["vals"]):
                self._wait("sp", (("d", q, i), v))
        inst = sp["eng"].nop()
        sp["cnt"] += 1
        inst.then_inc(sp["sem"], 1)
        for en in ("pe", "dve", "act", "pool"):
            self._wait(en, ("sp", sp["cnt"]))
        self.res = {}
        for en, e in self.E.items():
            for o in self.E:
                e["waited"][o] = self.E[o]["cnt"]
            for q, pool in self.dq.items():
                for i, v in enumerate(pool["vals"]):
                    e["waited"][("d", q, i)] = v


class Ctx:
    def __init__(self, nc, T):
        self.nc = nc
        self.T = T
        self.S = Sched(nc)
        self.top = contextlib.ExitStack()
        self.ps = [self.top.enter_context(nc.psum_tensor(f"psb{i}", [128, 512], F32)) for i in range(8)]
        self.psi = 0
        self.ps_lim = 8
        self.es = None
        self.uid = 0

    def psum(self):
        i = self.psi % self.ps_lim
        self.psi = (i + 1) % self.ps_lim
        return self.ps[i], ("ps", i)

    def psum_fixed(self, i):
        return self.ps[i], ("ps", i)

    def begin(self):
        self.es = contextlib.ExitStack()

    def end(self):
        self.S.barrier()
        self.es.close()
        self.es = None

    def sb(self, name, shape, dt):
        self.uid += 1
        return self.es.enter_context(self.nc.sbuf_tensor(f"{name}_{self.uid}", shape, dt))

    def sbtop(self, name, shape, dt):
        return self.top.enter_context(self.nc.sbuf_tensor(name, shape, dt))

    def mm(self, out, lhsT, rhs, start, stop, rd, wr):
        self.S.op("pe", "matmul", out, lhsT=lhsT, rhs=rhs, start=start, stop=stop, rd=rd, wr=wr)

    def tr(self, out, in_, ident, rd, wr):
        self.S.op("pe", "transpose", out, in_, ident, rd=rd, wr=wr)

    def act(self, out, in_, func, rd, wr, **kw):
        self.S.op("act", "activation", out=out, in_=in_, func=func, rd=rd, wr=wr, **kw)

    def v(self, en, meth, rd, wr, *a, **kw):
        self.S.op(en, meth, *a, rd=rd, wr=wr, **kw)

    def dma(self, q, out, in_, rd, wr, **kw):
        self.S.dma(q, out, in_, rd=rd, wr=wr, **kw)


class Rot:
    def __init__(self, cx, name, shape, dt, n):
        self.bufs = [cx.sb(f"{name}{i}", shape, dt) for i in range(n)]
        self.keys = [(name, cx.uid, i) for i in range(n)]
        self.i = 0

    def next(self):
        i = self.i
        self.i = (i + 1) % len(self.bufs)
        return self.bufs[i], self.keys[i]


def phase_A(cx, io):
    nc, T = cx.nc, cx.T
    cx.begin()
    NS = T // 512
    w_sb = cx.sb("w_in", [128, 8, DIN], BF16)
    wsm = cx.sb("wsm", [128, 8, 288], BF16)
    normw = cx.sb("normw", [128, D], F32)
    cw = cx.sb("cw", [128, 4, 12], F32)
    xc = [cx.sb(f"xc{i}", [128, 515], F32) for i in range(12)]
    xtR = Rot(cx, "xt", [128, 4, D], F32, 1)
    rots = make_norm_rots2(cx, 4, 4)
    yR = Rot(cx, "y", [128, 512], F32, 4)
    ysR = Rot(cx, "ys", [128, 512], F32, 5)
    lnR = Rot(cx, "ln", [128, 512], F32, 4)
    sqR = Rot(cx, "sq", [128, 512], BF16, 3)
    oR = Rot(cx, "o", [128, 512], BF16, 4)
    smt = Rot(cx, "smt", [128, 288], F32, 2)
    vpR = Rot(cx, "vpad", [128, 260], BF16, 2)
    for b_ in vpR.bufs:
        cx.v("pool", "memset", [], ["vpinit"], b_[:], 1.0)
    C = dict(cx.consts)
    C["ones_bf"] = cx.sb("ones_bf", [128, 128], BF16)
    cx.v("pool", "memset", [], ["consts"], C["ones_bf"][:], 1.0)

    WB = [0, 512, 2048, 3584, DIN]

    def wg(c0, width=128):
        return sorted({max(i for i in range(len(WB) - 1) if WB[i] <= c) for c in (c0, c0 + width - 1)})

    def wload(gis):
        for gi in gis:
            c0_, c1_ = WB[gi], WB[gi + 1]
            for kc in range(8):
                cx.dma("pool", w_sb[:, kc, c0_:c1_], io["w_in"][kc * 128:(kc + 1) * 128, c0_:c1_], rd=[], wr=[("w", kc, gi)])

    def wload_rest():
        wload(range(1, len(WB) - 1))
        for kc in range(8):
            rows = slice(kc * 128, (kc + 1) * 128)
            for (d0, s0_, n) in ((0, 2048, 8), (8, 2952, 128), (136, 3208, 128), (264, 3336, 24)):
                cx.dma("pool", wsm[:, kc, d0:d0 + n], io["w_in"][rows, s0_:s0_ + n], rd=[], wr=[("wsm", kc)])

    wload([0])
    cx.dma("sp", normw[:], io["norm_mix_pre"].partition_broadcast(128), rd=[], wr=["normw"])
    cwr = cx.sb("cwr", [12, 4, 128], F32)
    idf = cx.sb("idf", [12, 12], F32)
    cx.dma("sp", idf[:], io["c_ident_f"][0:12, 0:12], rd=[], wr=["idf"])
    for k in range(4):
        cx.dma("sp", cwr[:, k, :], io["conv_qkv"][k, :].rearrange("(c p) -> c p", p=128), rd=[], wr=[("cwr", k)])
    for k in range(4):
        ps_, pk_ = cx.psum()
        cx.tr(ps_[:, 0:12], cwr[:, k, :], idf[:, :], rd=[("cwr", k), "idf"], wr=[pk_])
        cx.act(cw[:, k, :], ps_[:, 0:12], AF.Copy, rd=[pk_], wr=["cw"])
    for i in range(12):
        cx.v("pool", "memset", [], [("xc", i)], xc[i][:, 0:3], 0.0)

    fch = []
    for c in range(12):
        fch.append((c * 128, "gdn", c))
    for c in range(4):
        fch.append((2056 + c * 128, "qb", c))
    for i, c0 in enumerate((2568, 2696, 2824, 3080)):
        fch.append((c0, "feat", i))
    for c in range(16):
        fch.append((3360 + c * 128, "gmix", c))
    ST = [dict() for _ in range(NS)]

    def xload_item(s):
        st = ST[s]
        tok = slice(s * 512, (s + 1) * 512)
        st["xb"], st["xk"] = xtR.next()
        cx.dma("sp", st["xb"][:], io["x"][tok, :].rearrange("(a p) d -> p a d", p=128), rd=[], wr=[st["xk"]])
        yield

    def n_item(s):
        st = ST[s]
        yield from norm_T_gen(cx, st["xb"], st["xk"], 4, normw, "normw", rots, st)

    def f_item(s, c0, kind, idx):
        st = ST[s]
        tok = slice(s * 512, (s + 1) * 512)
        yield
        yield
        uTb, uTall = st["uT"], st["uTall"]
        ps, pk = cx.psum()
        for kc in range(8):
            cx.mm(ps[:, :], w_sb[:, kc, c0:c0 + 128], uTb[:, kc, :], kc == 0, kc == 7,
                  rd=[("w", kc, g_) for g_ in wg(c0)] + uTall, wr=[pk])
        if kind == "gdn":
            c = idx
            h = c % 4
            xk_ = ("xc", c)
            cx.act(xc[c][:, 3:515], ps[:, :], AF.Copy, rd=[pk], wr=[xk_])
            yield
            y, yk = yR.next()
            cx.v("dve", "tensor_scalar", [xk_, "cw"], [yk], out=y[:], in0=xc[c][:, 0:512],
                 scalar1=cw[:, 0, c:c + 1], scalar2=None, op0=ALU.mult)
            yield
            cx.v("dve", "scalar_tensor_tensor", [xk_, "cw", yk], [yk], out=y[:], in0=xc[c][:, 1:513],
                 scalar=cw[:, 1, c:c + 1], in1=y[:], op0=ALU.mult, op1=ALU.add)
            yield
            cx.v("dve", "scalar_tensor_tensor", [xk_, "cw", yk], [yk], out=y[:], in0=xc[c][:, 2:514],
                 scalar=cw[:, 2, c:c + 1], in1=y[:], op0=ALU.mult, op1=ALU.add)
            yield
            cx.v("dve", "scalar_tensor_tensor", [xk_, "cw", yk], [yk], out=y[:], in0=xc[c][:, 3:515],
                 scalar=cw[:, 3, c:c + 1], in1=y[:], op0=ALU.mult, op1=ALU.add)
            cx.v("pool", "tensor_copy", [xk_], [xk_], out=xc[c][:, 0:3], in_=xc[c][:, 512:515])
            ys, ysk = ysR.next()
            cx.act(ys[:], y[:], AF.Silu, rd=[yk], wr=[ysk])
            yield
            if c < 8:
                sq, sqk = sqR.next()
                cx.v("pool", "tensor_tensor", [ysk], [sqk], out=sq[:], in0=ys[:], in1=ys[:], op=ALU.mult)
                yield
                ps2, pk2 = cx.psum()
                cx.mm(ps2[:, :], C["ones_bf"][:, :], sq[:], True, True, rd=[sqk, "consts"], wr=[pk2])
                ln, lnk = lnR.next()
                cx.act(ln[:], ps2[:, :], AF.Ln, rd=[pk2, "consts"], wr=[lnk], bias=C["eps_col"][:, 0:1])
                bcol = C["lnq_col"] if c < 4 else C["zero_col"]
                cx.act(ln[:], ln[:], AF.Exp, rd=[lnk, "consts"], wr=[lnk], scale=-0.5, bias=bcol[:, 0:1])
                yield
                o, ok = oR.next()
                cx.v("dve", "tensor_tensor", [ysk, lnk], [ok], out=o[:], in0=ys[:], in1=ln[:], op=ALU.mult)
                dst = io["gq"] if c < 4 else io["gk"]
                cx.dma("sp", dst[h, :, tok], o[:], rd=[ok], wr=[])
            else:
                yv, yvk = sqR.next()
                cx.v("pool", "tensor_copy", [ysk], [yvk], out=yv[:], in_=ys[:])
                yield
                ps2, pk2 = cx.psum()
                ps2b = ps2[:].bitcast(BF16)
                for a in range(4):
                    cx.tr(ps2b[:, a * 128:(a + 1) * 128], yv[:, a * 128:(a + 1) * 128], C["ident_bf"][:],
                          rd=[yvk, "consts"], wr=[pk2])
                vt, vtk = oR.next()
                cx.act(vt[:], ps2b[:, 0:512], AF.Copy, rd=[pk2], wr=[vtk])
                cx.dma("sp", io["gv"][tok, h * 128:(h + 1) * 128].rearrange("(a p) e -> p a e", p=128),
                       vt[:].rearrange("p (a e) -> p a e", a=4), rd=[vtk], wr=[])
        elif kind == "qb":
            o, ok = oR.next()
            cx.act(o[:], ps[:, :], AF.Copy, rd=[pk], wr=[ok], scale=0.125)
            cx.dma("sp", io["QN"][2 * idx, 0:64, tok], o[0:64, :], rd=[ok], wr=[])
            cx.dma("sp", io["QN"][2 * idx + 1, 0:64, tok], o[64:128, :], rd=[ok], wr=[])
        elif kind == "feat":
            o, ok = oR.next()
            cx.v("dve", "tensor_copy", [pk], [ok], out=o[:], in_=ps[:, :])
            cx.dma("sp", io["featT"][idx, :, tok], o[:], rd=[ok], wr=[])
        else:
            o, ok = oR.next()
            cx.act(o[:], ps[:, :], AF.Sigmoid, rd=[pk], wr=[ok])
            cx.dma("sp", io["gmixT"][idx * 128:(idx + 1) * 128, tok], o[:], rd=[ok], wr=[])

    def t_item(s, a):
        st = ST[s]
        yield
        yield
        uTb, uTall = st["uT"], st["uTall"]
        tk = slice(s * 512 + a * 128, s * 512 + (a + 1) * 128)
        ps, pk = cx.psum()
        for kc in range(8):
            cx.mm(ps[:, :], uTb[:, kc, a * 128:(a + 1) * 128], w_sb[:, kc, 1536:2048], kc == 0, kc == 7,
                  rd=[("w", kc, g_) for g_ in wg(1536, 512)] + uTall, wr=[pk])
        o, ok = oR.next()
        cx.act(o[:], ps[:, :], AF.Copy, rd=[pk], wr=[ok])
        cx.dma("sp", io["gz"][tk, :], o[:], rd=[ok], wr=[])
        ps, pk = cx.psum()
        for kc in range(8):
            cx.mm(ps[:, 0:288], uTb[:, kc, a * 128:(a + 1) * 128], wsm[:, kc, :], kc == 0, kc == 7,
                  rd=[("wsm", kc)] + uTall, wr=[pk])
        o2, ok2 = smt.next()
        cx.v("dve", "tensor_copy", [pk], [ok2], out=o2[:], in_=ps[:, 0:288])
        cx.dma("sp", io["gsm"][tk, :], o2[:], rd=[ok2], wr=[])
        o3, ok3 = vpR.next()
        cx.v("pool", "tensor_copy", [ok2, "vpinit"], [ok3], out=o3[:].rearrange("p (b e) -> p b e", b=4)[:, :, 0:64],
             in_=o2[:, 8:264].rearrange("p (b e) -> p b e", b=4))
        cx.dma("sp", io["gvs"][tk, :], o3[:], rd=[ok3], wr=[])

    for g_ in (xload_item(0), n_item(0)):
        for _ in g_:
            pass
    wload_rest()

    def items():
        for s in range(NS):
            for j, (c0, kind, idx) in enumerate(fch):
                yield f_item(s, c0, kind, idx)
                if j == 4 and s + 1 < NS:
                    yield xload_item(s + 1)
                if j == 22 and s + 1 < NS:
                    yield n_item(s + 1)
            for a in range(4):
                yield t_item(s, a)

    pipeline(items())
    cx.end()


def norm_T(cx, xb, xk, nsub, normw, nk, rots, do_norm=True):
    C = cx.consts
    ssr, msr, rsr, ubr, uTr, junk = rots
    uTb, uTk = uTr.next()
    if do_norm:
        ssb, ssk = ssr.next()
        msb, msk = msr.next()
        rsb, rsk = rsr.next()
        for a in range(nsub):
            cx.act(junk[:], xb[:, a, :], AF.Square, rd=[xk], wr=["junk", (ssk, a)], accum_out=ssb[:, a:a + 1])
        cx.v("dve", "tensor_scalar", [(ssk, a) for a in range(nsub)], [msk], out=msb[:, 0:nsub], in0=ssb[:, 0:nsub],
             scalar1=1.0 / D, scalar2=EPS, op0=ALU.mult, op1=ALU.add)
        cx.act(msb[:, 0:nsub], msb[:, 0:nsub], AF.Sqrt, rd=[msk], wr=[msk])
        cx.v("dve", "reciprocal", [msk], [rsk], out=rsb[:, 0:nsub], in_=msb[:, 0:nsub])
    for a in range(nsub):
        u, uk = ubr.next()
        if do_norm:
            cx.v("dve", "scalar_tensor_tensor", [xk, rsk, nk], [uk], out=u[:], in0=xb[:, a, :],
                 scalar=rsb[:, a:a + 1], in1=normw[:], op0=ALU.mult, op1=ALU.mult)
        else:
            cx.v("dve", "tensor_copy", [xk], [uk], out=u[:], in_=xb[:, a, :])
        ps, pk = cx.psum()
        psb = ps[:].bitcast(BF16)
        for kc in range(8):
            cx.tr(psb[:, kc * 128:(kc + 1) * 128], u[:, kc * 128:(kc + 1) * 128], C["ident_bf"][:],
                  rd=[uk, "consts"], wr=[pk])
        cx.act(uTb[:, :, a * 128:(a + 1) * 128], psb.rearrange("p (k t) -> p k t", k=8), AF.Copy,
               rd=[pk], wr=[(uTk, a)])
    return uTb, [(uTk, a) for a in range(nsub)]


def make_norm_rots(cx, nsub):
    return (Rot(cx, "ss", [128, 4], F32, 2), Rot(cx, "ms", [128, 4], F32, 2), Rot(cx, "rstd", [128, 4], F32, 2),
            Rot(cx, "ub", [128, D], BF16, 2), Rot(cx, "uT", [128, 8, nsub * 128], BF16, 2),
            cx.sb("junk", [128, D], BF16))


def epilogue(cx, m, mk, resid, rk, wB, wk, dst, er):
    ssr, junk = er
    ssb, ssk = ssr.next()
    cx.act(junk[:], m[:], AF.Square, rd=[mk], wr=["junk", ssk], accum_out=ssb[:, 0:1])
    cx.v("dve", "tensor_scalar", [ssk], [ssk], out=ssb[:, 1:2], in0=ssb[:, 0:1],
         scalar1=1.0 / D, scalar2=EPS, op0=ALU.mult, op1=ALU.add)
    cx.act(ssb[:, 1:2], ssb[:, 1:2], AF.Sqrt, rd=[ssk], wr=[ssk])
    cx.v("dve", "reciprocal", [ssk], [ssk], out=ssb[:, 2:3], in_=ssb[:, 1:2])
    cx.v("dve", "scalar_tensor_tensor", [mk, ssk, wk], [mk], out=m[:], in0=m[:],
         scalar=ssb[:, 2:3], in1=wB[:], op0=ALU.mult, op1=ALU.mult)
    cx.v("pool", "tensor_tensor", [mk, rk], [mk], out=m[:], in0=m[:], in1=resid, op=ALU.add)
    cx.dma("sp", dst, m[:], rd=[mk], wr=[])


def load_w_bf16(cx, name, dram, K, N, key, gsize=None, order=None):
    kc_n = K // 128
    t = cx.sb(name, [128, kc_n, N], BF16)
    gsize = gsize or N
    ng = (N + gsize - 1) // gsize
    for gi in (order or range(ng)):
        c0, c1 = gi * gsize, min(N, (gi + 1) * gsize)
        for kc in range(kc_n):
            cx.dma("pool", t[:, kc, c0:c1], dram[kc * 128:(kc + 1) * 128, c0:c1], rd=[], wr=[(key, kc, gi)])
    return t


def wkeys(key, kcs, c0, width, gsize):
    gs = sorted({c0 // gsize, (c0 + width - 1) // gsize})
    return [(key, kc, g) for kc in kcs for g in gs]


def phase_M(cx, io):
    nc, T = cx.nc, cx.T
    cx.begin()
    NS = T // 512
    wa = load_w_bf16(cx, "wa", io["w_a2d"], 512, D, "wa", 512)
    wb = load_w_bf16(cx, "wb", io["w_b2d"], 512, D, "wb", 512)
    wo = load_w_bf16(cx, "wo", io["w_o"], D, D, "wo", 512)
    normw = cx.sb("normw", [128, D], F32)
    cx.dma("sp", normw[:], io["norm_mix_post"].partition_broadcast(128), rd=[], wr=["normw"])
    yaR = Rot(cx, "ya", [128, 4, 512], BF16, 2)
    ybR = Rot(cx, "yb", [128, 4, 512], BF16, 2)
    gmR = Rot(cx, "gm", [128, 16, 512], BF16, 2)
    xR = Rot(cx, "xt", [128, 4, D], F32, 3)
    mixR = Rot(cx, "mix", [128, 8, 512], BF16, 2)
    t1R = Rot(cx, "t1", [128, 512], F32, 3)
    t2R = Rot(cx, "t2", [128, 512], F32, 3)
    mR = Rot(cx, "m", [128, D], F32, 6)
    er = (Rot(cx, "ess", [128, 4], F32, 6), cx.sb("junk", [128, D], BF16))
    ST = [dict() for _ in range(NS)]

    def load_item(s):
        st = ST[s]
        tok = slice(s * 512, (s + 1) * 512)
        st["ya"], st["yak"] = yaR.next()
        st["yb"], st["ybk"] = ybR.next()
        st["gm"], st["gmk"] = gmR.next()
        st["x"], st["xk"] = xR.next()
        st["mix"], st["mixk"] = mixR.next()
        cx.dma("sp", st["ya"][:], io["yaT"][:, tok].rearrange("(k p) t -> p k t", p=128), rd=[], wr=[st["yak"]])
        cx.dma("sp", st["yb"][:], io["ybT"][:, tok].rearrange("(k p) t -> p k t", p=128), rd=[], wr=[st["ybk"]])
        cx.dma("sp", st["gm"][:], io["gmixT"][:, tok].rearrange("(k p) t -> p k t", p=128), rd=[], wr=[st["gmk"]])
        cx.dma("sp", st["x"][:], io["x"][tok, :].rearrange("(a p) d -> p a d", p=128), rd=[], wr=[st["xk"]])
        yield

    def oc_item(s, oc):
        st = ST[s]
        yield
        ya, yak, yb, ybk, gm, gmk, mix, mixk = (st[k] for k in ("ya", "yak", "yb", "ybk", "gm", "gmk", "mix", "mixk"))
        psA, pkA = cx.psum()
        for kc in range(4):
            cx.mm(psA[:, :], wa[:, kc, oc * 128:(oc + 1) * 128], ya[:, kc, :], kc == 0, kc == 3, rd=[("wa", kc, oc // 4), yak], wr=[pkA])
        psB, pkB = cx.psum()
        for kc in range(4):
            cx.mm(psB[:, :], wb[:, kc, oc * 128:(oc + 1) * 128], yb[:, kc, :], kc == 0, kc == 3, rd=[("wb", kc, oc // 4), ybk], wr=[pkB])
        t1, t1k = t1R.next()
        t2, t2k = t2R.next()
        cx.v("dve", "tensor_tensor", [pkA, gmk], [t1k], out=t1[:], in0=psA[:, :], in1=gm[:, oc, :], op=ALU.mult)
        cx.v("dve", "tensor_tensor", [pkB, gmk], [t2k], out=t2[:], in0=psB[:, :], in1=gm[:, 8 + oc, :], op=ALU.mult)
        yield
        cx.v("pool", "tensor_tensor", [t1k, t2k], [(mixk, oc)], out=mix[:, oc, :], in0=t1[:], in1=t2[:], op=ALU.add)

    def a_item(s, a):
        st = ST[s]
        for _ in range(4):
            yield
        mix, mixk = st["mix"], st["mixk"]
        mixall = [(mixk, oc) for oc in range(8)]
        m, mk = mR.next()
        for half in range(2):
            ps, pk = cx.psum()
            for kc in range(8):
                cx.mm(ps[:, :], mix[:, kc, a * 128:(a + 1) * 128], wo[:, kc, half * 512:(half + 1) * 512],
                      kc == 0, kc == 7, rd=[("wo", kc, half)] + mixall, wr=[pk])
            cx.act(m[:, half * 512:(half + 1) * 512], ps[:, :], AF.Copy, rd=[pk], wr=[mk])
        tk = slice(s * 512 + a * 128, s * 512 + (a + 1) * 128)
        yield from epilogue_gen(cx, m, mk, st["x"][:, a, :], st["xk"], normw, "normw", io["out"][tk, :], er)

    def items():
        yield load_item(0)
        for s in range(NS):
            for oc in range(8):
                yield oc_item(s, oc)
                if oc == 2 and s + 1 < NS:
                    yield load_item(s + 1)
            for a in range(4):
                yield a_item(s, a)

    pipeline(items())
    cx.end()


def norm_T_gen(cx, xb, xk, nsub, normw, nk, rots, st, do_norm=True):
    C = cx.consts
    ssr, msr, rsr, ubr, uTr, junk = rots
    uTb, uTk = uTr.next()
    st["uT"], st["uTall"] = uTb, [(uTk, a) for a in range(nsub)]
    if do_norm:
        ssb, ssk = ssr.next()
        msb, msk = msr.next()
        rsb, rsk = rsr.next()
        for a in range(nsub):
            cx.act(junk[:], xb[:, a, :], AF.Square, rd=[xk], wr=["junk", (ssk, a)], accum_out=ssb[:, a:a + 1])
        cx.v("dve", "tensor_scalar", [(ssk, a) for a in range(nsub)], [msk], out=msb[:, 0:nsub], in0=ssb[:, 0:nsub],
             scalar1=1.0 / D, scalar2=EPS, op0=ALU.mult, op1=ALU.add)
        cx.v("pool", "tensor_tensor", [msk, "consts"], [rsk], out=rsb[:, 0:nsub], in0=msb[:, 0:nsub],
             in1=C["negh_col"][:, 0:1].to_broadcast([128, nsub]), op=ALU.pow)
        yield
    us = []
    for a in range(nsub):
        u, uk = ubr.next()
        us.append((u, uk))
        if do_norm:
            cx.v("dve", "scalar_tensor_tensor", [xk, rsk, nk], [uk], out=u[:], in0=xb[:, a, :],
                 scalar=rsb[:, a:a + 1], in1=normw[:], op0=ALU.mult, op1=ALU.mult)
        else:
            cx.v("pool", "tensor_copy", [xk], [uk], out=u[:], in_=xb[:, a, :])
    yield
    for a in range(nsub):
        u, uk = us[a]
        ps, pk = cx.psum()
        psb = ps[:].bitcast(BF16)
        for kc in range(8):
            cx.tr(psb[:, kc * 128:(kc + 1) * 128], u[:, kc * 128:(kc + 1) * 128], C["ident_bf"][:],
                  rd=[uk, "consts"], wr=[pk])
        cx.act(uTb[:, :, a * 128:(a + 1) * 128], psb.rearrange("p (k t) -> p k t", k=8), AF.Copy,
               rd=[pk], wr=[(uTk, a)])


def make_norm_rots2(cx, nsub, nu):
    return (Rot(cx, "ss", [128, 4], F32, 2), Rot(cx, "ms", [128, 4], F32, 2), Rot(cx, "rstd", [128, 4], F32, 2),
            Rot(cx, "ub", [128, D], BF16, nu), Rot(cx, "uT", [128, 8, nsub * 128], BF16, 2),
            cx.sb("junk", [128, D], BF16))


def epilogue_gen(cx, m, mk, resid, rk, wB, wk, dst, er):
    ssr, junk = er
    ssb, ssk = ssr.next()
    cx.act(junk[:], m[:], AF.Square, rd=[mk], wr=["junk", ssk], accum_out=ssb[:, 0:1])
    yield
    cx.v("dve", "tensor_scalar", [ssk], [ssk], out=ssb[:, 1:2], in0=ssb[:, 0:1],
         scalar1=1.0 / D, scalar2=EPS, op0=ALU.mult, op1=ALU.add)
    yield
    cx.v("pool", "tensor_tensor", [ssk, "consts"], [ssk], out=ssb[:, 2:3], in0=ssb[:, 1:2],
         in1=cx.consts["negh_col"][:, 0:1], op=ALU.pow)
    yield
    cx.v("dve", "scalar_tensor_tensor", [mk, ssk, wk], [mk], out=m[:], in0=m[:],
         scalar=ssb[:, 2:3], in1=wB[:], op0=ALU.mult, op1=ALU.mult)
    yield
    if resid is None:
        cx.dma("pool", dst, m[:], rd=[mk], wr=[], accum_op=ALU.add)
    else:
        cx.v("pool", "tensor_tensor", [mk, rk], [mk], out=m[:], in0=m[:], in1=resid, op=ALU.add)
        cx.dma("sp", dst, m[:], rd=[mk], wr=[])


def phase_F(cx, io):
    nc, T = cx.nc, cx.T
    cx.begin()
    W = 256
    NSUB = W // 128
    NS = T // W
    wu = cx.sb("wu", [128, 8, 2 * DFF], BF16)
    wd = cx.sb("wd", [128, 22, D], BF16)

    def wu_load(gis):
        for gi in gis:
            c0_, c1_ = gi * 1408, (gi + 1) * 1408
            for kc in range(8):
                cx.dma("pool", wu[:, kc, c0_:c1_], io["w_up"][kc * 128:(kc + 1) * 128, c0_:c1_], rd=[], wr=[("wu", kc, gi)])

    def wd_load():
        for gi in range(2):
            for kc in range(22):
                cx.dma("pool", wd[:, kc, gi * 512:(gi + 1) * 512], io["w_down"][kc * 128:(kc + 1) * 128, gi * 512:(gi + 1) * 512],
                       rd=[], wr=[("wd", kc, gi)])

    wu_load([0, 2])
    npre = cx.sb("npre", [128, D], F32)
    npost = cx.sb("npost", [128, D], F32)
    cx.dma("sp", npre[:], io["norm_ffn_pre"].partition_broadcast(128), rd=[], wr=["npre"])
    cx.dma("sp", npost[:], io["norm_ffn_post"].partition_broadcast(128), rd=[], wr=["npost"])
    cf = cx.sb("cf", [128, 3, 44], F32)
    cb = cx.sb("cb", [128, 44], F32)
    carry = cx.sb("carry", [128, 44, 2], F32)
    cx.v("pool", "memset", [], ["carry"], carry[:], 0.0)
    hR = Rot(cx, "h", [128, NSUB, D], F32, 1)
    rots = make_norm_rots2(cx, NSUB, 2)
    actR = Rot(cx, "actT", [128, 22, W], BF16, 2)
    fbR = Rot(cx, "fb", [128, 2, W + 2], F32, 3)
    gR = Rot(cx, "g", [128, W], F32, 8)
    mR = Rot(cx, "m", [128, D], F32, 2)
    er = (Rot(cx, "ess", [128, 4], F32, 4), rots[5])
    ST = [dict() for _ in range(NS)]
    junkF = rots[5][:].bitcast(F32)
    cfr = junkF[0:44, :].rearrange("p (k e) -> p k e", k=4)
    idf = mR.bufs[0][0:44, 0:44]
    idk = mR.keys[0]
    cx.dma("sp", idf, io["c_ident_f"][0:44, 0:44], rd=[], wr=[idk])
    for k in range(3):
        cx.dma("sp", cfr[:, k, :], io["conv_ffn"][k, :].rearrange("(c p) -> c p", p=128), rd=[], wr=["junk"])
    cx.dma("sp", cfr[:, 3, :], io["conv_ffn_b"].rearrange("(c p) -> c p", p=128), rd=[], wr=["junk"])
    for k in range(4):
        ps_, pk_ = cx.psum()
        cx.tr(ps_[:, 0:44], cfr[:, k, :], idf, rd=["junk", idk], wr=[pk_])
        dst_ = cf[:, k, :] if k < 3 else cb[:]
        cx.act(dst_, ps_[:, 0:44], AF.Copy, rd=[pk_], wr=["cf"])

    def hload_item(s):
        st = ST[s]
        tok = slice(s * W, (s + 1) * W)
        st["h"], st["hk"] = hR.next()
        cx.dma("sp", st["h"][:], io["out"][tok, :].rearrange("(a p) d -> p a d", p=128), rd=[], wr=[st["hk"]])
        yield

    def n_item(s):
        st = ST[s]
        st["act"], st["actk"] = actR.next()
        yield from norm_T_gen(cx, st["h"], st["hk"], NSUB, npre, "npre", rots, st)

    def pair_item(s, i):
        st = ST[s]
        yield
        yield
        uTb, uTall = st["uT"], st["uTall"]
        actT, actk = st["act"], st["actk"]
        fb, fk = fbR.next()
        gs_ = []
        for j, c in enumerate((i, 22 + i)):
            ps, pk = cx.psum()
            for kc in range(8):
                cx.mm(ps[:, 0:W], wu[:, kc, c * 128:(c + 1) * 128], uTb[:, kc, :], kc == 0, kc == 7,
                      rd=[("wu", kc, c // 11)] + uTall, wr=[pk])
            g, gk = gR.next()
            cx.act(fb[:, j, 2:W + 2], ps[:, 0:W], AF.Copy, rd=[pk], wr=[(fk, j)])
            cx.act(g[:], ps[:, 0:W], AF.Identity, rd=[pk, "cf"], wr=[gk], scale=cf[:, 2, c:c + 1], bias=cb[:, c:c + 1])
            gs_.append((c, g, gk))
        cv = carry[:, :, :].rearrange("p (j c) k -> p j c k", j=2)[:, :, i, :]
        cx.v("pool", "tensor_copy", ["carry", ("carry", i)], [(fk, 2)], out=fb[:, :, 0:2], in_=cv)
        yield
        for k in (0, 1):
            for j, (c, g, gk) in enumerate(gs_):
                cx.v("dve", "scalar_tensor_tensor", [(fk, j), (fk, 2), "cf", gk], [gk], out=g[:], in0=fb[:, j, k:k + W],
                     scalar=cf[:, k, c:c + 1], in1=g[:], op0=ALU.mult, op1=ALU.add)
        cx.v("pool", "tensor_copy", [(fk, 0), (fk, 1)], [("carry", i)], out=cv, in_=fb[:, :, W:W + 2])
        yield
        (_, g, gk), (_, vv, vk) = gs_
        cx.act(g[:], g[:], AF.Gelu_apprx_tanh, rd=[gk], wr=[gk])
        yield
        cx.v("pool", "tensor_tensor", [gk, vk], [(actk, i)], out=actT[:, i, :], in0=g[:], in1=vv[:], op=ALU.mult)

    def down_item(s, a):
        st = ST[s]
        for _ in range(6):
            yield
        actT, actk = st["act"], st["actk"]
        actall = [(actk, i) for i in range(22)]
        m, mk = mR.next()
        for half in range(2):
            ps, pk = cx.psum()
            for i in range(22):
                cx.mm(ps[:, :], actT[:, i, a * 128:(a + 1) * 128], wd[:, i, half * 512:(half + 1) * 512],
                      i == 0, i == 21, rd=[("wd", i, half)] + actall, wr=[pk])
            cx.act(m[:, half * 512:(half + 1) * 512], ps[:, :], AF.Copy, rd=[pk], wr=[mk])
        tk = slice(s * W + a * 128, s * W + (a + 1) * 128)
        yield from epilogue_gen(cx, m, mk, None, None, npost, "npost", io["out"][tk, :], er)

    for g_ in (hload_item(0), n_item(0)):
        for _ in g_:
            pass
    wu_load([1, 3])
    wd_load()

    def items():
        for s in range(NS):
            for i in range(22):
                yield pair_item(s, i)
                if i == 2 and s + 1 < NS:
                    yield hload_item(s + 1)
                if i == 12 and s + 1 < NS:
                    yield n_item(s + 1)
                if i == 5 and s > 0:
                    for a in range(NSUB):
                        yield down_item(s - 1, a)
        for _ in range(8):
            yield iter(())
        for a in range(NSUB):
            yield down_item(NS - 1, a)

    pipeline(items())
    cx.end()


def phase_P(cx, io):
    nc, T = cx.nc, cx.T
    cx.begin()
    C = cx.consts
    NS = T // 512
    wg = load_w_bf16(cx, "wg", io["w_ple_gate"], D, D, "wg", 512)
    wp = load_w_bf16(cx, "wp", io["w_ple"], 256, D, "wp", 512)
    npost = cx.sb("npost", [128, D], F32)
    cx.dma("sp", npost[:], io["norm_ple_post"].partition_broadcast(128), rd=[], wr=["npost"])
    hR = Rot(cx, "h", [128, 4, D], F32, 1)
    pR = Rot(cx, "p", [128, 4, 256], F32, 1)
    pbR = Rot(cx, "pb", [128, 256], BF16, 4)
    pTR = Rot(cx, "pT", [128, 2, 512], BF16, 2)
    rots = make_norm_rots2(cx, 4, 4)
    sgR = Rot(cx, "sg", [128, 512], F32, 3)
    mR = Rot(cx, "m", [128, D], F32, 6)
    er = (Rot(cx, "ess", [128, 4], F32, 6), rots[5])
    ST = [dict() for _ in range(NS)]

    def hload_item(s):
        st = ST[s]
        tok = slice(s * 512, (s + 1) * 512)
        st["hb"], st["hk"] = hR.next()
        st["pb"], st["pk_"] = pR.next()
        cx.dma("sp", st["hb"][:], io["out"][tok, :].rearrange("(a p) d -> p a d", p=128), rd=[], wr=[st["hk"]])
        cx.dma("sp", st["pb"][:], io["p"][tok, :].rearrange("(a p) d -> p a d", p=128), rd=[], wr=[st["pk_"]])
        yield

    def n_item(s):
        st = ST[s]
        hb, hk, pb, pk_ = st["hb"], st["hk"], st["pb"], st["pk_"]
        pT, pTk = pTR.next()
        st["pT"], st["pTall"] = pT, [(pTk, a) for a in range(4)]
        pbs = []
        for a in range(4):
            pbb, pbk = pbR.next()
            pbs.append((pbb, pbk))
            cx.v("pool", "tensor_copy", [pk_], [pbk], out=pbb[:], in_=pb[:, a, :])
        gen = norm_T_gen(cx, hb, hk, 4, None, None, rots, st, do_norm=False)
        next(gen)
        yield
        for a in range(4):
            pbb, pbk = pbs[a]
            ps, pk = cx.psum()
            psb = ps[:].bitcast(BF16)
            for kc in range(2):
                cx.tr(psb[:, kc * 128:(kc + 1) * 128], pbb[:, kc * 128:(kc + 1) * 128], C["ident_bf"][:],
                      rd=[pbk, "consts"], wr=[pk])
            cx.act(pT[:, :, a * 128:(a + 1) * 128], psb[:, 0:256].rearrange("p (k t) -> p k t", k=2), AF.Copy,
                   rd=[pk], wr=[(pTk, a)])
        for _ in gen:
            yield

    def a_item(s, a, half):
        st = ST[s]
        for _ in range(3):
            yield
        hT, hTall, pT, pTall = st["uT"], st["uTall"], st["pT"], st["pTall"]
        if half == 0:
            st[("m", a)] = mR.next()
        m, mk = st[("m", a)]
        hs = slice(half * 512, (half + 1) * 512)
        ps, pk = cx.psum()
        for kc in range(8):
            cx.mm(ps[:, :], hT[:, kc, a * 128:(a + 1) * 128], wg[:, kc, hs], kc == 0, kc == 7,
                  rd=[("wg", kc, half)] + hTall, wr=[pk])
        sg, sgk = sgR.next()
        cx.act(sg[:], ps[:, :], AF.Sigmoid, rd=[pk], wr=[sgk])
        ps2, pk2 = cx.psum()
        for kc in range(2):
            cx.mm(ps2[:, :], pT[:, kc, a * 128:(a + 1) * 128], wp[:, kc, hs], kc == 0, kc == 1,
                  rd=[("wp", kc, half)] + pTall, wr=[pk2])
        cx.v("dve", "tensor_tensor", [pk2, sgk], [mk], out=m[:, hs], in0=ps2[:, :], in1=sg[:], op=ALU.mult)
        if half == 1:
            tk = slice(s * 512 + a * 128, s * 512 + (a + 1) * 128)
            yield
            yield from epilogue_gen(cx, m, mk, None, None, npost, "npost", io["out"][tk, :], er)

    def items():
        yield hload_item(0)
        yield n_item(0)
        if NS > 1:
            yield hload_item(1)
        for s in range(NS):
            for a in range(4):
                for half in range(2):
                    yield a_item(s, a, half)
                if a == 0 and s + 1 < NS:
                    yield n_item(s + 1)
                if a == 1 and s + 2 < NS:
                    yield hload_item(s + 2)

    pipeline(items())
    cx.end()


def phase_G(cx, io):
    nc, T = cx.nc, cx.T
    cx.begin()
    C = cx.consts
    NB = 2
    NBLK = T // 512
    id64 = C["ident_bf"][0:64, 0:64]
    tri = cx.sb("tri", [64, 64], F32)
    stri = cx.sb("stri", [64, 64], F32)
    msk = cx.sb("msk", [64, 2, 64], F32)
    onesf = cx.sb("onesf", [64, 128], F32)
    identB = cx.sb("identB", [64, 64], F32)
    ab = cx.sb("ab", [64, 8], F32)
    nexpA = cx.sb("nexpA", [64, 4], F32)
    gnw = cx.sb("gnw", [64, 128], F32)
    one_col = cx.sb("one_col", [128, 1], F32)
    cx.dma("sp", tri[:], io["c_tri"][:, :], rd=[], wr=["gc"])
    cx.dma("sp", stri[:], io["c_stri"][:, :], rd=[], wr=["gc"])
    cx.dma("sp", msk[:], io["c_msk"][:, :, :], rd=[], wr=["gc"])
    cx.dma("sp", identB[:], io["c_ident_f"][0:64, 0:64], rd=[], wr=["gc"])
    cx.dma("sp", ab[:, 0:4], io["a_log"].partition_broadcast(64), rd=[], wr=["gc"])
    cx.dma("sp", ab[:, 4:8], io["dt_bias"].partition_broadcast(64), rd=[], wr=["gc"])
    cx.dma("sp", gnw[:], io["gdn_norm"].partition_broadcast(64), rd=[], wr=["gc"])
    cx.v("pool", "memset", [], ["gc"], onesf[:], 1.0)
    cx.v("pool", "memset", [], ["gc"], one_col[:], 1.0)
    cx.act(nexpA[:], ab[:, 0:4], AF.Exp, rd=["gc"], wr=["gc2"])
    cx.v("dve", "tensor_scalar", ["gc2"], ["gc2"], out=nexpA[:], in0=nexpA[:], scalar1=-1.0, scalar2=None, op0=ALU.mult)
    S32 = cx.sb("S32", [128, 4, 128], F32)
    Sb = cx.sb("Sb", [128, 4, 128], BF16)
    cx.v("pool", "memset", [], ["S32"], S32[:], 0.0)
    cx.v("pool", "memset", [], ["Sb"], Sb[:], 0.0)

    qR = Rot(cx, "gq", [128, 4, 512], BF16, 2)
    kR = Rot(cx, "gk", [128, 4, 512], BF16, 2)
    vR = Rot(cx, "gv", [64, 8, 512], BF16, 2)
    zR = Rot(cx, "gz", [64, 8, 512], BF16, 2)
    smR = Rot(cx, "gsm", [64, 8, 8], F32, 2)
    betaR = Rot(cx, "beta", [64, 8, 4], F32, 2)
    g8R = Rot(cx, "g8", [64, 8, 4], F32, 2)
    yaTR = Rot(cx, "yaT", [128, 4, 512], BF16, 2)
    def R2(name, shape, dt, n=2):
        return Rot(cx, name, shape, dt, n)
    ktokR = R2("ktok", [64, NB, 4, 128], BF16)
    ggR = R2("gg", [64, 16], F32)
    gamR = R2("gam", [64, NB, 4], F32, 3)
    tailR = R2("tail", [64, NB, 4], F32)
    glR = R2("gl", [128, NB, 4], F32, 3)
    bgR = R2("bg", [64, NB, 4], F32)
    gTriR = R2("gTri", [64, NB * 4, 64], F32)
    decR = R2("dec", [64, NB * 4, 64], F32)
    decmR = R2("decm", [64, 2, NB * 4, 64], F32)
    tmpR = R2("ltmp", [64, NB, 4, 64], F32, 6)
    pqR = [R2(f"pq{i}", [64, NB, 4, 64], BF16, 8) for i in range(2)]
    qkR = R2("qk", [64, NB, 4, 64], BF16)
    qkTR = R2("qkT", [64, NB, 4, 64], BF16, 3)
    T32R = R2("T32", [64, NB, 4, 64], F32)
    TbR = R2("Tb", [64, NB, 4, 64], BF16, 6)
    TfR = R2("Tf", [64, NB, 4, 64], BF16, 3)
    RkR = R2("Rk", [64, NB, 4, 128], BF16)
    RvR = R2("Rv", [64, NB, 4, 128], BF16, 3)
    ktR = R2("ktail", [64, NB, 4, 128], BF16, 3)
    wTR = R2("wT", [128, NB, 4, 64], BF16, 3)
    uR = R2("u", [64, 4, 128], BF16)
    o1R = R2("o1", [64, 4, 128], F32, 6)
    sqR = R2("osq", [64, 4, 128], F32, 4)
    ossR = R2("oss", [64, 8], F32)
    yaR = R2("ya", [64, 4, 128], BF16)

    def bc(ap, shape):
        return ap.to_broadcast(shape)

    BS = [dict() for _ in range(NBLK)]

    def load_block(blk):
        B = BS[blk]
        tok = slice(blk * 512, (blk + 1) * 512)
        q4, qk_ = qR.next()
        k4, kk_ = kR.next()
        v8, vk_ = vR.next()
        z8, zk_ = zR.next()
        sm, smk = smR.next()
        cx.dma("sp", q4[:], io["gq"][:, :, tok].rearrange("h d t -> d h t"), rd=[], wr=[qk_])
        cx.dma("sp", k4[:], io["gk"][:, :, tok].rearrange("h d t -> d h t"), rd=[], wr=[kk_])
        cx.dma("sp", v8[:], io["gv"][tok, :].rearrange("(c p) e -> p c e", p=64), rd=[], wr=[vk_])
        cx.dma("sp", z8[:], io["gz"][tok, :].rearrange("(c p) e -> p c e", p=64), rd=[], wr=[zk_])
        cx.dma("sp", sm[:], io["gsm"][tok, 0:8].rearrange("(c p) e -> p c e", p=64), rd=[], wr=[smk])
        cx.act(z8[:], z8[:], AF.Silu, rd=[zk_], wr=[zk_])
        beta8, bk = betaR.next()
        g8, g8k = g8R.next()
        cx.act(beta8[:], sm[:, :, 0:4], AF.Sigmoid, rd=[smk], wr=[bk])
        cx.v("dve", "tensor_tensor", [smk, "gc"], [g8k], out=g8[:], in0=sm[:, :, 4:8],
             in1=bc(ab[:, 4:8].unsqueeze(1), [64, 8, 4]), op=ALU.add)
        cx.act(g8[:], g8[:], AF.Exp, rd=[g8k], wr=[g8k])
        cx.act(g8[:], g8[:], AF.Ln, rd=[g8k, "gc"], wr=[g8k], bias=one_col[0:64, 0:1])
        cx.v("dve", "tensor_tensor", [g8k, "gc2"], [g8k], out=g8[:], in0=g8[:],
             in1=bc(nexpA[:].unsqueeze(1), [64, 8, 4]), op=ALU.mult)
        yaT, yaTk = yaTR.next()

        B.update(q4=q4, qk_=qk_, k4=k4, kk_=kk_, v8=v8, vk_=vk_, z8=z8, zk_=zk_, beta8=beta8, bk=bk, g8=g8, g8k=g8k, yaT=yaT, yaTk=yaTk)

    def prep(blk, pc, st):
        if pc == 0:
            load_block(blk)
        B = BS[blk]
        q4, qk_, k4, kk_, v8, vk_, z8, zk_, beta8, bk, g8, g8k = (B[k] for k in ('q4', 'qk_', 'k4', 'kk_', 'v8', 'vk_', 'z8', 'zk_', 'beta8', 'bk', 'g8', 'g8k'))
        yield
        cs = slice(pc * NB, (pc + 1) * NB)
        ps, pk = cx.psum()
        psb = ps[0:64, :].bitcast(BF16)
        for cb in range(NB):
            for h in range(4):
                cc = (pc * NB + cb) * 64
                i0 = (cb * 4 + h) * 128
                cx.tr(psb[:, i0:i0 + 128], k4[:, h, cc:cc + 64], C["ident_bf"][:], rd=[kk_, "consts"], wr=[pk])
        ktok, ktk = ktokR.next()
        cx.act(ktok[:].rearrange("p a h e -> p (a h e)"), psb[:, :], AF.Copy, rd=[pk], wr=[ktk])
        yield
        gsl = g8[:, cs, :].rearrange("p a h -> p (a h)")
        ps, pk = cx.psum()
        cx.mm(ps[0:64, 0:8], tri[:, :], gsl, True, True, rd=["gc", g8k], wr=[pk])
        cx.mm(ps[0:64, 8:16], onesf[:, 0:64], gsl, True, True, rd=["gc", g8k], wr=[pk])
        cx.mm(ps[:, 16:24], onesf[:, :], gsl, True, True, rd=["gc", g8k], wr=[pk])
        gg, ggk = ggR.next()
        gam, gamk = gamR.next()
        tail, tailk = tailR.next()
        gl, glk = glR.next()
        cx.act(gg[:], ps[0:64, 0:16], AF.Copy, rd=[pk], wr=[ggk])
        cx.act(gl[:].rearrange("p a h -> p (a h)"), ps[:, 16:24], AF.Exp, rd=[pk], wr=[glk])
        yield
        cx.act(gam[:].rearrange("p a h -> p (a h)"), gg[:, 0:8], AF.Exp, rd=[ggk], wr=[gamk])
        cx.v("dve", "tensor_tensor", [ggk], [tailk], out=tail[:].rearrange("p a h -> p (a h)"), in0=gg[:, 8:16],
             in1=gg[:, 0:8], op=ALU.subtract)
        cx.act(tail[:], tail[:], AF.Exp, rd=[tailk], wr=[tailk])
        bg, bgk = bgR.next()
        cx.v("dve", "tensor_tensor", [bk, gamk], [bgk], out=bg[:], in0=beta8[:, cs, :], in1=gam[:], op=ALU.mult)
        gTri, gTk = gTriR.next()
        cx.v("dve", "tensor_tensor", ["gc", g8k], [gTk], out=gTri[:], in0=bc(stri[:].unsqueeze(1), [64, NB * 4, 64]),
             in1=bc(gsl.unsqueeze(2), [64, NB * 4, 64]), op=ALU.mult)
        yield
        ps, pk = cx.psum()
        cx.mm(ps[0:64, 0:NB * 4 * 64], tri[:, :], gTri[:].rearrange("p a j -> p (a j)"), True, True, rd=[gTk, "gc"], wr=[pk])
        dec, deck = decR.next()
        cx.act(dec[:].rearrange("p a j -> p (a j)"), ps[0:64, :], AF.Exp, rd=[pk], wr=[deck])
        decm, dmk = decmR.next()
        for s_ in range(2):
            cx.v("pool", "tensor_tensor", [deck, "gc"], [(dmk, s_)], out=decm[:, s_], in0=dec[:],
                 in1=bc(msk[:, s_:s_ + 1, :], [64, NB * 4, 64]), op=ALU.mult)
        yield
        psk, pkk = cx.psum()
        psq, pkq = cx.psum()
        for cb in range(NB):
            for h in range(4):
                cc = (pc * NB + cb) * 64
                i0 = (cb * 4 + h) * 64
                cx.mm(psk[0:64, i0:i0 + 64], k4[:, h, cc:cc + 64], k4[:, h, cc:cc + 64], True, True, rd=[kk_], wr=[pkk])
                cx.mm(psq[0:64, i0:i0 + 64], q4[:, h, cc:cc + 64], k4[:, h, cc:cc + 64], True, True, rd=[kk_, qk_], wr=[pkq])
        ltmp, ltk = tmpR.next()
        cx.v("dve", "tensor_tensor", [pkk, (dmk, 0)], [ltk], out=ltmp[:].rearrange("p a h j -> p (a h j)"),
             in0=psk[0:64, :], in1=decm[:, 0].rearrange("p a j -> p (a j)"), op=ALU.mult)
        Q1, Q1k = pqR[1].next()
        cx.v("dve", "tensor_tensor", [ltk, bk], [Q1k], out=Q1[:], in0=ltmp[:],
             in1=bc(beta8[:, cs, :].unsqueeze(3), [64, NB, 4, 64]), op=ALU.mult)
        qkm, qkmk = qkR.next()
        ltmp2, ltk2 = tmpR.next()
        cx.v("dve", "tensor_tensor", [pkq, (dmk, 1)], [ltk2], out=ltmp2[:].rearrange("p a h j -> p (a h j)"),
             in0=psq[0:64, :], in1=decm[:, 1].rearrange("p a j -> p (a j)"), op=ALU.mult)
        cx.act(qkm[:], ltmp2[:], AF.Copy, rd=[ltk2], wr=[qkmk])
        yield
        ps, pk = cx.psum()
        psb = ps[0:64, :].bitcast(BF16)
        for cb in range(NB):
            for h in range(4):
                i0 = (cb * 4 + h) * 64
                cx.tr(psb[:, i0:i0 + 64], Q1[:, cb, h, :], id64, rd=[Q1k, "consts"], wr=[pk])
                cx.tr(psb[:, 512 + i0:512 + i0 + 64], qkm[:, cb, h, :], id64, rd=[qkmk, "consts"], wr=[pk])
        P1, P1k = pqR[0].next()
        qkT, qkTk = qkTR.next()
        cx.act(P1[:].rearrange("p a h j -> p (a h j)"), psb[:, 0:512], AF.Copy, rd=[pk], wr=[P1k])
        cx.act(qkT[:].rearrange("p a h j -> p (a h j)"), psb[:, 512:1024], AF.Copy, rd=[pk], wr=[qkTk])
        T32, T32k = T32R.next()
        Tb, Tbk = TbR.next()
        cx.v("dve", "tensor_tensor", ["gc", P1k], [T32k], out=T32[:].rearrange("p a h j -> p (a h) j"),
             in0=bc(identB[:].unsqueeze(1), [64, NB * 4, 64]), in1=P1[:].rearrange("p a h j -> p (a h) j"), op=ALU.subtract)
        cx.act(Tb[:], T32[:], AF.Copy, rd=[T32k], wr=[Tbk])
        Pm, Pmk, Qm, Qmk = P1, P1k, Q1, Q1k
        yield
        for step in range(5):
            last = step == 4
            psQ, pkQ = cx.psum()
            for cb in range(NB):
                for h in range(4):
                    i0 = (cb * 4 + h) * 64
                    cx.mm(psQ[0:64, i0:i0 + 64], Pm[:, cb, h, :], Qm[:, cb, h, :], True, True, rd=[Pmk, Qmk], wr=[pkQ])
            Qn, Qnk = pqR[1].next()
            cx.act(Qn[:].rearrange("p a h j -> p (a h j)"), psQ[0:64, :], AF.Copy, rd=[pkQ], wr=[Qnk])
            if not last:
                psP, pkP = cx.psum()
                for cb in range(NB):
                    for h in range(4):
                        i0 = (cb * 4 + h) * 64
                        cx.mm(psP[0:64, i0:i0 + 64], Qm[:, cb, h, :], Pm[:, cb, h, :], True, True, rd=[Pmk, Qmk], wr=[pkP])
                Pn, Pnk = pqR[0].next()
                cx.act(Pn[:].rearrange("p a h j -> p (a h j)"), psP[0:64, :], AF.Copy, rd=[pkP], wr=[Pnk])
            yield
            psT, pkT = cx.psum()
            for cb in range(NB):
                for h in range(4):
                    i0 = (cb * 4 + h) * 64
                    cx.mm(psT[0:64, i0:i0 + 64], Qn[:, cb, h, :], Tb[:, cb, h, :], True, True, rd=[Qnk, Tbk], wr=[pkT])
            cx.v("dve", "tensor_tensor", [T32k, pkT], [T32k], out=T32[:].rearrange("p a h j -> p (a h j)"),
                 in0=T32[:].rearrange("p a h j -> p (a h j)"), in1=psT[0:64, :], op=ALU.add)
            Tb, Tbk = (TfR if last else TbR).next()
            cx.act(Tb[:], T32[:], AF.Copy, rd=[T32k], wr=[Tbk])
            if not last:
                Pm, Pmk = Pn, Pnk
            Qm, Qmk = Qn, Qnk
            yield
        Rk, Rkk = RkR.next()
        Rv, Rvk = RvR.next()
        kt, ktlk = ktR.next()
        cx.v("dve", "tensor_tensor", [ktk, bgk], [Rkk], out=Rk[:], in0=ktok[:],
             in1=bc(bg[:].unsqueeze(3), [64, NB, 4, 128]), op=ALU.mult)
        cx.v("pool", "tensor_tensor", [vk_, bk], [Rvk], out=Rv[:],
             in0=v8[:, cs, :].rearrange("p a (h e) -> p a h e", h=4),
             in1=bc(beta8[:, cs, :].unsqueeze(3), [64, NB, 4, 128]), op=ALU.mult)
        cx.v("pool", "tensor_tensor", [ktk, tailk], [ktlk], out=kt[:], in0=ktok[:],
             in1=bc(tail[:].unsqueeze(3), [64, NB, 4, 128]), op=ALU.mult)
        psw, pkw = cx.psum()
        for cb in range(NB):
            for h in range(4):
                i0 = (cb * 4 + h) * 64
                cx.mm(psw[:, i0:i0 + 64], Rk[:, cb, h, :], Tb[:, cb, h, :], True, True, rd=[Rkk, Tbk], wr=[pkw])
        wT, wTk = wTR.next()
        cx.act(wT[:].rearrange("p a h j -> p (a h j)"), psw[:, :], AF.Copy, rd=[pkw], wr=[wTk], scale=-1.0)
        st.update(wT=wT, wTk=wTk, u0=(Tb, Rv), u0k=(Tbk, Rvk), kt=kt, ktlk=ktlk, qkT=qkT, qkTk=qkTk, gam=gam, gamk=gamk, gl=gl, glk=glk)

    outq = []

    def outgen(blk, pc, cb, o1, o1k):
        B = BS[blk]
        z8, zk_, yaT, yaTk = (B[k] for k in ('z8', 'zk_', 'yaT', 'yaTk'))
        cl = pc * NB + cb
        cc = cl * 64
        sq, sqk = sqR.next()
        oss, ossk = ossR.next()
        cx.v("pool", "tensor_tensor", [o1k], [sqk], out=sq[:], in0=o1[:], in1=o1[:], op=ALU.mult)
        cx.v("dve", "tensor_reduce", [sqk], [ossk], out=oss[:, 0:4], in_=sq[:], axis=AX.X, op=ALU.add)
        cx.v("dve", "tensor_scalar", [ossk], [ossk], out=oss[:, 4:8], in0=oss[:, 0:4], scalar1=1.0 / 128, scalar2=EPS,
             op0=ALU.mult, op1=ALU.add)
        cx.v("pool", "tensor_tensor", [ossk, "consts"], [ossk], out=oss[:, 0:4], in0=oss[:, 4:8],
             in1=C["negh_col"][0:64, 0:1].to_broadcast([64, 4]), op=ALU.pow)
        yield
        cx.v("dve", "tensor_tensor", [o1k, ossk], [o1k], out=o1[:], in0=o1[:],
             in1=bc(oss[:, 0:4].unsqueeze(2), [64, 4, 128]), op=ALU.mult)
        cx.v("pool", "tensor_tensor", [o1k, "gc"], [o1k], out=o1[:], in0=o1[:],
             in1=bc(gnw[:].unsqueeze(1), [64, 4, 128]), op=ALU.mult)
        ya, yak = yaR.next()
        cx.v("pool", "tensor_tensor", [o1k, zk_], [yak], out=ya[:], in0=o1[:], in1=z8[:, cl, :].rearrange("p (h e) -> p h e", h=4), op=ALU.mult)
        yield
        ps, pk = cx.psum()
        psb = ps[:].bitcast(BF16)
        for h in range(4):
            cx.tr(psb[:, h * 64:(h + 1) * 64], ya[:, h, :], id64, rd=[yak, "consts"], wr=[pk])
        cx.act(yaT[:, :, cc:cc + 64], psb[:, 0:256].rearrange("p (h t) -> p h t", h=4), AF.Copy,
               rd=[pk], wr=[(yaTk, cl)])
        if pc == 3 and cb == NB - 1:
            tok = slice(blk * 512, (blk + 1) * 512)
            cx.dma("sp", io["yaT"][:, tok].rearrange("(h e) t -> e h t", h=4), yaT[:], rd=[(yaTk, cl_) for cl_ in range(8)], wr=[])

    def scan(blk, pc, st):
        B = BS[blk]
        q4, qk_, z8, zk_, yaT, yaTk = (B[k] for k in ('q4', 'qk_', 'z8', 'zk_', 'yaT', 'yaTk'))
        wT, wTk, u0, u0k, kt, ktlk, qkT, qkTk, gam, gamk, gl, glk = (st[k] for k in ('wT', 'wTk', 'u0', 'u0k', 'kt', 'ktlk', 'qkT', 'qkTk', 'gam', 'gamk', 'gl', 'glk'))
        for cb in range(NB):
            cl = pc * NB + cb
            cc = cl * 64
            psw2, pkw2 = cx.psum()
            pso1, pko1 = cx.psum()
            (Tf, Rvv), (Tfk, Rvvk) = u0, u0k
            for h in range(4):
                cx.mm(psw2[0:64, h * 128:(h + 1) * 128], Tf[:, cb, h, :], Rvv[:, cb, h, :], True, False, rd=[Rvvk, Tfk], wr=[pkw2])
                cx.mm(psw2[0:64, h * 128:(h + 1) * 128], wT[:, cb, h, :], Sb[:, h, :], False, True, rd=[wTk, "Sb"], wr=[pkw2])
            for h in range(4):
                cx.mm(pso1[0:64, h * 128:(h + 1) * 128], q4[:, h, cc:cc + 64], Sb[:, h, :], True, True, rd=[qk_, "Sb"], wr=[pko1])
            u, uk = uR.next()
            cx.act(u[:].rearrange("p h e -> p (h e)"), psw2[0:64, :], AF.Copy, rd=[pkw2], wr=[uk])
            o1, o1k = o1R.next()
            cx.v("dve", "tensor_tensor", [pko1, gamk], [o1k], out=o1[:], in0=pso1[0:64, :].rearrange("p (h e) -> p h e", h=4),
                 in1=bc(gam[:, cb, :].unsqueeze(2), [64, 4, 128]), op=ALU.mult)
            yield
            psS, pkS = cx.psum()
            for h in range(4):
                cx.mm(psS[:, h * 128:(h + 1) * 128], kt[:, cb, h, :], u[:, h, :], True, True, rd=[ktlk, uk], wr=[pkS])
            pso2, pko2 = cx.psum()
            for h in range(4):
                cx.mm(pso2[0:64, h * 128:(h + 1) * 128], qkT[:, cb, h, :], u[:, h, :], True, True, rd=[qkTk, uk], wr=[pko2])
            cx.v("dve", "tensor_tensor", ["S32", glk], ["S32"], out=S32[:], in0=S32[:],
                 in1=bc(gl[:, cb, :].unsqueeze(2), [128, 4, 128]), op=ALU.mult)
            cx.v("dve", "tensor_tensor", ["S32", pkS], ["S32"], out=S32[:].rearrange("p h e -> p (h e)"),
                 in0=S32[:].rearrange("p h e -> p (h e)"), in1=psS[:, :], op=ALU.add)
            cx.act(Sb[:], S32[:], AF.Copy, rd=["S32"], wr=["Sb"])
            cx.v("dve", "tensor_tensor", [o1k, pko2], [o1k], out=o1[:].rearrange("p h e -> p (h e)"),
                 in0=o1[:].rearrange("p h e -> p (h e)"), in1=pso2[0:64, :], op=ALU.add)
            outq.append((blk * 4 + pc, outgen(blk, pc, cb, o1, o1k)))
            yield
        me = blk * 4 + pc
        while outq and outq[0][0] < me:
            _, og = outq.pop(0)
            for _ in og:
                yield

    pairs = [(blk, pc) for blk in range(NBLK) for pc in range(4)]
    sts = [dict() for _ in pairs]

    def drain(gen):
        for _ in gen:
            pass

    N = len(pairs)
    active = []
    done_prep = set()
    nxt = 0
    scan_i = 0
    scan_gen = None
    while scan_i < N:
        while len(active) < 2 and nxt < N and nxt <= scan_i + 2:
            active.append((nxt, prep(pairs[nxt][0], pairs[nxt][1], sts[nxt])))
            nxt += 1
        if scan_gen is None and scan_i in done_prep:
            scan_gen = scan(pairs[scan_i][0], pairs[scan_i][1], sts[scan_i])
        if scan_gen is not None:
            try:
                next(scan_gen)
            except StopIteration:
                scan_gen = None
                scan_i += 1
        still = []
        for (j, gen) in active:
            try:
                next(gen)
                still.append((j, gen))
            except StopIteration:
                done_prep.add(j)
        active = still
    for _, og in outq:
        for _ in og:
            pass
    cx.end()


def pipeline(gens):
    active = []

    def rnd():
        nonlocal active
        nxt = []
        for a in active:
            try:
                next(a)
                nxt.append(a)
            except StopIteration:
                pass
        active = nxt

    for g in gens:
        active.insert(0, g)
        rnd()
    while active:
        rnd()


def gelu_tanh(cx, x, xk, out, outk, shape, tR):
    t, tk = tR
    cx.v("pool", "tensor_tensor", [xk], [tk], out=t[:], in0=x, in1=x, op=ALU.mult)
    cx.v("pool", "tensor_scalar", [tk], [tk], out=t[:], in0=t[:], scalar1=0.044715, scalar2=1.0, op0=ALU.mult, op1=ALU.add)
    cx.v("pool", "tensor_tensor", [tk, xk], [tk], out=t[:], in0=t[:], in1=x, op=ALU.mult)
    cx.act(t[:], t[:], AF.Sigmoid, rd=[tk], wr=[tk], scale=1.5957691216057308)
    cx.v("dve", "tensor_tensor", [tk, xk], [outk], out=out, in0=t[:], in1=x, op=ALU.mult)


def phase_C(cx, io):
    nc, T = cx.nc, cx.T
    cx.begin()
    C = cx.consts
    NCH = T // 512
    NCP = T // 16
    NT = (NCP + 127) // 128
    NTW = min(NCP, 128)
    kcT = [cx.sb(f"kcT{g}", [64, NT * 128], BF16) for g in range(2)]
    VE = [cx.sb(f"VE{g}", [128, NT, 128], BF16) for g in range(2)]
    cmask = cx.sb("cmaskC", [128, NT, T], BF16)
    alc = cx.sb("alc", [1, 8, T], BF16)
    bcmp = cx.sb("bcmp", [128, NT, 8], F32)
    ones_row = cx.sb("ones_row", [1, 128], BF16)
    cx.dma("sp", cmask[:], io["c_cmaskC"][:, :, :], rd=[], wr=["cc"])
    cx.dma("sp", alc[:], io["c_alc"][:, :, :], rd=[], wr=["cc"])
    cx.dma("sp", bcmp[:], io["c_bcmp"][:, :, :], rd=[], wr=["cc"])
    cx.v("pool", "memset", [], ["cc"], ones_row[:], 1.0)
    for g in range(2):
        cx.v("pool", "memset", [], [("VE", g)], VE[g][:], 0.0)
        cx.v("pool", "memset", [], [("kcT", g)], kcT[g][:], 0.0)
        cx.dma("sp", VE[g][:, :, 0:64], io["c_ovE"][:, 0:NT, :], rd=[], wr=[("VE", g)])
    with contextlib.ExitStack() as es2:
        def sb2(name, shape, dt):
            cx.uid += 1
            return es2.enter_context(nc.sbuf_tensor(f"{name}_{cx.uid}", shape, dt))
        kblk = sb2("kblk", [64, 32, NCP], BF16)
        srcT = sb2("srcT", [64, T], BF16)
        posT = sb2("posT", [64, 32], F32)
        w1s = sb2("w1s", [64, 32, 256], BF16)
        w2s = sb2("w2s", [128, 2, 64], BF16)
        hT = sb2("hT", [128, 2, NCP], BF16)
        hx = sb2("hx", [128, NCP], F32)
        tR = (sb2("gt", [128, NCP], F32), "gt")
        for kv in range(2):
            sfx = "k" if kv == 0 else "v"
            cx.dma("pool", w1s[:], io["cmp_w1_" + sfx].rearrange("(l d) h -> d l h", d=64), rd=[], wr=["w1s"])
            cx.dma("pool", w2s[:], io["cmp_w2_" + sfx].rearrange("(c p) e -> p c e", p=128), rd=[], wr=["w2s"])
            cx.dma("sp", posT[:], io["cmp_pos_" + sfx].rearrange("l d -> d l"), rd=[], wr=["posT"], allow_slow_non_contiguous=True)
            for g in range(2):
                cx.dma("sp", srcT[:], io["featT"][kv, g * 64:(g + 1) * 64, :], rd=[], wr=["srcT"])
                cx.v("pool", "memset", [], ["kblk"], kblk[:, :, NCP - 1:NCP], 0.0)
                sv = srcT[:].rearrange("d (n s) -> d s n", s=16)
                for half in range(2):
                    cx.v("dve", "tensor_tensor", ["srcT", "posT", "kblk"], ["kblk"], out=kblk[:, half * 16:(half + 1) * 16, 0:NCP - 1],
                         in0=sv[:, :, half:half + NCP - 1],
                         in1=posT[:, half * 16:(half + 1) * 16].unsqueeze(2).to_broadcast([64, 16, NCP - 1]), op=ALU.add)
                for hc in range(2):
                    ps, pk = cx.psum()
                    for l in range(32):
                        cx.mm(ps[:, 0:NCP], w1s[:, l, hc * 128:(hc + 1) * 128], kblk[:, l, :], l == 0, l == 31,
                              rd=["w1s", "kblk"], wr=[pk])
                    cx.act(hT[:, hc, :], ps[:, 0:NCP], AF.Gelu_apprx_tanh, rd=[pk], wr=[("hT", hc)])
                if kv == 0:
                    ps, pk = cx.psum()
                    for hc in range(2):
                        cx.mm(ps[0:64, 0:NCP], w2s[:, hc, :], hT[:, hc, :], hc == 0, hc == 1,
                              rd=["w2s", ("hT", 0), ("hT", 1)], wr=[pk])
                    cx.act(kcT[g][:, 0:NCP], ps[0:64, 0:NCP], AF.Copy, rd=[pk], wr=[("kcT", g)])
                else:
                    for nt in range(NT):
                        ps, pk = cx.psum()
                        for hc in range(2):
                            cx.mm(ps[0:NTW, 0:64], hT[:, hc, nt * 128:nt * 128 + NTW], w2s[:, hc, :], hc == 0, hc == 1,
                                  rd=["w2s", ("hT", 0), ("hT", 1)], wr=[pk])
                        cx.act(VE[g][0:NTW, nt, 64:128], ps[0:NTW, 0:64], AF.Copy, rd=[pk], wr=[("VE", g)])
        cx.S.barrier()
    qR = Rot(cx, "qc", [64, 8, 512], BF16, 2)
    gsR = Rot(cx, "gs", [128, 4, 24], F32, 2)
    imR = Rot(cx, "imm", [128, 4, 64], F32, 2)
    iaR = Rot(cx, "ima", [128, 4, 64], F32, 2)
    alwR = Rot(cx, "alw", [64, 8, 512], BF16, 2)
    pR = Rot(cx, "pc", [128, 512], BF16, 4)
    ocR = Rot(cx, "oc", [128, 4, 512], F32, 2)
    impR = Rot(cx, "imp", [128, 4, 2, 64], F32, 2)
    itR = Rot(cx, "impt", [128, 4, 64], F32, 2)
    zR = Rot(cx, "zc", [128, 12], F32, 4)
    m8R = Rot(cx, "m8", [128, 16], F32, 4)
    wkR = Rot(cx, "wk", [128, 64], F32, 4)
    thR = Rot(cx, "thr", [128, 4, 2], F32, 2)
    selR = Rot(cx, "self", [128, 4, 2, 64], F32, 2)
    smR = Rot(cx, "selm", [128, 4, 2, 64], BF16, 2)
    stR = Rot(cx, "selT", [64, 2, 512], BF16, 2)
    mtR = Rot(cx, "MT", [64, 8, 512], BF16, 2)
    cx.ps_lim = 4

    def load_item(c, st):
        tok = slice(c * 512, (c + 1) * 512)
        st["qc"], st["qk"] = qR.next()
        st["gs"], st["gsk"] = gsR.next()
        st["imm"], st["immk"] = imR.next()
        st["ima"], st["imak"] = iaR.next()
        st["alw"], st["alwk"] = alwR.next()
        st["oc"], st["ock"] = ocR.next()
        st["imp"], st["impk"] = impR.next()
        cx.dma("sp", st["qc"][:], io["QN"][:, 0:64, tok].rearrange("h d t -> d h t"), rd=[], wr=[st["qk"]])
        cx.dma("sp", st["gs"][:], io["gsm"][tok, 264:288].rearrange("(a p) e -> p a e", p=128), rd=[], wr=[st["gsk"]])
        cx.dma("sp", st["imm"][:], io["c_impmul"][tok, :].rearrange("(a p) e -> p a e", p=128), rd=[], wr=[st["immk"]])
        cx.dma("sp", st["ima"][:], io["c_impadd"][tok, :].rearrange("(a p) e -> p a e", p=128), rd=[], wr=[st["imak"]])
        cx.dma("sp", st["alw"][:], io["c_ALW"][:, :, tok].rearrange("h j t -> j h t"), rd=[], wr=[st["alwk"]])
        cx.act(st["gs"][:], st["gs"][:], AF.Sigmoid, rd=[st["gsk"]], wr=[st["gsk"]])
        yield

    def job(c, st, h, nt, first, last):
        tok = slice(c * 512, (c + 1) * 512)
        g = h // 4
        pu, puk = cx.psum_fixed(4 + h % 4)
        if first:
            cx.v("dve", "memset", [], [puk], pu[:, :], 0.0)
        ps, pk = cx.psum()
        cx.mm(ps[:, :], kcT[g][:, nt * 128:(nt + 1) * 128], st["qc"][:, h, :], True, False, rd=[("kcT", g), st["qk"]], wr=[pk])
        cx.mm(ps[:, :], ones_row[:, :], alc[:, h, tok], False, False, rd=["cc"], wr=[pk])
        cx.mm(ps[:, :], C["ident_bf"][:], cmask[:, nt, tok], False, True, rd=["cc", "consts"], wr=[pk])
        pc, pck = pR.next()
        cx.act(pc[:], ps[:, :], AF.Exp, rd=[pk, "cc"], wr=[pck], bias=bcmp[:, nt, h:h + 1])
        yield
        yield
        for a in range(4):
            cx.mm(pu[:, a * 128:(a + 1) * 128], pc[:, a * 128:(a + 1) * 128], VE[g][:, nt, :], False, last,
                  rd=[pck, ("VE", g)], wr=[puk])
        if last:
            gs, gsk, oc, ock, imp, impk = st["gs"], st["gsk"], st["oc"], st["ock"], st["imp"], st["impk"]
            pu4 = pu[:, :].rearrange("p (a e) -> p a e", a=4)
            z, zk = zR.next()
            cx.v("dve", "tensor_reduce", [puk], [zk], out=z[:, 0:4], in_=pu4[:, :, 0:64], axis=AX.X, op=ALU.add)
            cx.v("dve", "tensor_scalar", [zk], [zk], out=z[:, 0:4], in0=z[:, 0:4], scalar1=1e-30, scalar2=None, op0=ALU.max)
            cx.v("dve", "reciprocal", [zk], [zk], out=z[:, 4:8], in_=z[:, 0:4])
            cx.v("dve", "tensor_tensor", [zk, gsk], [zk], out=z[:, 8:12], in0=z[:, 4:8], in1=gs[:, :, 3 * h], op=ALU.mult)
            cx.v("dve", "tensor_tensor", [puk, zk], [(ock, h)], out=oc[:, :, h * 64:(h + 1) * 64], in0=pu4[:, :, 64:128],
                 in1=z[:, 8:12].unsqueeze(2).to_broadcast([128, 4, 64]), op=ALU.mult)
            if h % 4 == 0:
                cx.v("dve", "tensor_tensor", [puk, zk], [(impk, g)], out=imp[:, :, g, :], in0=pu4[:, :, 0:64],
                     in1=z[:, 4:8].unsqueeze(2).to_broadcast([128, 4, 64]), op=ALU.mult)
            else:
                it, itk = itR.next()
                cx.v("dve", "tensor_tensor", [puk, zk], [itk], out=it[:], in0=pu4[:, :, 0:64],
                     in1=z[:, 4:8].unsqueeze(2).to_broadcast([128, 4, 64]), op=ALU.mult)
                cx.v("pool", "tensor_tensor", [itk, (impk, g)], [(impk, g)], out=imp[:, :, g, :], in0=imp[:, :, g, :], in1=it[:], op=ALU.add)

    def fin_item(c, st):
        tok = slice(c * 512, (c + 1) * 512)
        yield
        yield
        yield
        oc, ock, imp, impk = st["oc"], st["ock"], st["imp"], st["impk"]
        imm, immk, ima, imak, alw, alwk = st["imm"], st["immk"], st["ima"], st["imak"], st["alw"], st["alwk"]
        cx.dma("sp", io["ocmp"][tok, :].rearrange("(a p) e -> p a e", p=128), oc[:], rd=[(ock, h) for h in range(8)], wr=[])
        thr, thrk = thR.next()
        for g in range(2):
            ik = (impk, g)
            cx.v("dve", "tensor_tensor", [ik, immk], [ik], out=imp[:, :, g, :], in0=imp[:, :, g, :], in1=imm[:], op=ALU.mult)
            cx.v("dve", "tensor_tensor", [ik, imak], [ik], out=imp[:, :, g, :], in0=imp[:, :, g, :], in1=ima[:], op=ALU.add)
        yield
        m8s = []
        for g in range(2):
            for a in range(4):
                m8, m8k = m8R.next()
                wk, wkk = wkR.next()
                iv = imp[:, a, g, :]
                cx.v("dve", "max", [(impk, g)], [m8k], out=m8[:, 0:8], in_=iv)
                cx.v("dve", "match_replace", [(impk, g), m8k], [wkk], out=wk[:], in_to_replace=m8[:, 0:8], in_values=iv, imm_value=-3.0e38)
                cx.v("dve", "max", [wkk, m8k], [m8k], out=m8[:, 8:16], in_=wk[:])
                cx.v("pool", "tensor_copy", [m8k], [(thrk, a, g)], out=thr[:, a, g:g + 1], in_=m8[:, 15:16])
            yield
        selfl, selfk = selR.next()
        selm, selmk = smR.next()
        cx.v("dve", "tensor_tensor", [(impk, 0), (impk, 1)] + [(thrk, a, g) for a in range(4) for g in range(2)], [selfk],
             out=selfl[:], in0=imp[:], in1=thr[:].unsqueeze(3).to_broadcast([128, 4, 2, 64]), op=ALU.is_ge)
        yield
        cx.v("dve", "tensor_scalar", [selfk], [selmk], out=selm[:], in0=selfl[:], scalar1=-NEGM, scalar2=NEGM, op0=ALU.mult, op1=ALU.add)
        yield
        selT, selTk = stR.next()
        for g in range(2):
            ps, pk = cx.psum()
            psb = ps[0:64, :].bitcast(BF16)
            for a in range(4):
                cx.tr(psb[:, a * 128:(a + 1) * 128], selm[:, a, g, :], C["ident_bf"][:], rd=[selmk, "consts"], wr=[pk])
            cx.act(selT[:, g, :], psb[:, 0:512], AF.Copy, rd=[pk], wr=[(selTk, g)])
        yield
        MT, MTk = mtR.next()
        for h in range(8):
            cx.v("pool", "tensor_tensor", [alwk, (selTk, h // 4)], [(MTk, h)], out=MT[:, h, :], in0=alw[:, h, :], in1=selT[:, h // 4, :], op=ALU.add)
        cx.dma("sp", io["QN"][:, 64:128, tok].rearrange("h j t -> j h t"), MT[:], rd=[(MTk, h) for h in range(8)], wr=[])

    def items():
        for c in range(NCH):
            st = {}
            yield load_item(c, st)
            nts = [nt for nt in range(NT) if 16 * nt * 128 + 31 <= c * 512 + 511]
            for h in range(8):
                for nt in nts:
                    yield job(c, st, h, nt, nt == nts[0], nt == nts[-1])
            yield fin_item(c, st)

    pipeline(items())
    cx.ps_lim = 8
    cx.end()


def phase_S(cx, io):
    nc, T = cx.nc, cx.T
    cx.begin()
    C = cx.consts
    NCH = T // 512
    KT = T // 128
    KE = [cx.sb(f"KE{g}", [128, T], BF16) for g in range(2)]
    KW = [cx.sb(f"KW{g}", [128, T], BF16) for g in range(2)]
    Vall = cx.sb("Vall", [128, KT, 260], BF16)
    cmS = cx.sb("cmS", [128, 4, 512], BF16)
    wmS = cx.sb("wmS", [128, 8, 512], BF16)
    bsel = cx.sb("bsel", [128, 8], F32)
    cx.dma("sp", cmS[:], io["c_cmaskS"][:, :, :], rd=[], wr=["sc"])
    cx.dma("sp", wmS[:], io["c_wmask"][:, :, :], rd=[], wr=["sc"])
    cx.dma("sp", bsel[:], io["c_bsel"][:, :], rd=[], wr=["sc"])
    for g in range(2):
        cx.dma("sp", KE[g][0:64, :], io["featT"][2, g * 64:(g + 1) * 64, :], rd=[], wr=[("K", g)])
        cx.dma("sp", KE[g][64:128, :], io["c_E"][:, :], rd=[], wr=[("K", g)])
        cx.dma("sp", KW[g][0:64, :], io["featT"][3, g * 64:(g + 1) * 64, :], rd=[], wr=[("K", g)])
        cx.dma("sp", KW[g][64:128, :], io["c_E"][:, :], rd=[], wr=[("K", g)])
        pass
    for k0 in range(0, KT, 8):
        k1 = min(KT, k0 + 8)
        cx.dma("sp", Vall[:, k0:k1, :], io["gvs"][k0 * 128:k1 * 128, :].rearrange("(k p) e -> p k e", p=128), rd=[], wr=[("V", k0)])
    qnR = Rot(cx, "qn", [128, 8, 512], BF16, 2)
    wnR = Rot(cx, "wn", [128, 8, 512], BF16, 2)
    gsR = Rot(cx, "gs", [128, 4, 24], F32, 2)
    ocR = Rot(cx, "oc", [128, 4, 512], F32, 2)
    pR = Rot(cx, "pp", [128, 512], BF16, 4)
    zR = Rot(cx, "zs", [128, 4], F32, 4)
    ybR = Rot(cx, "ybb", [128, 512], BF16, 2)
    ybTR = Rot(cx, "ybT", [128, 4, 512], BF16, 2)
    cx.ps_lim = 4

    def load_item(c, st):
        tok = slice(c * 512, (c + 1) * 512)
        st["qn"], st["qnk"] = qnR.next()
        st["wn"], st["wnk"] = wnR.next()
        st["gs"], st["gsk"] = gsR.next()
        st["oc"], st["ock"] = ocR.next()
        qn, wn, gs, oc = st["qn"], st["wn"], st["gs"], st["oc"]
        cx.dma("sp", qn[:], io["QN"][:, :, tok].rearrange("h r t -> r h t"), rd=[], wr=[st["qnk"]])
        cx.dma("sp", wn[0:64], io["QN"][:, 0:64, tok].rearrange("h r t -> r h t"), rd=[], wr=[st["wnk"]])
        cx.dma("sp", wn[64:128], io["c_ALW"][:, :, tok].rearrange("h j t -> j h t"), rd=[], wr=[st["wnk"]])
        cx.dma("sp", gs[:], io["gsm"][tok, 264:288].rearrange("(a p) e -> p a e", p=128), rd=[], wr=[st["gsk"]])
        cx.dma("sp", oc[:], io["ocmp"][tok, :].rearrange("(a p) e -> p a e", p=128), rd=[], wr=[st["ock"]])
        cx.act(gs[:], gs[:], AF.Sigmoid, rd=[st["gsk"]], wr=[st["gsk"]])
        yield

    def job(c, st, h, br, kt, first_kt, last_kt):
        g = h // 4
        pso, pko = cx.psum_fixed(4 + 2 * (h % 2) + br)
        if kt == first_kt:
            cx.v("dve", "memset", [], [pko], pso[:, 0:260], 0.0)
        r = kt - 4 * c
        if br == 0:
            Kt, vo, rhs, rk = KE[g], g * 65, st["qn"], st["qnk"]
            a_lo, a_hi = max(0, r), 3
            mask = cmS[:, r, :] if r >= 0 else None
        else:
            Kt, vo, rhs, rk = KW[g], (2 + g) * 65, st["wn"], st["wnk"]
            a_lo, a_hi = max(0, r), min(3, r + 4)
            mask = wmS[:, r + 4, :]
        cs = slice(a_lo * 128, (a_hi + 1) * 128)
        ps, pk = cx.psum()
        cx.mm(ps[:, cs], Kt[:, kt * 128:(kt + 1) * 128], rhs[:, h, cs], True, mask is None, rd=[("K", g), rk], wr=[pk])
        if mask is not None:
            cx.mm(ps[:, cs], C["ident_bf"][:], mask[:, cs], False, True, rd=["sc", "consts"], wr=[pk])
        pp, ppk = pR.next()
        cx.act(pp[:, cs], ps[:, cs], AF.Exp, rd=[pk, "sc"], wr=[ppk], bias=bsel[:, h:h + 1])
        yield
        yield
        for a in range(a_lo, a_hi + 1):
            cx.mm(pso[:, a * 65:(a + 1) * 65], pp[:, a * 128:(a + 1) * 128], Vall[:, kt, vo:vo + 65], False, kt == 4 * c + a,
                  rd=[ppk, ("V", (kt // 8) * 8)], wr=[pko])
        if br == 1 and kt == last_kt:
            ps_s, pk_s = cx.psum_fixed(4 + 2 * (h % 2))
            ps_w, pk_w = pso, pko
            gs, gsk, oc, ock = st["gs"], st["gsk"], st["oc"], st["ock"]
            zs = [zR.next() for a in range(4)]
            for a in range(4):
                z, zk = zs[a]
                cx.v("dve", "tensor_scalar", [pk_s], [zk], out=z[:, 0:1], in0=ps_s[:, a * 65 + 64:a * 65 + 65], scalar1=1e-30, scalar2=None, op0=ALU.max)
            for a in range(4):
                z, zk = zs[a]
                cx.v("dve", "tensor_scalar", [pk_w], [zk], out=z[:, 1:2], in0=ps_w[:, a * 65 + 64:a * 65 + 65], scalar1=1e-30, scalar2=None, op0=ALU.max)
            for a in range(4):
                z, zk = zs[a]
                cx.v("dve", "reciprocal", [zk], [zk], out=z[:, 2:4], in_=z[:, 0:2])
            for a in range(4):
                z, zk = zs[a]
                cx.v("dve", "tensor_tensor", [zk, gsk], [zk], out=z[:, 0:2], in0=z[:, 2:4], in1=gs[:, a, 3 * h + 1:3 * h + 3], op=ALU.mult)
            for a in range(4):
                z, zk = zs[a]
                ov = oc[:, a, h * 64:(h + 1) * 64]
                cx.v("dve", "scalar_tensor_tensor", [pk_s, zk, ock, (ock, a, h)], [(ock, a, h)], out=ov, in0=ps_s[:, a * 65:a * 65 + 64], scalar=z[:, 0:1], in1=ov,
                     op0=ALU.mult, op1=ALU.add)
            for a in range(4):
                z, zk = zs[a]
                ov = oc[:, a, h * 64:(h + 1) * 64]
                cx.v("dve", "scalar_tensor_tensor", [pk_w, zk, ock, (ock, a, h)], [(ock, a, h)], out=ov, in0=ps_w[:, a * 65:a * 65 + 64], scalar=z[:, 1:2], in1=ov,
                     op0=ALU.mult, op1=ALU.add)

    def fin_item(c, st):
        tok = slice(c * 512, (c + 1) * 512)
        yield
        yield
        yield
        oc, ock = st["oc"], st["ock"]
        ybT, ybTk = ybTR.next()
        for a in range(4):
            yb, ybk = ybR.next()
            cx.v("pool", "tensor_copy", [ock] + [(ock, a, h) for h in range(8)], [ybk], out=yb[:], in_=oc[:, a, :])
            ps, pk = cx.psum()
            psb = ps[:].bitcast(BF16)
            for kc in range(4):
                cx.tr(psb[:, kc * 128:(kc + 1) * 128], yb[:, kc * 128:(kc + 1) * 128], C["ident_bf"][:], rd=[ybk, "consts"], wr=[pk])
            cx.act(ybT[:, :, a * 128:(a + 1) * 128], psb[:, 0:512].rearrange("p (k t) -> p k t", k=4), AF.Copy, rd=[pk], wr=[(ybTk, a)])
        cx.dma("sp", io["ybT"][:, tok].rearrange("(k p) t -> p k t", p=128), ybT[:], rd=[(ybTk, a) for a in range(4)], wr=[])

    def items():
        for c in range(NCH):
            st = {}
            yield load_item(c, st)
            for h in range(8):
                for br in range(2):
                    kts = list(range(0, 4 * c + 4)) if br == 0 else list(range(max(0, 4 * c - 4), 4 * c + 4))
                    for kt in kts:
                        yield job(c, st, h, br, kt, kts[0], kts[-1])
            yield fin_item(c, st)

    pipeline(items())
    cx.ps_lim = 8
    cx.end()


def host_consts(T):
    bf = ml_dtypes.bfloat16
    c = {}
    c["ident_bf"] = np.eye(128, dtype=np.float32).astype(bf)
    c["ident_f"] = np.eye(128, dtype=np.float32)
    m = np.arange(64)
    c["tri"] = (m[:, None] <= m[None, :]).astype(np.float32)
    c["stri"] = (m[:, None] > m[None, :]).astype(np.float32)
    c["msk"] = np.stack([(m[:, None] > m[None, :]), (m[:, None] >= m[None, :])], 1).astype(np.float32)
    slopes = 2.0 ** (-(np.arange(8) + 1.0))
    q = np.arange(T)
    NCP = T // 16
    NC = NCP - 1
    NT = (NCP + 127) // 128
    n = np.arange(NT * 128)
    valid = (n[:, None] < NC) & (q[None, :] >= 16 * n[:, None] + 31)
    c["cmaskC"] = np.where(valid, 0.0, NEGM).astype(np.float32).reshape(NT, 128, T).transpose(1, 0, 2).astype(bf)
    c["alc"] = (-slopes[:, None] * 16.0 * (q[None, :] // 16)).astype(np.float32)[None].astype(bf)
    c["bcmp"] = (slopes[None, None, :] * 16.0 * n.reshape(NT, 128).T[:, :, None]).astype(np.float32)
    NS = T // 64
    j = np.arange(NS)
    bs = n * 16
    ov = np.clip(np.minimum(bs[:, None] + 32, j[None, :] * 64 + 64) - np.maximum(bs[:, None], j[None, :] * 64), 0, None) / 32.0
    ov = np.where(n[:, None] < NC, ov, 0.0)
    ovp = np.zeros((NT * 128, 64), np.float32)
    ovp[:, :NS] = ov
    c["ovE"] = ovp.reshape(NT, 128, 64).transpose(1, 0, 2).astype(bf)
    cur = q // 64
    j64 = np.arange(64)
    forced = (j64[None, :] == 0) | (j64[None, :] == cur[:, None]) | (j64[None, :] == cur[:, None] - 1)
    selv = (j64[None, :] <= cur[:, None]) & (j64[None, :] < NS)
    c["impmul"] = (selv & ~forced).astype(np.float32)
    c["impadd"] = np.where(selv, np.where(forced, 1e9, 0.0), -1e30).astype(np.float32)
    c["ALW"] = (-slopes[:, None, None] * 64.0 * (cur[None, None, :] - j64[None, :, None])).astype(np.float32).astype(bf)
    c["E"] = (q[None, :] // 64 == j64[:, None]).astype(np.float32).astype(bf)
    p = np.arange(128)
    qq = np.arange(512)
    c["cmaskS"] = np.stack([np.where((128 * r + p[:, None]) <= qq[None, :], 0.0, NEGM) for r in range(4)], 1).astype(np.float32).astype(bf)
    wm = []
    for r in range(-4, 4):
        dist = qq[None, :] - (128 * r + p[:, None])
        wm.append(np.where((dist >= 0) & (dist < 512), 0.0, NEGM))
    c["wmask"] = np.stack(wm, 1).astype(np.float32).astype(bf)
    c["bsel"] = (slopes[None, :] * (p[:, None] % 64)).astype(np.float32)
    return c


def build(T, phases, debug=False, dbg_in=()):
    nc = bass.Bass("TRN2", target_bir_lowering=False)
    io = {}

    def din(name, shape, dt=F32):
        io[name] = nc.dram_tensor(name, list(shape), dt, kind="ExternalInput").ap()

    def dscr(name, shape, dt):
        kind = "ExternalOutput" if debug else "Internal"
        if debug and name in dbg_in:
            kind = "ExternalInput"
        io[name] = nc.dram_tensor(name, list(shape), dt, kind=kind).ap()

    din("x", [T, D])
    din("p", [T, 256])
    for name, shape in WSHAPES.items():
        din(name, shape)
    hc = host_consts(T)
    for name, arr in hc.items():
        din("c_" + name, arr.shape, BF16 if arr.dtype == ml_dtypes.bfloat16 else F32)
    io["out"] = nc.dram_tensor("out", [T, D], F32, kind="ExternalOutput").ap()
    dscr("gq", [4, 128, T], BF16)
    dscr("gk", [4, 128, T], BF16)
    dscr("gv", [T, 512], BF16)
    dscr("gz", [T, 512], BF16)
    dscr("gsm", [T, 288], F32)
    dscr("gvs", [T, 260], BF16)
    dscr("QN", [8, 128, T], BF16)
    dscr("featT", [4, 128, T], BF16)
    dscr("gmixT", [2048, T], BF16)
    dscr("yaT", [512, T], BF16)
    dscr("ocmp", [T, 512], F32)
    dscr("ybT", [512, T], BF16)

    cx = Ctx(nc, T)
    C = {}
    C["ident_bf"] = cx.sbtop("ident_bf", [128, 128], BF16)
    C["eps_col"] = cx.sbtop("eps_col", [128, 1], F32)
    C["zero_col"] = cx.sbtop("zero_col", [128, 1], F32)
    C["lnq_col"] = cx.sbtop("lnq_col", [128, 1], F32)
    C["negh_col"] = cx.sbtop("negh_col", [128, 1], F32)
    cx.consts = C
    cx.dma("sp", C["ident_bf"][:], io["c_ident_bf"][:, :], rd=[], wr=["consts"])
    cx.v("pool", "memset", [], ["consts"], C["eps_col"][:], EPS)
    cx.v("pool", "memset", [], ["consts"], C["zero_col"][:], 0.0)
    cx.v("pool", "memset", [], ["consts"], C["lnq_col"][:], float(np.log(128.0 ** -0.5)))
    cx.v("pool", "memset", [], ["consts"], C["negh_col"][:], -0.5)
    cx.S.barrier()

    if "A" in phases:
        phase_A(cx, io)
    if "C" in phases:
        phase_C(cx, io)
    if "S" in phases:
        phase_S(cx, io)
    if "G" in phases:
        phase_G(cx, io)
    if "M" in phases:
        phase_M(cx, io)
    if "F" in phases:
        phase_F(cx, io)
    if "P" in phases:
        phase_P(cx, io)
    cx.S.barrier()
    cx.top.close()
    return nc, hc


WSHAPES = {
    "norm_mix_pre": [D], "w_in": [D, DIN], "conv_qkv": [4, 1536], "a_log": [4], "dt_bias": [4],
    "gdn_norm": [128], "cmp_pos_k": [32, 64], "cmp_w1_k": [2048, 256], "cmp_w2_k": [256, 64],
    "cmp_pos_v": [32, 64], "cmp_w1_v": [2048, 256], "cmp_w2_v": [256, 64],
    "w_a2d": [512, D], "w_b2d": [512, D], "w_o": [D, D], "norm_mix_post": [D], "norm_ffn_pre": [D],
    "w_up": [D, 2 * DFF], "conv_ffn": [3, 2 * DFF], "conv_ffn_b": [2 * DFF], "w_down": [DFF, D],
    "norm_ffn_post": [D], "w_ple": [256, D], "w_ple_gate": [D, D], "norm_ple_post": [D],
}


def run(inputs, T, ncores, phases, debug=False, dbg_in=None):
    dbg_in = dbg_in or {}
    nc, hc = build(T, phases, debug, tuple(dbg_in))
    in_maps = []
    for c in range(ncores):
        m = {"x": np.ascontiguousarray(inputs["x"][c]), "p": np.ascontiguousarray(inputs["p"][0, c])}
        for name in WSHAPES:
            m[name] = np.ascontiguousarray(inputs[name][0])
        for name, arr in hc.items():
            m["c_" + name] = arr
        for name, arr in dbg_in.items():
            m[name] = arr
        in_maps.append(m)
    res = run_bass_kernel_spmd(nc, in_maps, core_ids=list(range(ncores)))
    return res.results


def kernel(**inputs):
    inputs = {k: np.asarray(v) for k, v in inputs.items()}
    B, T, _ = inputs["x"].shape
    results = run(inputs, T, B, "ACSGMFP")
    return np.stack([r["out"] for r in results], axis=0).astype(np.float32)
```
